# Optimizing a Trainium2 kernel written in Bass

```python
import jax
import jax.numpy as jnp
from jax import lax
import numpy as np

D_MODEL = 1024
BATCH = 2
SEQ = 8192
DEPTH = 4
DEC_BATCH = 128
DEC_SEQ = 1
PAST_LEN = 2048
PAGE_SIZE = 128

HEAD_DIM = 64
NSA_HEADS = D_MODEL // (2 * HEAD_DIM)
NSA_KV_HEADS = 2
NSA_HPG = NSA_HEADS // NSA_KV_HEADS
NSA_WIDTH = NSA_HEADS * HEAD_DIM
KV_W = NSA_KV_HEADS * HEAD_DIM
CMP_BLOCK = 32
CMP_STRIDE = 16
CMP_RATIO = CMP_BLOCK // CMP_STRIDE
CMP_HIDDEN = 256
SEL_BLOCK = 64
SEL_TOPK = 16
WINDOW = 512
Q_BLOCK = 128
FORCE_SCORE = 1e6
ROPE_DIM = HEAD_DIM // 4
ROPE_THETA = 500000.0
RG_WIDTH = D_MODEL // 4
RG_BLOCKS = 4
RG_BW = RG_WIDTH // RG_BLOCKS
RG_C = 8.0
CONV_W = 4
HG_HEADS = 4
HG_DK = 64
HG_DV = D_MODEL // 4 // HG_HEADS
HG_WIDTH = HG_HEADS * HG_DV
HG_CHUNK = 64
MIX_WIDTH = NSA_WIDTH + RG_WIDTH + HG_WIDTH
D_FF = 4 * D_MODEL
EPS = 1e-6
IN_SIZES = (NSA_WIDTH, 6 * KV_W, 3 * NSA_HEADS, RG_WIDTH, RG_WIDTH, HG_HEADS * HG_DK, HG_HEADS * HG_DK, HG_WIDTH, HG_WIDTH)
D_IN = sum(IN_SIZES)

kernel_name = 'nsa_rglru_hgrn2_parallel_hybrid_step'


def _split(a, sizes):
    out, o = [], 0
    for s in sizes:
        out.append(a[..., o:o + s])
        o += s
    return out


def rms_norm(x, g):
    x32 = x.astype(jnp.float32)
    y = x32 * lax.rsqrt(jnp.mean(x32 * x32, axis=-1, keepdims=True) + EPS)
    return (y * g.astype(jnp.float32)).astype(x.dtype)


def partial_rope(x, pos):
    half = ROPE_DIM // 2
    inv = ROPE_THETA ** (-jnp.arange(half, dtype=jnp.float32) * 2.0 / ROPE_DIM)
    ang = pos.astype(jnp.float32)[:, None] * inv
    cos = jnp.cos(ang)[:, None, :].astype(x.dtype)
    sin = jnp.sin(ang)[:, None, :].astype(x.dtype)
    x1, x2, rest = x[..., :half], x[..., half:ROPE_DIM], x[..., ROPE_DIM:]
    return jnp.concatenate([x1 * cos - x2 * sin, x2 * cos + x1 * sin, rest], axis=-1)


def masked_softmax(s, mask):
    s = jnp.where(mask, s, -1e30)
    m = jnp.max(s, axis=-1, keepdims=True)
    e = jnp.where(mask, jnp.exp(s - m), 0.0)
    return e / jnp.maximum(jnp.sum(e, axis=-1, keepdims=True), 1e-30)


def compress_kv(raw, pos_emb, w1, b1, w2, b2):
    B, L = raw.shape[0], raw.shape[1]
    nch = L // CMP_STRIDE
    n_cmp = nch - CMP_RATIO + 1
    chunks = raw[:, :nch * CMP_STRIDE].reshape(B, nch, CMP_STRIDE, NSA_KV_HEADS, HEAD_DIM)
    h = b1
    for r in range(CMP_RATIO):
        sl = slice(r * CMP_STRIDE, (r + 1) * CMP_STRIDE)
        pre = jnp.einsum('bnsgd,sdh->bngh', chunks + pos_emb[sl][:, None, :], w1[sl])
        h = h + pre[:, r:r + n_cmp]
    out = jax.nn.gelu(h) @ w2 + b2
    end = jnp.arange(n_cmp, dtype=jnp.int32) * CMP_STRIDE + CMP_BLOCK - 1
    return out, end


def selection_map(n_cmp, n_sel):
    c0 = jnp.arange(n_cmp) * CMP_STRIDE
    s0 = jnp.arange(n_sel) * SEL_BLOCK
    ov = (c0[:, None] < s0[None, :] + SEL_BLOCK) & (c0[:, None] + CMP_BLOCK > s0[None, :])
    return ov.astype(jnp.float32)


def nsa_attend_block(q, gate, t_pos, kw, vw, w_pos, k_cmp, v_cmp, c_end, ks_blk, vs_blk, smap):
    f32 = jnp.float32
    scale = HEAD_DIM ** -0.5
    s_c = jnp.einsum('bqghd,bngd->bqghn', q, k_cmp).astype(f32) * scale
    m_c = (c_end[None, :] <= t_pos[:, None])[None, :, None, None, :]
    p_c = masked_softmax(s_c, m_c)
    o_c = jnp.einsum('bqghn,bngd->bqghd', p_c.astype(v_cmp.dtype), v_cmp)
    imp = jnp.einsum('bqghn,nj->bqgj', p_c, smap)
    n_sel = ks_blk.shape[1]
    j = jnp.arange(n_sel)[None, :]
    cur = (t_pos // SEL_BLOCK)[:, None]
    valid = j * SEL_BLOCK <= t_pos[:, None]
    forced = (j == 0) | (j == cur) | (j == cur - 1)
    score = jnp.where(valid[None, :, None, :], jnp.where(forced[None, :, None, :], FORCE_SCORE, imp), -FORCE_SCORE)
    _, idx = lax.top_k(score, min(SEL_TOPK, n_sel))
    bi = jnp.arange(q.shape[0])[:, None, None, None]
    gi = jnp.arange(NSA_KV_HEADS)[None, None, :, None]
    kb = ks_blk[bi, idx, gi]
    vb = vs_blk[bi, idx, gi]
    kpos = idx[..., None] * SEL_BLOCK + jnp.arange(SEL_BLOCK)
    m_s = (kpos <= t_pos[None, :, None, None, None])[:, :, :, None]
    s_s = jnp.einsum('bqghd,bqgkld->bqghkl', q, kb).astype(f32) * scale
    shp = s_s.shape
    p_s = masked_softmax(s_s.reshape(*shp[:4], -1), m_s.reshape(*m_s.shape[:4], -1)).reshape(shp)
    o_s = jnp.einsum('bqghkl,bqgkld->bqghd', p_s.astype(vb.dtype), vb)
    s_w = jnp.einsum('bqghd,bwgd->bqghw', q, kw).astype(f32) * scale
    dist = t_pos[:, None] - w_pos[None, :]
    m_w = ((dist >= 0) & (dist < WINDOW) & (w_pos[None, :] >= 0))[None, :, None, None, :]
    p_w = masked_softmax(s_w, m_w)
    o_w = jnp.einsum('bqghw,bwgd->bqghd', p_w.astype(vw.dtype), vw)
    return gate[..., 0:1] * o_c + gate[..., 1:2] * o_s + gate[..., 2:3] * o_w


def nsa_mixer(q, kvs, gate, pos, cmp_pos, cmp_w1, cmp_b1, cmp_w2, cmp_b2, past):
    B, T = q.shape[:2]
    G, HD = NSA_KV_HEADS, HEAD_DIM
    q = partial_rope(q.reshape(B, T, NSA_HEADS, HD), pos).reshape(B, T, G, NSA_HPG, HD)
    kc, vc, ks, vs, kw, vw = [a.reshape(B, T, G, HD) for a in _split(kvs, (KV_W,) * 6)]
    ks = partial_rope(ks, pos)
    kw = partial_rope(kw, pos)
    gate = jax.nn.sigmoid(gate.reshape(B, T, G, NSA_HPG, 3))
    new_cmp = jnp.stack([kc, vc], axis=2)
    new_sel = jnp.stack([ks, vs], axis=2)
    new_win = jnp.stack([kw, vw], axis=2)
    if past is None:
        cmp_all, sel_all = new_cmp, new_sel
        win_all = jnp.pad(new_win, ((0, 0), (WINDOW, 0), (0, 0), (0, 0), (0, 0)))
        w_pos = pos[0] - WINDOW + jnp.arange(T + WINDOW, dtype=jnp.int32)
        win_state = new_win[:, T - min(WINDOW, T):]
    else:
        cmp_past, sel_past, win_buf = past
        cmp_all = jnp.concatenate([cmp_past, new_cmp], axis=1)
        sel_all = jnp.concatenate([sel_past, new_sel], axis=1)
        win_all = jnp.concatenate([win_buf, new_win], axis=1)
        nb = win_buf.shape[1]
        w_pos = pos[0] - nb + jnp.arange(nb + T, dtype=jnp.int32)
        win_state = win_all[:, T:]
    L = cmp_all.shape[1]
    k_cmp, c_end = compress_kv(cmp_all[:, :, 0], cmp_pos[0], cmp_w1[0], cmp_b1[0], cmp_w2[0], cmp_b2[0])
    v_cmp, _ = compress_kv(cmp_all[:, :, 1], cmp_pos[1], cmp_w1[1], cmp_b1[1], cmp_w2[1], cmp_b2[1])
    k_cmp = partial_rope(k_cmp, c_end)
    n_sel = -(-L // SEL_BLOCK)
    sel_pad = jnp.pad(sel_all, ((0, 0), (0, n_sel * SEL_BLOCK - L), (0, 0), (0, 0), (0, 0)))
    sel_blk = sel_pad.reshape(B, n_sel, SEL_BLOCK, 2, G, HD).transpose(3, 0, 1, 4, 2, 5)
    smap = selection_map(k_cmp.shape[1], n_sel)
    kw_all, vw_all = win_all[:, :, 0], win_all[:, :, 1]
    if past is None and T > Q_BLOCK and T % Q_BLOCK == 0:
        def one_block(n):
            q0 = n * Q_BLOCK
            return nsa_attend_block(
                lax.dynamic_slice_in_dim(q, q0, Q_BLOCK, axis=1),
                lax.dynamic_slice_in_dim(gate, q0, Q_BLOCK, axis=1),
                lax.dynamic_slice_in_dim(pos, q0, Q_BLOCK, axis=0),
                lax.dynamic_slice_in_dim(kw_all, q0, Q_BLOCK + WINDOW, axis=1),
                lax.dynamic_slice_in_dim(vw_all, q0, Q_BLOCK + WINDOW, axis=1),
                lax.dynamic_slice_in_dim(w_pos, q0, Q_BLOCK + WINDOW, axis=0),
                k_cmp, v_cmp, c_end, sel_blk[0], sel_blk[1], smap)
        o = lax.map(one_block, jnp.arange(T // Q_BLOCK))
        o = jnp.moveaxis(o, 0, 1).reshape(B, T, NSA_WIDTH)
    else:
        o = nsa_attend_block(q, gate, pos, kw_all, vw_all, w_pos, k_cmp, v_cmp, c_end,
                             sel_blk[0], sel_blk[1], smap).reshape(B, T, NSA_WIDTH)
    return o, new_cmp, new_sel, win_state


def _lin_combine(e1, e2):
    a1, b1 = e1
    a2, b2 = e2
    return a1 * a2, a2 * b1 + b2


def rglru_mixer(xr, gr, conv_buf, h0, conv_w, conv_b, wa, ba, wx, bx, lam):
    B, T = xr.shape[:2]
    f32 = jnp.float32
    xcat = jnp.concatenate([conv_buf.astype(xr.dtype), xr], axis=1)
    xc = conv_b + sum(conv_w[k] * xcat[:, k:k + T] for k in range(CONV_W))
    new_buf = xcat[:, T:]
    xb = xc.reshape(B, T, RG_BLOCKS, RG_BW)
    r = jax.nn.sigmoid(jnp.einsum('btnd,nde->btne', xb, wa).reshape(B, T, RG_WIDTH) + ba)
    i = jax.nn.sigmoid(jnp.einsum('btnd,nde->btne', xb, wx).reshape(B, T, RG_WIDTH) + bx)
    log_a = -RG_C * r.astype(f32) * jax.nn.softplus(-lam.astype(f32))
    a = jnp.exp(log_a)
    b = jnp.sqrt(-jnp.expm1(2.0 * log_a)) * (i * xc).astype(f32)
    b = b.at[:, 0].add(a[:, 0] * h0.astype(f32))
    _, h = lax.associative_scan(_lin_combine, (a, b), axis=1)
    out = h.astype(xr.dtype) * jax.nn.gelu(gr)
    return out, h[:, -1].astype(h0.dtype), new_buf


def gated_recurrence(q, k, v, logf, s0):
    B, T, H, DK = q.shape
    DV = v.shape[-1]
    C = HG_CHUNK if T % HG_CHUNK == 0 else T
    nc = T // C

    def to_chunks(a):
        return jnp.moveaxis(a.reshape(B, nc, C, *a.shape[2:]), 1, 0)

    tri = jnp.tril(jnp.ones((C, C), dtype=bool))[None, :, :, None, None]

    def step(S, inp):
        qc, kc, vc, gc = inp
        bcum = jnp.cumsum(gc, axis=1)
        o = jnp.einsum('bthk,bhkv->bthv', qc * jnp.exp(bcum), S)
        dec = jnp.exp(jnp.where(tri, bcum[:, :, None] - bcum[:, None, :], -jnp.inf))
        A = jnp.einsum('bthk,bshk,btshk->bhts', qc, kc, dec)
        o = o + jnp.einsum('bhts,bshv->bthv', A, vc)
        bl = bcum[:, -1]
        S = jnp.exp(bl)[..., None] * S + jnp.einsum('bshk,bshv->bhkv', kc * jnp.exp(bl[:, None] - bcum), vc)
        return S, o

    s_fin, o = lax.scan(step, s0, (to_chunks(q), to_chunks(k), to_chunks(v), to_chunks(logf)))
    return jnp.moveaxis(o, 0, 1).reshape(B, T, H, DV), s_fin


def hgrn2_mixer(hq, hf, hi, hg, s0, lb, gain):
    B, T = hq.shape[:2]
    f32 = jnp.float32
    q = jax.nn.silu(hq.astype(f32)).reshape(B, T, HG_HEADS, HG_DK)
    lbh = lb.reshape(HG_HEADS, HG_DK)
    f = lbh + (1.0 - lbh) * jax.nn.sigmoid(hf.astype(f32).reshape(B, T, HG_HEADS, HG_DK))
    v = hi.astype(f32).reshape(B, T, HG_HEADS, HG_DV)
    o, s_fin = gated_recurrence(q, 1.0 - f, v, jnp.log(f), s0.astype(f32))
    o = rms_norm(o, gain.reshape(HG_HEADS, HG_DV)).reshape(B, T, HG_WIDTH).astype(hq.dtype)
    return o * jax.nn.silu(hg), s_fin.astype(s0.dtype)


def forward_layer(x, pos, w, past):
    (norm_mix, w_in, w_out, norm_ffn, w_up, w_down, cmp_pos, cmp_w1, cmp_b1, cmp_w2, cmp_b2,
     rg_conv_w, rg_conv_b, rg_wa, rg_ba, rg_wx, rg_bx, rg_lambda, hg_lb, hg_gain) = w
    B = x.shape[0]
    y = rms_norm(x, norm_mix)
    q, kvs, gate, rg_x, rg_g, hg_q, hg_f, hg_i, hg_g = _split(y @ w_in, IN_SIZES)
    if past is None:
        nsa_past = None
        rg_buf = jnp.zeros((B, CONV_W - 1, RG_WIDTH), x.dtype)
        rg_h0 = jnp.zeros((B, RG_WIDTH), x.dtype)
        hg_s0 = jnp.zeros((B, HG_HEADS, HG_DK, HG_DV), x.dtype)
    else:
        cmp_past, sel_past, win_buf, rg_h0, rg_buf, hg_s0 = past
        nsa_past = (cmp_past, sel_past, win_buf)
    o_nsa, new_cmp, new_sel, new_win = nsa_mixer(q, kvs, gate, pos, cmp_pos, cmp_w1, cmp_b1, cmp_w2, cmp_b2, nsa_past)
    o_rg, new_h, new_buf = rglru_mixer(rg_x, rg_g, rg_buf, rg_h0, rg_conv_w, rg_conv_b, rg_wa, rg_ba, rg_wx, rg_bx, rg_lambda)
    o_hg, new_s = hgrn2_mixer(hg_q, hg_f, hg_i, hg_g, hg_s0, hg_lb, hg_gain)
    x = x + jnp.concatenate([o_nsa, o_rg, o_hg], axis=-1) @ w_out
    hmid = jax.nn.relu(rms_norm(x, norm_ffn) @ w_up)
    x = x + (hmid * hmid) @ w_down
    return x, (new_cmp, new_sel, new_win, new_h, new_buf, new_s)


def setup_inputs(seed: int = 0) -> dict:
    key = jax.random.key(seed)
    keys = iter(jax.random.split(key, 48))
    f32 = jnp.float32

    def nrm(shape, scale):
        return jax.random.normal(next(keys), shape, f32) * scale

    n_pages = PAST_LEN // PAGE_SIZE
    n_used = DEC_BATCH * n_pages
    n_pool = (n_used * 5) // 4
    win_buf = min(WINDOW, PAST_LEN)
    x_prompt = nrm((BATCH, SEQ, D_MODEL), 1.0)
    x_sample = nrm((DEC_BATCH, DEC_SEQ, D_MODEL), 1.0)
    cache_nsa_cmp_kv = nrm((DEPTH, n_pool, PAGE_SIZE, 2, NSA_KV_HEADS, HEAD_DIM), 1.0)
    cache_nsa_sel_kv = nrm((DEPTH, n_pool, PAGE_SIZE, 2, NSA_KV_HEADS, HEAD_DIM), 1.0)
    cache_nsa_win_kv = nrm((DEPTH, DEC_BATCH, win_buf, 2, NSA_KV_HEADS, HEAD_DIM), 1.0)
    state_rglru_h = nrm((DEPTH, DEC_BATCH, RG_WIDTH), 0.5)
    state_rglru_conv = nrm((DEPTH, DEC_BATCH, CONV_W - 1, RG_WIDTH), 1.0)
    state_hgrn_s = nrm((DEPTH, DEC_BATCH, HG_HEADS, HG_DK, HG_DV), 0.3)
    page_table = jax.random.permutation(next(keys), n_pool)[:n_used].reshape(DEC_BATCH, n_pages).astype(jnp.int32)
    u = jax.random.uniform(next(keys), (DEPTH, RG_WIDTH), f32, minval=0.9, maxval=0.999)
    return {
        'x_prompt': x_prompt,
        'x_sample': x_sample,
        'cache_nsa_cmp_kv': cache_nsa_cmp_kv,
        'cache_nsa_sel_kv': cache_nsa_sel_kv,
        'cache_nsa_win_kv': cache_nsa_win_kv,
        'state_rglru_h': state_rglru_h,
        'state_rglru_conv': state_rglru_conv,
        'state_hgrn_s': state_hgrn_s,
        'page_table': page_table,
        'norm_mix': 1.0 + nrm((DEPTH, D_MODEL), 0.01),
        'w_in': nrm((DEPTH, D_MODEL, D_IN), D_MODEL ** -0.5),
        'w_out': nrm((DEPTH, MIX_WIDTH, D_MODEL), MIX_WIDTH ** -0.5),
        'norm_ffn': 1.0 + nrm((DEPTH, D_MODEL), 0.01),
        'w_up': nrm((DEPTH, D_MODEL, D_FF), D_MODEL ** -0.5),
        'w_down': nrm((DEPTH, D_FF, D_MODEL), D_FF ** -0.5),
        'cmp_pos': nrm((DEPTH, 2, CMP_BLOCK, HEAD_DIM), 0.1),
        'cmp_w1': nrm((DEPTH, 2, CMP_BLOCK, HEAD_DIM, CMP_HIDDEN), (CMP_BLOCK * HEAD_DIM) ** -0.5),
        'cmp_b1': nrm((DEPTH, 2, CMP_HIDDEN), 0.01),
        'cmp_w2': nrm((DEPTH, 2, CMP_HIDDEN, HEAD_DIM), CMP_HIDDEN ** -0.5),
        'cmp_b2': nrm((DEPTH, 2, HEAD_DIM), 0.01),
        'rg_conv_w': nrm((DEPTH, CONV_W, RG_WIDTH), CONV_W ** -0.5),
        'rg_conv_b': nrm((DEPTH, RG_WIDTH), 0.01),
        'rg_wa': nrm((DEPTH, RG_BLOCKS, RG_BW, RG_BW), RG_BW ** -0.5),
        'rg_ba': nrm((DEPTH, RG_WIDTH), 0.01),
        'rg_wx': nrm((DEPTH, RG_BLOCKS, RG_BW, RG_BW), RG_BW ** -0.5),
        'rg_bx': nrm((DEPTH, RG_WIDTH), 0.01),
        'rg_lambda': jnp.log(u) - jnp.log1p(-u),
        'hg_lower_bounds': nrm((DEPTH, HG_HEADS * HG_DK), 0.1),
        'hg_gain': 1.0 + nrm((DEPTH, HG_WIDTH), 0.01),
        'final_norm': 1.0 + nrm((D_MODEL,), 0.01),
    }


def reference(x_prompt, x_sample, cache_nsa_cmp_kv, cache_nsa_sel_kv, cache_nsa_win_kv, state_rglru_h,
              state_rglru_conv, state_hgrn_s, page_table, norm_mix, w_in, w_out, norm_ffn, w_up, w_down,
              cmp_pos, cmp_w1, cmp_b1, cmp_w2, cmp_b2, rg_conv_w, rg_conv_b, rg_wa, rg_ba, rg_wx, rg_bx,
              rg_lambda, hg_lower_bounds, hg_gain, final_norm):
    lb = jnp.cumsum(jax.nn.softmax(hg_lower_bounds.astype(jnp.float32), axis=0), axis=0)
    lb = lb - lb[0]
    past_len = page_table.shape[1] * PAGE_SIZE
    n_dec = x_sample.shape[0]
    pos_p = jnp.arange(x_prompt.shape[1], dtype=jnp.int32)
    pos_s = past_len + jnp.arange(x_sample.shape[1], dtype=jnp.int32)
    xp, xs = x_prompt, x_sample
    st_p, st_s = [], []
    for l in range(DEPTH):
        w = (norm_mix[l], w_in[l], w_out[l], norm_ffn[l], w_up[l], w_down[l], cmp_pos[l], cmp_w1[l], cmp_b1[l],
             cmp_w2[l], cmp_b2[l], rg_conv_w[l], rg_conv_b[l], rg_wa[l], rg_ba[l], rg_wx[l], rg_bx[l],
             rg_lambda[l], lb[l], hg_gain[l])
        xp, sp = forward_layer(xp, pos_p, w, None)
        past = (cache_nsa_cmp_kv[l][page_table].reshape(n_dec, past_len, 2, NSA_KV_HEADS, HEAD_DIM),
                cache_nsa_sel_kv[l][page_table].reshape(n_dec, past_len, 2, NSA_KV_HEADS, HEAD_DIM),
                cache_nsa_win_kv[l], state_rglru_h[l], state_rglru_conv[l], state_hgrn_s[l])
        xs, ss = forward_layer(xs, pos_s, w, past)
        st_p.append(sp)
        st_s.append(ss)

    def stack(sts, i):
        return jnp.stack([s[i] for s in sts], axis=0)

    y_prompt = rms_norm(xp, final_norm)
    y_sample = rms_norm(xs, final_norm)
    return (y_prompt, y_sample, stack(st_p, 0), stack(st_p, 1), stack(st_p, 2), stack(st_p, 3), stack(st_p, 4),
            stack(st_p, 5), stack(st_s, 0), stack(st_s, 1), stack(st_s, 2), stack(st_s, 3), stack(st_s, 4),
            stack(st_s, 5))
```

```python
import contextlib
import numpy as np
import ml_dtypes
import concourse.bass as bass
import concourse.mybir as mybir
from concourse.bass_utils import run_bass_kernel_spmd

F32 = mybir.dt.float32
BF16 = mybir.dt.bfloat16
I32 = mybir.dt.int32
AF = mybir.ActivationFunctionType
ALU = mybir.AluOpType
AX = mybir.AxisListType

D = 1024
NQH = 8
EPS = 1e-6
IN_OFF = dict(q=0, kvs=512, gate=1280, rgx=1304, rgg=1560, hq=1816, hf=2072, hi=2328, hg=2584)
D_IN = 2840


class V:
    __slots__ = ("ap", "key")

    def __init__(self, ap, key):
        self.ap = ap
        self.key = key

    def r(self, pat, **kw):
        return V(self.ap.rearrange(pat, **kw), self.key)

    def b(self, shape):
        return V(self.ap.broadcast_to(shape), self.key)

    def __getitem__(self, idx):
        return V(self.ap[idx], self.key)


class _Sub:
    def __init__(self, t, k):
        self.t = t
        self.k = k

    def __getitem__(self, idx):
        return V(self.t.t[idx], (self.t.name, self.k))


class T:
    def __init__(self, name, t):
        self.name = name
        self.t = t

    def __getitem__(self, idx):
        return V(self.t[idx], (self.name, None))

    def s(self, k):
        return _Sub(self, k)


class StopBuild(Exception):
    pass


class Prog:
    SEM_CAP = 30000

    def __init__(self, nc, es):
        self.nc = nc
        self.es = es
        self.ops = []
        self.state = {}
        self.bar = set()
        self.eng = dict(pe=nc.tensor, act=nc.scalar, dve=nc.vector, pool=nc.gpsimd, sp=nc.sync)
        self.last = {}
        self.dma_hist = {"sp": [], "pool": [], "act": []}

    def _touch(self, idx, key, write, deps):
        name, sub = key
        st = self.state.setdefault(name, {})
        if sub is None:
            ents = list(st.values())
        else:
            ents = [e for k, e in st.items() if k == sub or k is None]
        for e in ents:
            if e[0] is not None:
                deps.add(e[0])
            if write:
                deps.update(e[1])
        if write:
            if sub is None:
                st.clear()
                st[None] = [idx, []]
            else:
                st[sub] = [idx, []]
        else:
            st.setdefault(sub, [None, []])[1].append(idx)

    def add(self, eng, fn, r=(), w=(), dma=False, rg=None):
        idx = len(self.ops)
        deps = set(self.bar)
        for k in r:
            self._touch(idx, k, k[0].startswith("ps_"), deps)
        for k in w:
            self._touch(idx, k, True, deps)
        deps.discard(idx)
        self.ops.append(dict(eng=eng, fn=fn, deps=deps, dma=dma, rg=rg))
        self.last[eng] = idx
        if dma:
            self.dma_hist[eng].append(idx)
        return idx

    def barrier(self):
        b = set(self.last.values())
        for h in self.dma_hist.values():
            b.update(h[-32:])
        self.bar = b

    @staticmethod
    def _k(*vs):
        return [v.key for v in vs if isinstance(v, V)]

    @staticmethod
    def _a(v):
        return v.ap if isinstance(v, V) else v

    def mm(self, out, lhsT, rhs, start=True, stop=True):
        nc = self.nc
        self.add("pe", lambda: nc.tensor.matmul(out.ap, lhsT=lhsT.ap, rhs=rhs.ap, start=start, stop=stop,
                                                skip_group_check=True),
                 r=self._k(lhsT, rhs), w=[out.key], rg=lhsT.ap.start_partition())

    def tr(self, out, in_, ident):
        nc = self.nc
        self.add("pe", lambda: nc.tensor.transpose(out.ap, in_.ap, ident.ap), r=self._k(in_, ident), w=[out.key],
                 rg=in_.ap.start_partition())

    def act(self, out, in_, func, bias=None, scale=None, eng="act"):
        nc = self.nc
        kw = {}
        if bias is not None:
            kw["bias"] = self._a(bias)
        if scale is not None:
            kw["scale"] = self._a(scale)
        self.add("act", lambda: nc.scalar.activation(out.ap, in_.ap, func, **kw),
                 r=self._k(in_, bias, scale), w=[out.key])

    def tt(self, out, in0, in1, op, eng="dve"):
        e = self.eng[eng]
        self.add(eng, lambda: e.tensor_tensor(out.ap, in0.ap, in1.ap, op), r=self._k(in0, in1), w=[out.key])

    def ts(self, out, in0, s1, op0, s2=None, op1=None, eng="dve"):
        e = self.eng[eng]
        a1, a2 = self._a(s1), self._a(s2)
        if op1 is None:
            self.add(eng, lambda: e.tensor_scalar(out.ap, in0.ap, a1, None, op0), r=self._k(in0, s1), w=[out.key])
        else:
            self.add(eng, lambda: e.tensor_scalar(out.ap, in0.ap, a1, a2, op0, op1),
                     r=self._k(in0, s1, s2), w=[out.key])

    def stt(self, out, in0, scalar, in1, op0, op1):
        nc = self.nc
        sa = self._a(scalar)
        self.add("dve", lambda: nc.vector.scalar_tensor_tensor(out.ap, in0.ap, sa, in1.ap, op0, op1),
                 r=self._k(in0, scalar, in1), w=[out.key])

    def cp(self, out, in_, eng="dve"):
        if eng == "act":
            nc = self.nc
            self.add("act", lambda: nc.scalar.copy(out.ap, in_.ap), r=self._k(in_), w=[out.key])
        else:
            e = self.eng[eng]
            self.add(eng, lambda: e.tensor_copy(out.ap, in_.ap), r=self._k(in_), w=[out.key])

    def memset(self, out, val, eng="pool"):
        e = self.eng[eng]
        self.add(eng, lambda: e.memset(out.ap, val), w=[out.key])

    def recip(self, out, in_):
        nc = self.nc
        self.add("dve", lambda: nc.vector.reciprocal(out.ap, in_.ap), r=self._k(in_), w=[out.key])

    def reduce(self, out, in_, op, axis=AX.X):
        nc = self.nc
        self.add("dve", lambda: nc.vector.tensor_reduce(out.ap, in_.ap, axis, op), r=self._k(in_), w=[out.key])

    def scan(self, out, d0, d1, init, op0, op1):
        nc = self.nc
        ia = self._a(init)
        self.add("dve", lambda: nc.vector.tensor_tensor_scan(out.ap, d0.ap, d1.ap, ia, op0, op1),
                 r=self._k(d0, d1, init), w=[out.key])

    def max8(self, out, in_):
        nc = self.nc
        self.add("dve", lambda: nc.vector.max(out.ap, in_.ap), r=self._k(in_), w=[out.key])

    def match_replace(self, out, rep, vals, imm):
        nc = self.nc
        self.add("dve", lambda: nc.vector.match_replace(out.ap, rep.ap, vals.ap, imm), r=self._k(rep, vals),
                 w=[out.key])

    def dma(self, out, in_, eng="sp", **kw):
        e = self.eng[eng]
        self.add(eng, lambda: e.dma_start(out=out.ap, in_=in_.ap, **kw), r=self._k(in_), w=[out.key], dma=True)

    def idma(self, out, in_, idx_v, axis=0):
        nc = self.nc
        self.add("pool", lambda: nc.gpsimd.indirect_dma_start(
            out=out.ap, out_offset=None, in_=in_.ap,
            in_offset=bass.IndirectOffsetOnAxis(ap=idx_v.ap, axis=axis)),
            r=self._k(in_, idx_v), w=[out.key], dma=True)

    def emit(self):
        nc, es, ops = self.nc, self.es, self.ops

        def skip(d, o):
            return (not d["dma"]) and (not o["dma"]) and d["eng"] == "pe" and o["eng"] == "pe" and d["rg"] == o["rg"]

        need = set()
        for o in ops:
            for d in o["deps"]:
                if not skip(ops[d], o):
                    need.add(d)
        csem = {}
        ccount = {e: 0 for e in self.eng}
        RING = {"sp": 24, "pool": 12, "act": 4}
        rings = {e: [] for e in RING}
        rcount = {e: 0 for e in RING}
        rtot = {}
        known = {e: {} for e in self.eng}
        tok = [None] * len(ops)
        for i, o in enumerate(ops):
            e = o["eng"]
            E = self.eng[e]
            kn = known[e]
            waits = {}
            for d in o["deps"]:
                if skip(ops[d], o):
                    continue
                s_, v = tok[d]
                if waits.get(id(s_), (None, 0))[1] < v:
                    waits[id(s_)] = (s_, v)
            pre = None
            if o["dma"]:
                j = rcount[e]
                rcount[e] += 1
                slot = j % RING[e]
                if slot >= len(rings[e]):
                    rings[e].append(es.enter_context(nc.semaphore(f"r_{e}_{slot}")))
                rs = rings[e][slot]
                prev = rtot.get(id(rs), 0)
                if prev > 0 and waits.get(id(rs), (None, 0))[1] < prev:
                    waits[id(rs)] = (rs, prev)
                rtot[id(rs)] = prev + 16
                pre = (rs, prev + 16)
            for key, (s_, v) in waits.items():
                if kn.get(key, 0) >= v:
                    continue
                E.wait_ge(s_, v)
                kn[key] = v
            ins = o["fn"]()
            if o["dma"]:
                ins.then_inc(pre[0], 16)
                tok[i] = pre
            elif i in need:
                c = ccount[e]
                ep, val = c // self.SEM_CAP, c % self.SEM_CAP + 1
                if (e, ep) not in csem:
                    csem[(e, ep)] = es.enter_context(nc.semaphore(f"c_{e}_{ep}"))
                ins.then_inc(csem[(e, ep)], 1)
                ccount[e] = c + 1
                tok[i] = (csem[(e, ep)], val)
        for e in rings:
            for rs in rings[e]:
                nc.sync.wait_ge(rs, rtot[id(rs)])


def _fix_waits():
    pass


def rope_tab(pos):
    inv = 500000.0 ** (-np.arange(8, dtype=np.float32) * 2.0 / 16.0)
    ang = pos.astype(np.float32)[:, None] * inv.astype(np.float32)
    return np.cos(ang).astype(np.float32), np.sin(ang).astype(np.float32)


def make_consts(Tn, past, ns=0):
    NT = Tn // 128
    c = {}
    c["ident"] = np.eye(128, dtype=np.float32)
    cs, sn = rope_tab(np.arange(Tn))
    c["ropeP"] = np.stack([cs.reshape(NT, 128, 8).transpose(1, 0, 2), sn.reshape(NT, 128, 8).transpose(1, 0, 2)], 1)
    slot = np.arange(NT * 8)
    cs, sn = rope_tab(16 * slot + 15)
    c["ropeC"] = np.stack([cs.reshape(NT, 8, 8).transpose(1, 0, 2), sn.reshape(NT, 8, 8).transpose(1, 0, 2)], 1)
    m = np.zeros((128, 17, 128), np.float32)
    p = np.arange(128)[:, None]
    qi = np.arange(128)[None, :]
    for r in range(16):
        sp = p - 8 * r
        m[:, r, :] = np.where(sp < 0, 1.0, np.where(sp <= 7, (qi >= 16 * sp + 15), 0.0))
    m[:, 16, :] = 1.0
    c["cmpmask"] = m
    nslot = NT * 8
    KT = (nslot + 127) // 128
    sm = np.zeros((KT * 128, 128), np.float32)
    for s in range(1, nslot):
        c0 = 16 * (s - 1)
        for j in range(128):
            if c0 < 64 * j + 64 and c0 + 32 > 64 * j:
                sm[s, j] = 1.0
    c["smap"] = sm.reshape(KT, 128, 128).transpose(1, 0, 2).copy()
    A = np.zeros((128, 255), np.float32)
    M = np.zeros((128, 255), np.float32)
    for q in range(128):
        hi = 1 if q >= 64 else 0
        for ci in range(255):
            cc = ci - 127
            valid = cc <= hi
            forced = (cc == hi) or (cc == hi - 1)
            if not valid:
                A[q, ci] = -1e6
            elif forced:
                A[q, ci] = 1e6
            else:
                M[q, ci] = 1.0
    c["selA"] = A
    c["selM"] = M
    ki = np.arange(128)[:, None]
    c["tri"] = np.stack([(ki <= qi).astype(np.float32), (ki > qi).astype(np.float32)], 1)
    bo = np.zeros((128, 128), np.float32)
    bo[:64, :64] = 1.0
    bo[64:, 64:] = 1.0
    c["blockones"] = bo
    s64 = np.arange(64)[:, None]
    t64 = np.arange(64)[None, :]
    c["tri64"] = np.concatenate([(s64 <= t64).astype(np.float32)] * 2, 0)
    if ns:
        NCB = past // 16 - 1
        NSB = past // 64 + 1
        CUR = past // 64
        cs, sn = rope_tab(np.full((ns,), past))
        c["ropeS"] = np.stack([cs, sn], 1)
        cs, sn = rope_tab(16 * np.arange(NCB) + 31)
        c["ropeCS"] = np.stack([cs, sn], 1)
        sm = np.zeros((NCB, NSB), np.float32)
        for i in range(NCB):
            for j in range(NSB):
                if 16 * i < 64 * j + 64 and 16 * i + 32 > 64 * j:
                    sm[i, j] = 1.0
        c["smapS"] = sm
        A = np.zeros((2 * ns, 40), np.float32)
        M = np.zeros((2 * ns, 40), np.float32)
        assert NSB <= 40
        for j in range(40):
            if j >= NSB:
                A[:, j] = -1e6
            elif j in (0, CUR, CUR - 1):
                A[:, j] = 1e6
            else:
                M[:, j] = 1.0
        c["selAS"] = A
        c["selMS"] = M
        Dm = np.zeros((40, past), np.float32)
        u = np.arange(past)
        Dm[u // 64, u] = 1.0
        c["DS"] = Dm
        c["pcol"] = np.arange(128, dtype=np.float32).reshape(128, 1)
    return c


CONST_BF = ("cmpmask", "smap", "tri", "blockones", "tri64", "smapS", "DS")


def build(cfg):
    Tn, L = cfg["T"], cfg["L"]
    NT = Tn // 128
    KTC = (NT * 8 + 127) // 128
    WIN_T = min(4, NT)
    nc = bass.Bass("TRN2", target_bir_lowering=False)
    es = contextlib.ExitStack()
    P = Prog(nc, es)
    if cfg.get("dry"):
        P.add = lambda *a, **k: None
    NS = cfg.get("NS", 0)
    PAST = cfg.get("past", 0)
    consts = make_consts(Tn, PAST, NS)

    def dr_in(name, shape, dt=F32):
        return T(name, nc.dram_tensor(name, list(shape), dt, kind="ExternalInput"))

    def dr_out(name, shape, dt=F32):
        return T(name, nc.dram_tensor(name, list(shape), dt, kind="ExternalOutput"))

    def sb(name, shape, dt=F32):
        return T("s_" + name, es.enter_context(nc.sbuf_tensor("s_" + name, list(shape), dt)))

    def ps(name, shape=(128, 512), dt=F32):
        return T(name, es.enter_context(nc.psum_tensor(name, list(shape), dt)))

    x_in = dr_in("x_prompt", [Tn, D])
    w_in_d = dr_in("w_in", [L, D, D_IN])
    w_out_d = dr_in("w_out", [L, D, D])
    w_up_d = dr_in("w_up", [L, D, 4 * D])
    w_dn_d = dr_in("w_down", [L, 4 * D, D])
    normT_d = dr_in("normT", [128, L, 2, 8])
    fnormT_d = dr_in("fnormT", [128, 8])
    w1_d = dr_in("cmp_w1", [L, 2, 32, 64, 256])
    posT_d = dr_in("cmp_posT", [128, L, 2, 16])
    b1T_d = dr_in("cmp_b1T", [128, L, 2, 2])
    w2_d = dr_in("cmp_w2", [L, 2, 256, 64])
    b2_d = dr_in("cmp_b2", [L, 2, 64])
    rgpT_d = dr_in("rg_pT", [128, L, 2, 9])
    rgwa_d = dr_in("rg_wa", [L, 4, 64, 64])
    rgwx_d = dr_in("rg_wx", [L, 4, 64, 64])
    hgpT_d = dr_in("hg_pT", [128, 2, 2, L])
    cdr = {k: dr_in("c_" + k, v.shape) for k, v in consts.items()}

    y_out = dr_out("y_prompt", [Tn, D])
    o_cmp = dr_out("p_cmp_kv", [L, Tn, 256])
    o_sel = dr_out("p_sel_kv", [L, Tn, 256])
    o_win = dr_out("p_win_kv", [L, WIN_T * 128, 256])
    o_rgh = dr_out("p_rg_h", [L, 256])
    o_rgc = dr_out("p_rg_conv", [L, 3, 256])
    o_hgs = dr_out("p_hg_s", [L, 4, 64, 64])
    xs_d = T("xscr", nc.dram_tensor("xscr", [D, Tn], F32, kind="Internal"))
    if NS:
        NPG = PAST // 128
        WB = min(512, PAST)
        NPOOL = cfg["npool"]
        xsam_d = dr_in("x_sample", [NS, D])
        pt_d = dr_in("page_table", [1, NS * NPG], I32)
        ccmp_d = dr_in("cache_cmp", [L * NPOOL * 128, 256])
        csel_d = dr_in("cache_sel", [L * NPOOL * 128, 256])
        cwin_d = dr_in("cache_win", [L, NS, WB, 256])
        srgh_d = dr_in("st_rgh", [L, NS, 256])
        srgc_d = dr_in("st_rgc", [L, NS, 3, 256])
        shgs_d = dr_in("st_hgs", [L, NS, 4, 64, 64])
        ys_out = dr_out("y_sample", [NS, D])
        os_cmp = dr_out("s_cmp_kv", [L, NS, 256])
        os_sel = dr_out("s_sel_kv", [L, NS, 256])
        os_win = dr_out("s_win_kv", [L, NS, WB, 256])
        os_rgh = dr_out("s_rg_h", [L, NS, 256])
        os_rgc = dr_out("s_rg_conv", [L, NS, 3, 256])
        os_hgs = dr_out("s_hg_s", [L, NS, 4, 64, 64])

    C = {}
    for k, v in consts.items():
        shp = list(v.shape)
        if k in ("ropeP", "ropeC"):
            continue
        if k in CONST_BF:
            C[k] = sb("k_" + k, shp, BF16)
            P.dma(C[k][:], cdr[k][:], eng="pool")
        else:
            C[k] = sb("k_" + k, shp, F32)
            P.dma(C[k][:], cdr[k][:])
    ident = C["ident"]
    ident_bf = sb("ident_bf", [128, 128], BF16)
    P.cp(ident_bf[:], ident[:], eng="pool")
    ones_bf = sb("ones_bf", [128, 128], BF16)
    P.memset(ones_bf[:], 1.0)
    ones_f = sb("ones_f", [128, 128], F32)
    P.memset(ones_f[:], 1.0)
    normT = sb("normT", [128, L, 2, 8])
    P.dma(normT[:], normT_d[:])
    fnormT = sb("fnormT", [128, 8])
    P.dma(fnormT[:], fnormT_d[:])
    rgp = sb("rgp", [128, L, 2, 9])
    P.dma(rgp[:], rgpT_d[:])
    hgp = sb("hgp", [128, 2, 2, L])
    P.dma(hgp[:], hgpT_d[:])
    lbe = sb("lbe", [128, 2, L])
    lb = sb("lb", [128, 2, L])
    lbs = sb("lbs", [128, 2, 1])
    oml = sb("oml", [128, 2, L])
    P.act(lbe[:], hgp[:, :, 0, :], AF.Exp)
    P.reduce(lbs[:, :, 0], lbe[:], ALU.add)
    P.recip(lbs[:], lbs[:])
    P.tt(lbe[:], lbe[:], lbs[:].b([128, 2, L]), ALU.mult)
    P.memset(lb[:, :, 0:1], 0.0)
    for l in range(1, L):
        P.tt(lb[:, :, l:l + 1], lb[:, :, l - 1:l], lbe[:, :, l:l + 1], ALU.add)
    P.ts(oml[:], lb[:], -1.0, ALU.mult, 1.0, ALU.add)
    rgc = sb("rgc", [128, L, 2, 1])
    P.act(rgc[:], rgp[:, :, :, 7:8], AF.Exp, scale=-1.0)
    P.act(rgc[:], rgc[:], AF.Ln, bias=1.0)
    P.ts(rgc[:], rgc[:], -8.0, ALU.mult)
    rgc2 = sb("rgc2", [128, L, 2, 1])
    P.ts(rgc2[:], rgc[:], 2.0, ALU.mult)

    pa = [ps("ps_a%d" % i) for i in range(3)]
    pS = [ps("ps_s%d" % i) for i in range(2)]
    pM = ps("ps_m")
    pO = ps("ps_o")
    pI = ps("ps_i")
    rot = [0]

    def nps():
        rot[0] = (rot[0] + 1) % 3
        return pa[rot[0]]

    xT = sb("xT", [128, 8, 128])
    yT = sb("yT", [128, 8, 128], BF16)
    sq = sb("sq", [128, 8, 128], BF16)
    rstd = sb("rstd", [128, 128])

    def rmsnorm_T(xv, gv, out_bf, ntok):
        sqv = sq[:, :, :ntok]
        P.act(sqv, xv, AF.Square)
        pz = nps()
        for c in range(8):
            P.mm(pz[:, :ntok], ones_bf[:], sq[:, c, :ntok], start=(c == 0), stop=(c == 7))
        P.act(rstd[:, :ntok], pz[:, :ntok], AF.Sqrt, bias=EPS, scale=1.0 / D)
        P.recip(rstd[:, :ntok], rstd[:, :ntok])
        tmp = sb_tmpn[:, :, :ntok]
        P.tt(tmp, xv, rstd[:, :ntok].r("p (o t) -> p o t", o=1).b([128, 8, ntok]), ALU.mult)
        P.tt(out_bf, tmp, gv.r("p (c o) -> p c o", o=1).b([128, 8, ntok]), ALU.mult)

    sb_tmpn = sb("tmpn", [128, 8, 128])
    if NS:
        xsT = sb("xsT", [128, 8, NS])
        with nc.sbuf_tensor("s_xstok", [NS, D], F32) as _xt:
            xstok = T("s_xstok", _xt)
            P.dma(xstok[:], xsam_d[:])
            for c in range(8):
                pz = nps()
                P.tr(pz[:, 0:NS], xstok[:, c * 128:(c + 1) * 128], ident[0:NS, 0:NS])
                P.cp(xsT[:, c, :], pz[:, 0:NS], eng="act")
        P.barrier()
        ptb = sb("ptb", [128, NS * NPG], I32)
        idxt = sb("idxt", [128, NS * NPG], I32)
        P.dma(ptb[:], pt_d[:].b([128, NS * NPG]))
        P.ts(idxt[:], ptb[:], 128.0, ALU.mult, C["pcol"][:, 0:1], ALU.add)

    def chk(k):
        if cfg.get("stop") == k:
            raise StopBuild()

    try:
        _layers(locals())
    except StopBuild:
        pass
    P.emit()
    return nc, consts


def _layers(env):
    globals().update({k: v for k, v in env.items() if not k.startswith("__")})
    chk(1)
    for l in range(L):
        with contextlib.ExitStack() as les:
            cur = [les]

            def lsb(name, shape, dt=F32):
                return T("s_" + name + "_%d" % l, cur[0].enter_context(nc.sbuf_tensor("s_" + name + "_%d" % l, list(shape), dt)))

            P.barrier()
            Wout = lsb("Wout", [128, 8, D], BF16)
            for c in range(8):
                P.dma(Wout[:, c, :], w_out_d[l, c * 128:(c + 1) * 128, :], eng="pool", max_dma_last_dim=4096)
            W1 = lsb("W1", [128, 2, 16, 256], BF16)
            for kv in range(2):
                for h2 in range(2):
                    P.dma(W1[h2 * 64:(h2 + 1) * 64, kv, :, :], w1_d[l, kv].r("(s t) d h -> t d s h", t=2)[h2], eng="pool",
                          max_dma_last_dim=1024)
            W2 = lsb("W2", [128, 2, 2, 64], BF16)
            P.dma(W2[:].r("p k c d -> p (k c) d"), w2_d[l].r("k (c p) d -> p (k c) d", p=128), eng="pool")
            b2r = lsb("b2r", [1, 2, 64], BF16)
            P.dma(b2r[:], b2_d[l:l + 1, :, :], eng="pool")
            posT = lsb("posT", [128, 2, 16], BF16)
            P.dma(posT[:], posT_d[:, l, :, :], eng="pool")
            b1T = lsb("b1T", [128, 2, 2])
            P.dma(b1T[:], b1T_d[:, l, :, :])
            cb1 = lsb("cb1", [128, 2, 2])
            for kv in range(2):
                for hc in range(2):
                    pz = nps()
                    for s in range(16):
                        P.mm(pz[:, 0:1], W1[:, kv, s, hc * 128:(hc + 1) * 128], posT[:, kv, s:s + 1],
                             start=(s == 0), stop=(s == 15))
                    P.tt(cb1[:, kv, hc:hc + 1], pz[:, 0:1], b1T[:, kv, hc:hc + 1], ALU.add)
            BDf = lsb("BDf", [128, 2, 2, 128])
            BD = lsb("BD", [128, 2, 2, 128], BF16)
            P.memset(BDf[:], 0.0)
            for c2 in range(2):
                for hh in range(2):
                    blk = c2 * 2 + hh
                    P.dma(BDf[hh * 64:(hh + 1) * 64, c2, 0, hh * 64:(hh + 1) * 64], rgwa_d[l, blk])
                    P.dma(BDf[hh * 64:(hh + 1) * 64, c2, 1, hh * 64:(hh + 1) * 64], rgwx_d[l, blk])
            P.cp(BD[:], BDf[:], eng="pool")

            chk(2)
            pes = contextlib.ExitStack()
            cur[0] = pes
            Win = lsb("Win", [128, 8, D_IN], BF16)
            for c in range(8):
                P.dma(Win[:, c, :], w_in_d[l, c * 128:(c + 1) * 128, :], eng="pool", max_dma_last_dim=4096)
            KsT = lsb("KsT", [128, Tn], BF16)
            KwT = lsb("KwT", [128, 8 * 128], BF16)
            Vs = lsb("Vs", [128, NT, 2, 65], BF16)
            Vw = lsb("Vw", [128, 8, 2, 65], BF16)
            KcT = lsb("KcT", [128, KTC * 128], BF16)
            Vc = lsb("Vc", [128, KTC, 2, 65], BF16)
            P.memset(KcT[:], 0.0)
            P.memset(Vc[:], 0.0)
            P.memset(Vs[:, :, :, 64:65], 1.0)
            P.memset(Vw[:, :, :, 64:65], 1.0)
            rawT = lsb("rawT", [128, 2, 2, 144], BF16)
            rkv = lsb("rkv", [128, 256], BF16)
            P.memset(rawT[:], 0.0)
            xcat = lsb("xcat", [128, 2, 131])
            P.memset(xcat[:], 0.0)
            hst = lsb("hst", [128, 2, 1])
            P.memset(hst[:], 0.0)
            S32 = lsb("S32", [128, 2, 64])
            Sbf = lsb("Sbf", [128, 2, 64], BF16)
            P.memset(S32[:], 0.0)
            P.memset(Sbf[:], 0.0)

            Ptok = lsb("Ptok", [128, 1560])
            Rtok = lsb("Rtok", [128, 1280])
            rpt = lsb("rpt", [128, 2, 8])
            rct = lsb("rct", [8, 2, 8])
            ra = lsb("ra", [128, 8, 8])
            rb = lsb("rb", [128, 8, 8])
            QT = lsb("QT", [128, 512], BF16)
            Ffm = lsb("Ffm", [128, 10, 128])
            vtok = lsb("vtok", [128, 256], BF16)
            ET = lsb("ET", [128, 512], BF16)
            PT = lsb("PT", [128, 512], BF16)
            OTs = lsb("OTs", [65, 512])
            Otok = lsb("Otok", [128, 3, 8, 65])
            impT = lsb("impT", [128, 2, 128])
            zr = lsb("zr", [128, 512])
            score = lsb("score", [128, 128])
            sc2 = lsb("sc2", [128, 128])
            mx8 = lsb("mx8", [128, 8])
            thr = lsb("thr", [128, 1])
            selm = lsb("selm", [128, 2, 128])
            Xb = [lsb("Xb%d" % i, [128, 128]) for i in range(2)]
            gates = lsb("gates", [128, 24])
            coef = lsb("coef", [128, 3, 8])
            onsa = lsb("onsa", [128, 512])
            mixT = lsb("mixT", [128, 8, 128], BF16)
            hpre = lsb("hpre", [128, 16])
            hu = lsb("hu", [128, 16])
            hgl = lsb("hgl", [128, 2, 2, 16], BF16)
            cblk = lsb("cblk", [8, 2, 2, 64])
            cblkr = lsb("cblkr", [8, 2, 64])
            cv = lsb("cv", [8, 2, 65], BF16)
            P.memset(cv[:, :, 64:65], 1.0)
            xc = lsb("xc", [128, 2, 128])
            xcb = lsb("xcb", [128, 2, 128], BF16)
            rg_r = lsb("rg_r", [128, 128])
            rg_i = lsb("rg_i", [128, 128])
            rg_a = lsb("rg_a", [128, 128])
            rg_b = lsb("rg_b", [128, 128])
            rg_h = lsb("rg_h", [128, 2, 128])
            gl = lsb("gl", [128, 128])
            gl2 = lsb("gl2", [128, 128])
            hg_f = lsb("hg_f", [128, 128])
            hg_g = lsb("hg_g", [128, 128])
            hg_k = lsb("hg_k", [128, 128])
            bc = lsb("bc", [128, 128])
            nbr = lsb("nbr", [128, 2, 3])
            ebl = lsb("ebl", [128, 2])
            qs = lsb("qs", [128, 128])
            ex = lsb("ex", [128, 128])
            qE = lsb("qE", [128, 2, 128], BF16)
            qe = lsb("qe", [128, 2, 128], BF16)
            ke = lsb("ke", [128, 2, 128], BF16)
            kdT = lsb("kdT", [128, 2, 128])
            kdtok = lsb("kdtok", [128, 2, 128], BF16)
            Am = lsb("Am", [128, 2, 64], BF16)
            oT = lsb("oT", [128, 2, 128])
            osq = lsb("osq", [128, 128], BF16)
            orstd = lsb("orstd", [128, 128])

            for n in range(NT):
                t0 = n * 128
                if l == 0:
                    xtok = Ptok
                    P.dma(Ptok[:, 0:1024], x_in[t0:t0 + 128, :])
                    for c in range(8):
                        pz = nps()
                        P.tr(pz[:, 0:128], Ptok[:, c * 128:(c + 1) * 128], ident[:])
                        P.cp(xT[:, c, :], pz[:, 0:128], eng="act")
                else:
                    P.dma(xT[:], xs_d[:, t0:t0 + 128].r("(c p) t -> p c t", p=128))
                rmsnorm_T(xT[:], normT[:, l, 0, :], yT[:], 128)
                tm_chunks = [(0, 512), (512, 512), (1024, 280), (IN_OFF["hi"], 256)]
                dst = 0
                for (c0, cw) in tm_chunks:
                    pz = nps()
                    for c in range(8):
                        P.mm(pz[:, :cw], yT[:, c, :], Win[:, c, c0:c0 + cw], start=(c == 0), stop=(c == 7))
                    P.cp(Ptok[:, dst:dst + cw], pz[:, :cw], eng="act")
                    dst += cw
                fm_cols = [IN_OFF["rgx"], IN_OFF["rgx"] + 128, IN_OFF["rgg"], IN_OFF["rgg"] + 128,
                           IN_OFF["hq"], IN_OFF["hq"] + 128, IN_OFF["hf"], IN_OFF["hf"] + 128,
                           IN_OFF["hg"], IN_OFF["hg"] + 128]
                for i, c0 in enumerate(fm_cols):
                    pz = nps()
                    for c in range(8):
                        P.mm(pz[:, :128], Win[:, c, c0:c0 + 128], yT[:, c, :], start=(c == 0), stop=(c == 7))
                    P.cp(Ffm[:, i, :], pz[:, :128], eng=("act" if i % 2 else "dve"))
                chk(3)
                P.cp(Rtok[:], Ptok[:, 0:1280], eng="pool")
                P.dma(rpt[:], cdr["ropeP"][:, :, n, :])
                P.dma(rct[:], cdr["ropeC"][:, :, n, :])
                cosv = rpt[:, 0, :]
                sinv = rpt[:, 1, :]
                for (h0, nh) in ((0, 8), (12, 2), (16, 2)):
                    src = Ptok[:, h0 * 64:(h0 + nh) * 64].r("p (h d) -> p h d", d=64)
                    dstv = Rtok[:, h0 * 64:(h0 + nh) * 64].r("p (h d) -> p h d", d=64)
                    cb = cosv.r("p (o e) -> p o e", o=1).b([128, nh, 8])
                    sbv = sinv.r("p (o e) -> p o e", o=1).b([128, nh, 8])
                    x1, x2 = src[:, :, 0:8], src[:, :, 8:16]
                    P.tt(ra[:, :nh, :], x1, cb, ALU.mult)
                    P.tt(rb[:, :nh, :], x2, sbv, ALU.mult)
                    P.tt(dstv[:, :, 0:8], ra[:, :nh, :], rb[:, :nh, :], ALU.subtract)
                    P.tt(ra[:, :nh, :], x2, cb, ALU.mult)
                    P.tt(rb[:, :nh, :], x1, sbv, ALU.mult)
                    P.tt(dstv[:, :, 8:16], ra[:, :nh, :], rb[:, :nh, :], ALU.add)
                chk(31)
                P.dma(o_cmp[l, t0:t0 + 128, :], Rtok[:, 512:768])
                P.dma(o_sel[l, t0:t0 + 128, :], Rtok[:, 768:1024])
                if n >= NT - WIN_T:
                    w0 = (n - (NT - WIN_T)) * 128
                    P.dma(o_win[l, w0:w0 + 128, :], Rtok[:, 1024:1280])
                chk(32)
                pz = nps()
                for j in range(4):
                    P.tr(pz[:, j * 128:(j + 1) * 128], Rtok[:, j * 128:(j + 1) * 128], ident[:])
                chk(321)
                P.cp(QT[:], pz[:], eng="act")
                chk(322)
                P.cp(rkv[:], Rtok[:, 512:768], eng="pool")
                pz = nps()
                for kv in range(2):
                    for g in range(2):
                        cs_ = rkv[:, kv * 128 + g * 64:kv * 128 + (g + 1) * 64]
                        for h2 in range(2):
                            P.mm(pz[h2 * 64:(h2 + 1) * 64, (kv * 2 + g) * 128:(kv * 2 + g + 1) * 128], cs_, ident_bf[:])
                P.cp(rawT[0:64, :, :, 16:144], pz[0:64, :].r("p (k g t) -> p k g t", k=2, g=2), eng="dve")
                P.cp(rawT[64:128, :, :, 15:143], pz[64:128, :].r("p (k g t) -> p k g t", k=2, g=2), eng="dve")
                pz = nps()
                P.tr(pz[:, 256:384], Rtok[:, 768:896], ident[:])
                P.tr(pz[:, 384:512], Rtok[:, 1024:1152], ident[:])
                P.cp(KsT[:, t0:t0 + 128], pz[:, 256:384], eng="dve")
                P.cp(KwT[:, (n % 8) * 128:(n % 8 + 1) * 128], pz[:, 384:512], eng="dve")
                chk(33)
                P.cp(Vs[:, n, :, 0:64], Rtok[:, 896:1024].r("p (g d) -> p g d", g=2), eng="pool")
                P.cp(Vw[:, n % 8, :, 0:64], Rtok[:, 1152:1280].r("p (g d) -> p g d", g=2), eng="pool")
                P.cp(vtok[:], Ptok[:, 1304:1560], eng="pool")
                chk(4)
                for kv in range(2):
                    for hc in range(2):
                        pz = nps()
                        for g in range(2):
                            for s in range(16):
                                rhs = rawT[:, kv, g, 2 * s:2 * s + 113:16]
                                P.mm(pz[:, g * 8:(g + 1) * 8], W1[:, kv, s, hc * 128:(hc + 1) * 128],
                                     rhs, start=(s == 0), stop=(s == 15))
                        P.ts(hpre[:], pz[:, 0:16], cb1[:, kv, hc:hc + 1], ALU.add)
                        P.tt(hu[:], hpre[:], hpre[:], ALU.mult)
                        P.ts(hu[:], hu[:], 0.044715, ALU.mult, 1.0, ALU.add)
                        P.tt(hu[:], hu[:], hpre[:], ALU.mult)
                        P.act(hu[:], hu[:], AF.Sigmoid, scale=1.5957691216)
                        P.tt(hgl[:, kv, hc, :], hu[:], hpre[:], ALU.mult)
                chk(41)
                pz = nps()
                for kv in range(2):
                    for g in range(2):
                        o_ = pz[0:8, (kv * 2 + g) * 64:(kv * 2 + g + 1) * 64]
                        for hc in range(2):
                            P.mm(o_, hgl[:, kv, hc, g * 8:(g + 1) * 8], W2[:, kv, hc, :], start=(hc == 0), stop=False)
                        P.mm(o_, ones_bf[0:1, 0:8], b2r[0:1, kv, :], start=False, stop=True)
                P.cp(cblk[:].r("b k g d -> b (k g d)"), pz[0:8, 0:256], eng="act")
                chk(42)
                cC = rct[:, 0, :].r("p (o e) -> p o e", o=1).b([8, 2, 8])
                sC = rct[:, 1, :].r("p (o e) -> p o e", o=1).b([8, 2, 8])
                P.cp(cblkr[:], cblk[:, 0, :, :], eng="pool")
                x1, x2 = cblk[:, 0, :, 0:8], cblk[:, 0, :, 8:16]
                P.tt(ra[0:8, 0:2, :], x1, cC, ALU.mult)
                P.tt(rb[0:8, 0:2, :], x2, sC, ALU.mult)
                P.tt(cblkr[:, :, 0:8], ra[0:8, 0:2, :], rb[0:8, 0:2, :], ALU.subtract)
                P.tt(ra[0:8, 0:2, :], x2, cC, ALU.mult)
                P.tt(rb[0:8, 0:2, :], x1, sC, ALU.mult)
                P.tt(cblkr[:, :, 8:16], ra[0:8, 0:2, :], rb[0:8, 0:2, :], ALU.add)
                pz = nps()
                P.tr(pz[:, 0:8], cblkr[:].r("b g d -> b (g d)"), ident[0:8, 0:8])
                P.cp(KcT[:, n * 8:(n + 1) * 8], pz[:, 0:8], eng="act")
                chk(43)
                P.cp(cv[:, :, 0:64], cblk[:, 1, :, :], eng="pool")
                kt_n, po = (n * 8) // 128, (n * 8) % 128
                if n == 0:
                    P.dma(Vc[1:8, 0, :, :], cv[1:8, :, :])
                else:
                    P.dma(Vc[po:po + 8, kt_n, :, :], cv[:, :, :])
                chk(44)
                P.cp(rawT[0:64, :, :, 0:16], rawT[0:64, :, :, 128:144], eng="pool")
                P.cp(rawT[64:128, :, :, 0:15], rawT[64:128, :, :, 128:143], eng="pool")

                chk(5)
                def finish(br, g):
                    P.cp(OTs[:], pO[0:65, :], eng="act")
                    pz_ = nps()
                    for j in range(4):
                        P.tr(pz_[:, j * 65:(j + 1) * 65], OTs[:, j * 128:(j + 1) * 128], ident[0:65, 0:65])
                    P.cp(Otok[:, br, g * 4:(g + 1) * 4, :], pz_[:, 0:260].r("p (j e) -> p j e", e=65),
                         eng=("act" if g else "dve"))

                def attend(br, g, KT_list):
                    last = len(KT_list) - 1
                    for i, (kT, va, mk, z0, extra) in enumerate(KT_list):
                        sps = pS[i % 2]
                        P.mm(sps[:], kT, QT[g * 64:(g + 1) * 64, :], start=True, stop=True)
                        P.act(ET[:], sps[:], AF.Exp, scale=0.125)
                        src = ET
                        if mk is not None:
                            P.tt(PT[:].r("p (j q) -> p j q", j=4), ET[:].r("p (j q) -> p j q", j=4),
                                 mk.r("p (o q) -> p o q", o=1).b([128, 4, 128]), ALU.mult)
                            src = PT
                        if z0:
                            P.memset(src[0:1, :], 0.0, eng="dve")
                        P.mm(pO[0:65, :], va, src[:], start=(i == 0), stop=(i == last))
                        if extra is not None:
                            extra(i, src, i == 0, i == last)
                    finish(br, g)

                nkt = n // 16 + 1
                for g in range(2):
                    lst = []
                    for kt in range(nkt):
                        mk = C["cmpmask"][:, n % 16, :] if kt == nkt - 1 else None

                        def extra(i, src, first, last_, kt=kt):
                            P.mm(pI[:], C["smap"][:, kt, :], src[:], start=first, stop=last_)
                            P.mm(pM[:], ones_bf[:], src[:], start=first, stop=last_)
                        lst.append((KcT[g * 64:(g + 1) * 64, kt * 128:(kt + 1) * 128], Vc[:, kt, g, :], mk, kt == 0, extra))
                    attend(0, g, lst)
                    P.ts(zr[:], pM[:], 1e-30, ALU.max)
                    P.recip(zr[:], zr[:])
                    P.tt(zr[:], pI[:], zr[:], ALU.mult)
                    P.reduce(impT[:, g, :], zr[:].r("p (j q) -> p q j", j=4), ALU.add)
                for g in range(2):
                    pz = nps()
                    P.tr(pz[:, 0:128], impT[:, g, :], ident[:])
                    P.tt(score[:], pz[:, 0:128], C["selM"][:, 127 - 2 * n:255 - 2 * n], ALU.mult)
                    P.tt(score[:], score[:], C["selA"][:, 127 - 2 * n:255 - 2 * n], ALU.add)
                    P.memset(score[:, 0:1], 1e6, eng="dve")
                    P.max8(mx8[:], score[:])
                    P.match_replace(sc2[:], mx8[:], score[:], -1e30)
                    P.max8(mx8[:], sc2[:])
                    P.ts(thr[:], mx8[:, 7:8], -1e5, ALU.max)
                    P.ts(selm[:, g, :], score[:], thr[:, 0:1], ALU.is_ge)
                chk(6)
                for g in range(2):
                    lst = []
                    for kt in range(n + 1):
                        lst.append((KsT[g * 64:(g + 1) * 64, kt * 128:(kt + 1) * 128], Vs[:, kt, g, :], kt))
                    last = len(lst) - 1
                    for i, (kT, va, kt) in enumerate(lst):
                        sps = pS[i % 2]
                        P.mm(sps[:], kT, QT[g * 64:(g + 1) * 64, :], start=True, stop=True)
                        xb_ = Xb[i % 2]
                        P.cp(xb_[:].r("p (b o) -> p b o", o=64), selm[:, g, 2 * kt:2 * kt + 2].r("p (b o) -> p b o", o=1).b([128, 2, 64]), eng="pool")
                        P.tr(pM[:, 0:128], xb_[:], ident[:])
                        P.act(ET[:], sps[:], AF.Exp, scale=0.125)
                        if kt == n:
                            P.tt(sc2[:], pM[:, 0:128], C["tri"][:, 0, :], ALU.mult)
                            mk = sc2[:]
                        else:
                            mk = pM[:, 0:128]
                        P.tt(PT[:].r("p (j q) -> p j q", j=4), ET[:].r("p (j q) -> p j q", j=4),
                             mk.r("p (o q) -> p o q", o=1).b([128, 4, 128]), ALU.mult)
                        P.mm(pO[0:65, :], va, PT[:], start=(i == 0), stop=(i == last))
                    finish(1, g)
                chk(7)
                for g in range(2):
                    lst = []
                    for kt in range(max(0, n - 4), n + 1):
                        if kt == n:
                            mk = C["tri"][:, 0, :]
                        elif kt == n - 4:
                            mk = C["tri"][:, 1, :]
                        else:
                            mk = None
                        lst.append((KwT[g * 64:(g + 1) * 64, (kt % 8) * 128:(kt % 8 + 1) * 128], Vw[:, kt % 8, g, :], mk, False, None))
                    attend(2, g, lst)
                P.act(gates[:], Ptok[:, 1280:1304], AF.Sigmoid)
                P.ts(coef[:], Otok[:, :, :, 64], 1e-30, ALU.max)
                P.recip(coef[:], coef[:])
                P.tt(coef[:], coef[:], gates[:].r("p (h b) -> p b h", b=3), ALU.mult)
                for h in range(8):
                    ov = onsa[:, h * 64:(h + 1) * 64]
                    P.ts(ov, Otok[:, 0, h, 0:64], coef[:, 0, h:h + 1], ALU.mult)
                    P.stt(ov, Otok[:, 1, h, 0:64], coef[:, 1, h:h + 1], ov, ALU.mult, ALU.add)
                    P.stt(ov, Otok[:, 2, h, 0:64], coef[:, 2, h:h + 1], ov, ALU.mult, ALU.add)
                pz = nps()
                for j in range(4):
                    P.tr(pz[:, j * 128:(j + 1) * 128], onsa[:, j * 128:(j + 1) * 128], ident[:])
                P.cp(mixT[:, 0:4, :], pz[:].r("p (c t) -> p c t", c=4), eng="act")

                chk(8)
                for c2 in range(2):
                    pr = rgp[:, l, c2, :]
                    P.cp(xcat[:, c2, 3:131], Ffm[:, c2, :], eng="pool")
                    P.ts(xc[:, c2, :], xcat[:, c2, 0:128], pr[:, 0:1], ALU.mult, pr[:, 4:5], ALU.add)
                    for k in range(1, 4):
                        P.stt(xc[:, c2, :], xcat[:, c2, k:k + 128], pr[:, k:k + 1], xc[:, c2, :], ALU.mult, ALU.add)
                    P.cp(xcb[:, c2, :], xc[:, c2, :], eng="pool")
                    pz = nps()
                    P.mm(pz[:, 0:128], BD[:, c2, 0, :], xcb[:, c2, :])
                    P.act(rg_r[:], pz[:, 0:128], AF.Sigmoid, bias=pr[:, 5:6])
                    pz = nps()
                    P.mm(pz[:, 0:128], BD[:, c2, 1, :], xcb[:, c2, :])
                    P.act(rg_i[:], pz[:, 0:128], AF.Sigmoid, bias=pr[:, 6:7])
                    P.act(rg_a[:], rg_r[:], AF.Exp, scale=rgc[:, l, c2, :])
                    P.act(rg_b[:], rg_r[:], AF.Exp, scale=rgc2[:, l, c2, :])
                    P.ts(rg_b[:], rg_b[:], -1.0, ALU.mult, 1.0, ALU.add)
                    P.ts(rg_b[:], rg_b[:], 0.0, ALU.max)
                    P.act(rg_b[:], rg_b[:], AF.Sqrt)
                    P.tt(rg_i[:], rg_i[:], xc[:, c2, :], ALU.mult)
                    P.tt(rg_b[:], rg_b[:], rg_i[:], ALU.mult)
                    P.scan(rg_h[:, c2, :], rg_a[:], rg_b[:], hst[:, c2, :], ALU.mult, ALU.add)
                    P.cp(hst[:, c2, :], rg_h[:, c2, 127:128], eng="pool")
                    gx = Ffm[:, 2 + c2, :]
                    P.tt(gl[:], gx, gx, ALU.mult)
                    P.ts(gl[:], gl[:], 0.044715, ALU.mult, 1.0, ALU.add)
                    P.tt(gl[:], gl[:], gx, ALU.mult)
                    P.act(gl[:], gl[:], AF.Sigmoid, scale=1.5957691216)
                    P.tt(gl[:], gl[:], gx, ALU.mult)
                    P.tt(mixT[:, 4 + c2, :], gl[:], rg_h[:, c2, :], ALU.mult)
                    P.cp(xcat[:, c2, 0:3], xcat[:, c2, 128:131], eng="pool")
                if n == NT - 1:
                    for c2 in range(2):
                        P.dma(o_rgh[l, c2 * 128:(c2 + 1) * 128].r("(p o) -> p o", o=1), hst[:, c2, :])
                    pz = nps()
                    for c in range(8):
                        P.mm(pz[:, 0:256], yT[:, c, :], Win[:, c, IN_OFF["rgx"]:IN_OFF["rgx"] + 256], start=(c == 0),
                             stop=(c == 7))
                    P.cp(score[:, 0:128], pz[:, 0:128], eng="act")
                    P.cp(sc2[:, 0:128], pz[:, 128:256], eng="act")
                    P.dma(o_rgc[l, :, 0:128], score[125:128, 0:128])
                    P.dma(o_rgc[l, :, 128:256], sc2[125:128, 0:128])

                chk(9)
                for c2 in range(2):
                    hqv = Ffm[:, 4 + c2, :]
                    hfv = Ffm[:, 6 + c2, :]
                    P.act(hg_f[:], hfv, AF.Sigmoid)
                    P.ts(hg_f[:], hg_f[:], oml[:, c2, l:l + 1], ALU.mult, lb[:, c2, l:l + 1], ALU.add)
                    P.act(hg_g[:], hg_f[:], AF.Ln)
                    P.ts(hg_k[:], hg_f[:], -1.0, ALU.mult, 1.0, ALU.add)
                    P.act(qs[:], hqv, AF.Sigmoid)
                    P.tt(qs[:], qs[:], hqv, ALU.mult)
                    for ch in range(2):
                        sl = slice(ch * 64, (ch + 1) * 64)
                        P.scan(bc[:, sl], ones_f[:, 0:64], hg_g[:, sl], 0.0, ALU.mult, ALU.add)
                        P.cp(nbr[:, ch, 1:2], bc[:, ch * 64 + 31:ch * 64 + 32], eng="pool")
                        P.cp(nbr[:, ch, 2:3], bc[:, ch * 64 + 63:ch * 64 + 64], eng="pool")
                        P.ts(nbr[:, ch, 0:1], nbr[:, ch, 1:2], -1.0, ALU.mult)
                        P.act(ebl[:, ch:ch + 1], nbr[:, ch, 2:3], AF.Exp)
                        P.act(ex[:, sl], bc[:, sl], AF.Exp)
                        P.tt(qE[:, c2, sl], qs[:, sl], ex[:, sl], ALU.mult)
                        P.act(ex[:, sl], bc[:, sl], AF.Exp, bias=nbr[:, ch, 0:1])
                        P.tt(qe[:, c2, sl], qs[:, sl], ex[:, sl], ALU.mult)
                        P.act(ex[:, sl], bc[:, sl], AF.Exp, scale=-1.0, bias=nbr[:, ch, 1:2])
                        P.tt(ke[:, c2, sl], hg_k[:, sl], ex[:, sl], ALU.mult)
                        P.act(ex[:, sl], bc[:, sl], AF.Exp, scale=-1.0, bias=nbr[:, ch, 2:3])
                        P.tt(kdT[:, c2, sl], hg_k[:, sl], ex[:, sl], ALU.mult)
                    pz = nps()
                    P.tr(pz[:, 0:128], kdT[:, c2, :], ident[:])
                    P.cp(kdtok[:, c2, :], pz[:, 0:128], eng="act")
                    for ch in range(2):
                        sl = slice(ch * 64, (ch + 1) * 64)
                        pz = nps()
                        for hh in range(2):
                            hp = slice(hh * 64, (hh + 1) * 64)
                            P.mm(pz[sl, hh * 64:(hh + 1) * 64], ke[hp, c2, sl], qe[hp, c2, sl])
                        P.tt(Am[sl, :, :], pz[sl, 0:128].r("p (h t) -> p h t", h=2),
                             C["tri64"][sl, :].r("p (o t) -> p o t", o=1).b([64, 2, 64]), ALU.mult)
                        pz2 = nps()
                        for hh in range(2):
                            hp = slice(hh * 64, (hh + 1) * 64)
                            h = c2 * 2 + hh
                            P.mm(pz2[hp, 0:64], vtok[sl, h * 64:(h + 1) * 64], Am[sl, hh, :], start=True, stop=False)
                            P.mm(pz2[hp, 0:64], Sbf[hp, c2, :], qE[hp, c2, sl], start=False, stop=True)
                            P.mm(pz2[hp, 64:128], kdtok[sl, c2, hp], vtok[sl, h * 64:(h + 1) * 64])
                        P.cp(oT[:, c2, sl], pz2[:, 0:64], eng="act")
                        P.stt(S32[:, c2, :], S32[:, c2, :], ebl[:, ch:ch + 1], pz2[:, 64:128], ALU.mult, ALU.add)
                        P.cp(Sbf[:, c2, :], S32[:, c2, :], eng="pool")
                    P.act(osq[:], oT[:, c2, :], AF.Square)
                    pz = nps()
                    P.mm(pz[:, 0:128], C["blockones"][:], osq[:])
                    P.act(orstd[:], pz[:, 0:128], AF.Sqrt, bias=EPS, scale=1.0 / 64)
                    P.recip(orstd[:], orstd[:])
                    P.tt(orstd[:], orstd[:], oT[:, c2, :], ALU.mult)
                    hgv = Ffm[:, 8 + c2, :]
                    P.act(gl2[:], hgv, AF.Sigmoid)
                    P.tt(gl2[:], gl2[:], hgv, ALU.mult)
                    P.stt(mixT[:, 6 + c2, :], orstd[:], hgp[:, c2, 1, l:l + 1], gl2[:], ALU.mult, ALU.mult)
                if n == NT - 1:
                    for c2 in range(2):
                        P.dma(o_hgs[l, c2 * 2:c2 * 2 + 2].r("h k v -> (h k) v"), S32[:, c2, :])

                chk(10)
                for dc in range(8):
                    pz = nps()
                    for k in range(8):
                        P.mm(pz[:, 0:128], Wout[:, k, dc * 128:(dc + 1) * 128], mixT[:, k, :], start=(k == 0), stop=(k == 7))
                    P.tt(xT[:, dc, :], xT[:, dc, :], pz[:, 0:128], ALU.add)
                P.dma(xs_d[:, t0:t0 + 128].r("(c p) t -> p c t", p=128), xT[:])
                chk(11)
            pes.close()
            P.barrier()
            chk(12)
            if NS:
                des = contextlib.ExitStack()
                cur[0] = des
                decode_mixer(l, lsb, cur, Wout, W1, W2, b2r, cb1, BD)
                des.close()

        chk(13)
        with contextlib.ExitStack() as les:
            def lsb(name, shape, dt=F32):
                return T("s_" + name + "_f%d" % l, les.enter_context(nc.sbuf_tensor("s_" + name + "_f%d" % l, list(shape), dt)))
            P.barrier()
            Wup = lsb("Wup", [128, 8, 4 * D], BF16)
            Wdn = lsb("Wdn", [128, 32, D], BF16)
            for c in range(8):
                P.dma(Wup[:, c, :], w_up_d[l, c * 128:(c + 1) * 128, :], eng="pool", max_dma_last_dim=4096)
            for c in range(32):
                P.dma(Wdn[:, c, :], w_dn_d[l, c * 128:(c + 1) * 128, :], eng="pool", max_dma_last_dim=4096)
            FB = 256
            xB = lsb("xB", [128, 8, FB])
            yB = lsb("yB", [128, 8, FB], BF16)
            rsB = lsb("rsB", [128, FB])
            HT = lsb("HT", [128, 32, FB], BF16)
            tB = V(HT.t[:, 0:16, :].rearrange("p a b -> p (a b)").bitcast(F32).rearrange("p (c t) -> p c t", c=8), HT[:].key)
            sqB = V(HT.t[:, 16:24, :], HT[:].key)
            hr = lsb("hr", [128, FB])
            ytok = lsb("ytok", [128, D])
            for blk in range(Tn // FB):
                t0 = blk * FB
                P.dma(xB[:], xs_d[:, t0:t0 + FB].r("(c p) t -> p c t", p=128))

                def norm(gv, outv):
                    P.act(sqB, xB[:], AF.Square)
                    pz = nps()
                    for c in range(8):
                        P.mm(pz[:, :FB], ones_bf[:], sqB[:, c, :], start=(c == 0), stop=(c == 7))
                    P.act(rsB[:], pz[:, :FB], AF.Sqrt, bias=EPS, scale=1.0 / D)
                    P.recip(rsB[:], rsB[:])
                    P.tt(tB, xB[:], rsB[:].r("p (o t) -> p o t", o=1).b([128, 8, FB]), ALU.mult)
                    P.tt(outv, tB, gv.r("p (c o) -> p c o", o=1).b([128, 8, FB]), ALU.mult)
                norm(normT[:, l, 1, :], yB[:])
                for f in range(32):
                    pz = nps()
                    for c in range(8):
                        P.mm(pz[:, :FB], Wup[:, c, f * 128:(f + 1) * 128], yB[:, c, :], start=(c == 0), stop=(c == 7))
                    P.act(hr[:], pz[:, :FB], AF.Relu)
                    P.tt(HT[:, f, :], hr[:], hr[:], ALU.mult, eng="pool")
                for dc in range(8):
                    pz = nps()
                    for f in range(32):
                        P.mm(pz[:, :FB], Wdn[:, f, dc * 128:(dc + 1) * 128], HT[:, f, :], start=(f == 0), stop=(f == 31))
                    P.tt(xB[:, dc, :], xB[:, dc, :], pz[:, :FB], ALU.add)
                if l < L - 1:
                    P.dma(xs_d[:, t0:t0 + FB].r("(c p) t -> p c t", p=128), xB[:])
                else:
                    norm(fnormT[:], tB)
                    for tt_ in range(FB // 128):
                        for c in range(8):
                            pz = nps()
                            P.tr(pz[:, 0:128], tB[:, c, tt_ * 128:(tt_ + 1) * 128], ident[:])
                            P.cp(ytok[:, c * 128:(c + 1) * 128], pz[:, 0:128], eng=("act" if c % 2 else "dve"))
                        P.dma(y_out[t0 + tt_ * 128:t0 + (tt_ + 1) * 128, :], ytok[:])
            if NS:
                decode_ffn(l, lsb, Wup, Wdn)


FM_COLS = [IN_OFF["rgx"], IN_OFF["rgx"] + 128, IN_OFF["rgg"], IN_OFF["rgg"] + 128,
           IN_OFF["hq"], IN_OFF["hq"] + 128, IN_OFF["hf"], IN_OFF["hf"] + 128,
           IN_OFF["hg"], IN_OFF["hg"] + 128]


def rope_tok(src_t, dst_t, cosv, sinv, ra, rb, npart):
    for (h0, nh) in ((0, 8), (12, 2), (16, 2)):
        src = src_t[:, h0 * 64:(h0 + nh) * 64].r("p (h d) -> p h d", d=64)
        dstv = dst_t[:, h0 * 64:(h0 + nh) * 64].r("p (h d) -> p h d", d=64)
        cb = cosv.r("p (o e) -> p o e", o=1).b([npart, nh, 8])
        sbv = sinv.r("p (o e) -> p o e", o=1).b([npart, nh, 8])
        x1, x2 = src[:, :, 0:8], src[:, :, 8:16]
        P.tt(ra[:, :nh, :], x1, cb, ALU.mult)
        P.tt(rb[:, :nh, :], x2, sbv, ALU.mult)
        P.tt(dstv[:, :, 0:8], ra[:, :nh, :], rb[:, :nh, :], ALU.subtract)
        P.tt(ra[:, :nh, :], x2, cb, ALU.mult)
        P.tt(rb[:, :nh, :], x1, sbv, ALU.mult)
        P.tt(dstv[:, :, 8:16], ra[:, :nh, :], rb[:, :nh, :], ALU.add)


def gelu_tanh(out, x, tmp):
    P.tt(tmp, x, x, ALU.mult)
    P.ts(tmp, tmp, 0.044715, ALU.mult, 1.0, ALU.add)
    P.tt(tmp, tmp, x, ALU.mult)
    P.act(tmp, tmp, AF.Sigmoid, scale=1.5957691216)
    P.tt(out, tmp, x, ALU.mult)


def decode_mixer(l, lsb, cur, Wout, W1, W2, b2r, cb1, BD):
    NCB = PAST // 16 - 1
    NSB = PAST // 64 + 1
    WT = WB // 128
    ysT = lsb("ysT", [128, 8, NS], BF16)
    rmsnorm_T(xsT[:], normT[:, l, 0, :], ysT[:], NS)
    Ps = lsb("Ps", [NS, D_IN])
    Fs = lsb("Fs", [128, 10, NS])
    outer = cur[0]
    wes = contextlib.ExitStack()
    cur[0] = wes
    Win = lsb("WinD", [128, 8, D_IN], BF16)
    for c in range(8):
        P.dma(Win[:, c, :], w_in_d[l, c * 128:(c + 1) * 128, :], eng="pool", max_dma_last_dim=4096)
    for c0 in range(0, D_IN, 512):
        cw = min(512, D_IN - c0)
        pz = nps()
        for c in range(8):
            P.mm(pz[0:NS, :cw], ysT[:, c, :], Win[:, c, c0:c0 + cw], start=(c == 0), stop=(c == 7))
        P.cp(Ps[:, c0:c0 + cw], pz[0:NS, :cw], eng="act")
    for i, c0 in enumerate(FM_COLS):
        pz = nps()
        for c in range(8):
            P.mm(pz[:, :NS], Win[:, c, c0:c0 + 128], ysT[:, c, :], start=(c == 0), stop=(c == 7))
        P.cp(Fs[:, i, :], pz[:, :NS], eng="dve")
    wes.close()
    cur[0] = outer
    P.barrier()
    Rs = lsb("Rs", [NS, 1280])
    ras = lsb("ras", [NS, 8, 8])
    rbs = lsb("rbs", [NS, 8, 8])
    P.cp(Rs[:], Ps[:, 0:1280], eng="pool")
    rope_tok(Ps, Rs, C["ropeS"][:, 0, :], C["ropeS"][:, 1, :], ras, rbs, NS)
    P.dma(os_cmp[l], Rs[:, 512:768])
    P.dma(os_sel[l], Rs[:, 768:1024])
    P.dma(os_win.s("a")[l, :, 0:WB - 1, :], cwin_d[l, :, 1:WB, :])
    P.dma(os_win.s("b")[l, :, WB - 1, :], Rs[:, 1024:1280])
    QsT = lsb("QsT", [128, 4, NS], BF16)
    pz = nps()
    for j in range(4):
        P.tr(pz[:, j * NS:(j + 1) * NS], Rs[:, j * 128:(j + 1) * 128], ident[0:NS, 0:NS])
    P.cp(QsT[:].r("p j s -> p (j s)"), pz[:, 0:4 * NS], eng="act")

    idxl = lsb("idxl", [128, NS * NPG], I32)
    P.ts(idxl[:], idxt[:], float(l * NPOOL * 128), ALU.add)
    OTall = lsb("OTall", [65, 3, 2, 4, NS])
    impTall = lsb("impTall", [40, 2 * NS])
    P.memset(impTall[:], 0.0)
    pg = [lsb("pg0", [128, NPG, 256])] * 2
    rawS = lsb("rawS", [128, 2, 2, PAST], BF16)
    pgbf = lsb("pgbf", [128, 256], BF16)
    hp_ = lsb("hp_", [128, 2 * NCB])
    hu_ = lsb("hu_", [128, 2 * NCB])
    hglS = lsb("hglS", [128, 2, 2, 2 * NCB], BF16)
    cblkS = lsb("cblkS", [NCB, 2, 2, 64])
    cblkrS = lsb("cblkrS", [NCB, 2, 64])
    rcs = lsb("rcs", [NCB, 2, 8])
    rds = lsb("rds", [NCB, 2, 8])
    KcS = lsb("KcS", [128, NCB], BF16)
    cvS = lsb("cvS", [NCB, 2, 65], BF16)
    P.memset(cvS[:, :, 64:65], 1.0)
    ETs = lsb("ETs", [128, 4 * max(NPG, 4)], BF16)
    PTs = lsb("PTs", [128, 4 * max(NPG, 4)], BF16)
    zs = lsb("zs", [40, 4])
    zq = lsb("zq", [40, 4])
    for s in range(NS):
        pgb = pg[s % 2]
        for k in range(NPG):
            P.idma(pgb[:, k, :], ccmp_d[:], idxl[:, s * NPG + k:s * NPG + k + 1])
        for k in range(NPG):
            P.cp(pgbf[:], pgb[:, k, :], eng="pool")
            pz = nps()
            for kv in range(2):
                for g in range(2):
                    cs_ = pgbf[:, kv * 128 + g * 64:kv * 128 + (g + 1) * 64]
                    for h2 in range(2):
                        P.mm(pz[h2 * 64:(h2 + 1) * 64, (kv * 2 + g) * 128:(kv * 2 + g + 1) * 128], cs_, ident_bf[:])
            P.cp(rawS[0:64, :, :, k * 128:(k + 1) * 128], pz[0:64, :].r("p (k g t) -> p k g t", k=2, g=2), eng="act")
            if k == 0:
                P.cp(rawS[64:128, :, :, 0:127], pz[64:128, :].r("p (k g t) -> p k g t", k=2, g=2)[:, :, :, 1:128], eng="dve")
            else:
                P.cp(rawS[64:128, :, :, k * 128 - 1:(k + 1) * 128 - 1], pz[64:128, :].r("p (k g t) -> p k g t", k=2, g=2), eng="dve")
        for kv in range(2):
            for hc in range(2):
                pz = nps()
                for g in range(2):
                    for s16 in range(16):
                        rhs = rawS[:, kv, g, 2 * s16:2 * s16 + 16 * (NCB - 1) + 1:16]
                        P.mm(pz[:, g * NCB:(g + 1) * NCB], W1[:, kv, s16, hc * 128:(hc + 1) * 128],
                             rhs, start=(s16 == 0), stop=(s16 == 15))
                P.ts(hp_[:], pz[:, 0:2 * NCB], cb1[:, kv, hc:hc + 1], ALU.add)
                gelu_tanh(hglS[:, kv, hc, :], hp_[:], hu_[:])
        pz = nps()
        for kv in range(2):
            for g in range(2):
                o_ = pz[0:NCB, (kv * 2 + g) * 64:(kv * 2 + g + 1) * 64]
                for hc in range(2):
                    P.mm(o_, hglS[:, kv, hc, g * NCB:(g + 1) * NCB], W2[:, kv, hc, :], start=(hc == 0), stop=False)
                P.mm(o_, ones_bf[0:1, 0:NCB], b2r[0:1, kv, :], start=False, stop=True)
        P.cp(cblkS[:].r("b k g d -> b (k g d)"), pz[0:NCB, 0:256], eng="act")
        cC = C["ropeCS"][:, 0, :].r("p (o e) -> p o e", o=1).b([NCB, 2, 8])
        sC = C["ropeCS"][:, 1, :].r("p (o e) -> p o e", o=1).b([NCB, 2, 8])
        P.cp(cblkrS[:], cblkS[:, 0, :, :], eng="pool")
        x1, x2 = cblkS[:, 0, :, 0:8], cblkS[:, 0, :, 8:16]
        P.tt(rcs[:], x1, cC, ALU.mult)
        P.tt(rds[:], x2, sC, ALU.mult)
        P.tt(cblkrS[:, :, 0:8], rcs[:], rds[:], ALU.subtract)
        P.tt(rcs[:], x2, cC, ALU.mult)
        P.tt(rds[:], x1, sC, ALU.mult)
        P.tt(cblkrS[:, :, 8:16], rcs[:], rds[:], ALU.add)
        pz = nps()
        P.tr(pz[:, 0:NCB], cblkrS[:].r("b g d -> b (g d)"), ident[0:NCB, 0:NCB])
        P.cp(KcS[:], pz[:, 0:NCB], eng="act")
        P.cp(cvS[:, :, 0:64], cblkS[:, 1, :, :], eng="pool")
        for g in range(2):
            P.mm(pS[0][0:NCB, 0:4], KcS[g * 64:(g + 1) * 64, :], QsT[g * 64:(g + 1) * 64, :, s])
            P.act(ETs[0:NCB, 0:4], pS[0][0:NCB, 0:4], AF.Exp, scale=0.125)
            P.mm(pO[0:65, 0:4], cvS[:, g, :], ETs[0:NCB, 0:4])
            P.mm(pI[0:NSB, 0:4], C["smapS"][:, :], ETs[0:NCB, 0:4])
            P.mm(pM[0:NSB, 0:4], ones_bf[0:NCB, 0:NSB], ETs[0:NCB, 0:4])
            P.cp(OTall[:, 0, g, :, s], pO[0:65, 0:4], eng="act")
            P.ts(zs[0:NSB, :], pM[0:NSB, 0:4], 1e-30, ALU.max)
            P.recip(zs[0:NSB, :], zs[0:NSB, :])
            P.tt(zq[0:NSB, :], pI[0:NSB, 0:4], zs[0:NSB, :], ALU.mult)
            P.reduce(impTall[0:NSB, 2 * s + g:2 * s + g + 1], zq[0:NSB, :], ALU.add)
    scoreS = lsb("scoreS", [2 * NS, 40])
    sc2S = lsb("sc2S", [2 * NS, 40])
    mx8S = lsb("mx8S", [2 * NS, 8])
    thrS = lsb("thrS", [2 * NS, 1])
    selmS = lsb("selmS", [2 * NS, 40])
    selmST = lsb("selmST", [40, 2 * NS], BF16)
    pz = nps()
    P.tr(pz[0:2 * NS, 0:40], impTall[:], ident[0:40, 0:40])
    P.tt(scoreS[:], pz[0:2 * NS, 0:40], C["selMS"][:], ALU.mult)
    P.tt(scoreS[:], scoreS[:], C["selAS"][:], ALU.add)
    P.max8(mx8S[:], scoreS[:])
    P.match_replace(sc2S[:], mx8S[:], scoreS[:], -1e30)
    P.max8(mx8S[:], sc2S[:])
    P.ts(thrS[:], mx8S[:, 7:8], -1e5, ALU.max)
    P.ts(selmS[:], scoreS[:], thrS[:, 0:1], ALU.is_ge)
    pz = nps()
    P.tr(pz[0:40, 0:2 * NS], selmS[:], ident[0:2 * NS, 0:2 * NS])
    P.cp(selmST[:], pz[0:40, 0:2 * NS], eng="act")
    KsS = lsb("KsS", [128, PAST], BF16)
    VsS = lsb("VsS", [128, NPG, 2, 65], BF16)
    P.memset(VsS[:, :, :, 64:65], 1.0)
    wbuf = lsb("wbuf", [128, WT, 256])
    KwS = lsb("KwS", [128, WB], BF16)
    VwS = lsb("VwS", [128, WT, 2, 65], BF16)
    P.memset(VwS[:, :, :, 64:65], 1.0)
    for s in range(NS):
        pgb = pg[s % 2]
        for k in range(NPG):
            P.idma(pgb[:, k, :], csel_d[:], idxl[:, s * NPG + k:s * NPG + k + 1])
        P.dma(wbuf[:], cwin_d[l, s].r("(t p) c -> p t c", p=128))
        for k in range(NPG):
            pz = nps()
            P.tr(pz[:, 0:128], pgb[:, k, 0:128], ident[:])
            P.cp(KsS[:, k * 128:(k + 1) * 128], pz[:, 0:128], eng=("act" if k % 2 else "dve"))
            P.cp(VsS[:, k, :, 0:64], pgb[:, k, 128:256].r("p (g d) -> p g d", g=2), eng="pool")
        for t in range(WT):
            pz = nps()
            P.tr(pz[:, 0:128], wbuf[:, t, 0:128], ident[:])
            P.cp(KwS[:, t * 128:(t + 1) * 128], pz[:, 0:128], eng=("act" if t % 2 else "dve"))
            P.cp(VwS[:, t, :, 0:64], wbuf[:, t, 128:256].r("p (g d) -> p g d", g=2), eng="pool")
        for g in range(2):
            sp = pS[g]
            for k in range(NPG):
                P.mm(sp[:, k * 4:(k + 1) * 4], KsS[g * 64:(g + 1) * 64, k * 128:(k + 1) * 128], QsT[g * 64:(g + 1) * 64, :, s])
            for k in range(NPG):
                P.mm(pM[:, k:k + 1], C["DS"][:, k * 128:(k + 1) * 128], selmST[:, 2 * s + g:2 * s + g + 1])
            P.act(ETs[:, 0:4 * NPG], sp[:, 0:4 * NPG], AF.Exp, scale=0.125)
            P.tt(PTs[:, 0:4 * NPG].r("p (k h) -> p k h", h=4), ETs[:, 0:4 * NPG].r("p (k h) -> p k h", h=4),
                 pM[:, 0:NPG].r("p (k o) -> p k o", o=1).b([128, NPG, 4]), ALU.mult)
            for k in range(NPG):
                P.mm(pO[0:65, 0:4], VsS[:, k, g, :], PTs[:, k * 4:(k + 1) * 4], start=(k == 0), stop=(k == NPG - 1))
            P.cp(OTall[:, 1, g, :, s], pO[0:65, 0:4], eng="act")
            for t in range(WT):
                P.mm(sp[:, t * 4:(t + 1) * 4], KwS[g * 64:(g + 1) * 64, t * 128:(t + 1) * 128], QsT[g * 64:(g + 1) * 64, :, s])
            P.act(ETs[:, 0:4 * WT], sp[:, 0:4 * WT], AF.Exp, scale=0.125)
            if WB == 512:
                P.memset(ETs[0:1, 0:4], 0.0, eng="dve")
            for t in range(WT):
                P.mm(pO[0:65, 0:4], VwS[:, t, g, :], ETs[:, t * 4:(t + 1) * 4], start=(t == 0), stop=(t == WT - 1))
            P.cp(OTall[:, 2, g, :, s], pO[0:65, 0:4], eng="act")
    OtokS = lsb("OtokS", [NS, 3, 8, 65])
    for br in range(3):
        for g in range(2):
            pz = nps()
            for j in range(4):
                P.tr(pz[0:NS, j * 65:(j + 1) * 65], OTall[:, br, g, j, :], ident[0:65, 0:65])
            P.cp(OtokS[:, br, g * 4:(g + 1) * 4, :], pz[0:NS, 0:260].r("p (j e) -> p j e", e=65), eng="act")
    prod = lsb("prod", [NS, 4, 2, 64])
    dots = lsb("dots", [NS, 4, 2])
    enew = lsb("enew", [NS, 4, 2])
    tmpo = lsb("tmpo", [NS, 2, 4, 64])
    qv = Rs[:, 0:512].r("p (j g d) -> p j g d", j=4, g=2)
    for br, kc0, vc0 in ((1, 768, 896), (2, 1024, 1152)):
        kn = Rs[:, kc0:kc0 + 128].r("p (o g d) -> p o g d", o=1, g=2).b([NS, 4, 2, 64])
        P.tt(prod[:], qv, kn, ALU.mult)
        P.reduce(dots[:], prod[:], ALU.add)
        P.act(enew[:], dots[:], AF.Exp, scale=0.125)
        ev = enew[:].r("p j g -> p g j")
        vn = Rs[:, vc0:vc0 + 128].r("p (g o d) -> p g o d", g=2, o=1).b([NS, 2, 4, 64])
        P.tt(tmpo[:], vn, ev.r("p g (j o) -> p g j o", o=1).b([NS, 2, 4, 64]), ALU.mult)
        ob = OtokS[:, br, :, 0:64].r("p (g j) d -> p g j d", g=2)
        P.tt(ob, ob, tmpo[:], ALU.add)
        zb = OtokS[:, br, :, 64].r("p (g j) -> p g j", g=2)
        P.tt(zb, zb, ev, ALU.add)
    gatesS = lsb("gatesS", [NS, 24])
    coefS = lsb("coefS", [NS, 3, 8])
    onsaS = lsb("onsaS", [NS, 512])
    mixTs = lsb("mixTs", [128, 8, NS], BF16)
    P.act(gatesS[:], Ps[:, 1280:1304], AF.Sigmoid)
    P.ts(coefS[:], OtokS[:, :, :, 64], 1e-30, ALU.max)
    P.recip(coefS[:], coefS[:])
    P.tt(coefS[:], coefS[:], gatesS[:].r("p (h b) -> p b h", b=3), ALU.mult)
    for h in range(8):
        ov = onsaS[:, h * 64:(h + 1) * 64]
        P.ts(ov, OtokS[:, 0, h, 0:64], coefS[:, 0, h:h + 1], ALU.mult)
        P.stt(ov, OtokS[:, 1, h, 0:64], coefS[:, 1, h:h + 1], ov, ALU.mult, ALU.add)
        P.stt(ov, OtokS[:, 2, h, 0:64], coefS[:, 2, h:h + 1], ov, ALU.mult, ALU.add)
    pz = nps()
    for j in range(4):
        P.tr(pz[:, j * NS:(j + 1) * NS], onsaS[:, j * 128:(j + 1) * 128], ident[0:NS, 0:NS])
    P.cp(mixTs[:, 0:4, :].r("p c s -> p (c s)"), pz[:, 0:4 * NS], eng="act")
    rgct = lsb("rgct", [NS, 3, 256])
    rght = lsb("rght", [NS, 256])
    P.dma(rgct[:], srgc_d[l])
    P.dma(rght[:], srgh_d[l])
    P.dma(os_rgc.s("a")[l, :, 0:2, :], srgc_d[l, :, 1:3, :])
    P.dma(os_rgc.s("b")[l, :, 2, :], Ps[:, IN_OFF["rgx"]:IN_OFF["rgx"] + 256])
    xcs = lsb("xcs", [128, 2, 4, NS])
    h0T = lsb("h0T", [128, 2, NS])
    xcd = lsb("xcd", [128, NS])
    xcdb = lsb("xcdb", [128, NS], BF16)
    r_ = lsb("r_", [128, NS])
    i_ = lsb("i_", [128, NS])
    a_ = lsb("a_", [128, NS])
    b_ = lsb("b_", [128, NS])
    hT_ = lsb("hT_", [128, 2, NS])
    g1 = lsb("g1", [128, NS])
    g2 = lsb("g2", [128, NS])
    htok = lsb("htok", [NS, 256])
    for c2 in range(2):
        pz = nps()
        for k in range(3):
            P.tr(pz[:, k * NS:(k + 1) * NS], rgct[:, k, c2 * 128:(c2 + 1) * 128], ident[0:NS, 0:NS])
        P.tr(pz[:, 3 * NS:4 * NS], rght[:, c2 * 128:(c2 + 1) * 128], ident[0:NS, 0:NS])
        P.cp(xcs[:, c2, 0:3, :].r("p k s -> p (k s)"), pz[:, 0:3 * NS], eng="act")
        P.cp(h0T[:, c2, :], pz[:, 3 * NS:4 * NS], eng="act")
        P.cp(xcs[:, c2, 3, :], Fs[:, c2, :], eng="pool")
        pr = rgp[:, l, c2, :]
        P.ts(xcd[:], xcs[:, c2, 0, :], pr[:, 0:1], ALU.mult, pr[:, 4:5], ALU.add)
        for k in range(1, 4):
            P.stt(xcd[:], xcs[:, c2, k, :], pr[:, k:k + 1], xcd[:], ALU.mult, ALU.add)
        P.cp(xcdb[:], xcd[:], eng="pool")
        pz = nps()
        P.mm(pz[:, 0:NS], BD[:, c2, 0, :], xcdb[:])
        P.act(r_[:], pz[:, 0:NS], AF.Sigmoid, bias=pr[:, 5:6])
        pz = nps()
        P.mm(pz[:, 0:NS], BD[:, c2, 1, :], xcdb[:])
        P.act(i_[:], pz[:, 0:NS], AF.Sigmoid, bias=pr[:, 6:7])
        P.act(a_[:], r_[:], AF.Exp, scale=rgc[:, l, c2, :])
        P.act(b_[:], r_[:], AF.Exp, scale=rgc2[:, l, c2, :])
        P.ts(b_[:], b_[:], -1.0, ALU.mult, 1.0, ALU.add)
        P.ts(b_[:], b_[:], 0.0, ALU.max)
        P.act(b_[:], b_[:], AF.Sqrt)
        P.tt(i_[:], i_[:], xcd[:], ALU.mult)
        P.tt(b_[:], b_[:], i_[:], ALU.mult)
        P.tt(a_[:], a_[:], h0T[:, c2, :], ALU.mult)
        P.tt(hT_[:, c2, :], a_[:], b_[:], ALU.add)
        gelu_tanh(g2[:], Fs[:, 2 + c2, :], g1[:])
        P.tt(mixTs[:, 4 + c2, :], g2[:], hT_[:, c2, :], ALU.mult)
        pz = nps()
        P.tr(pz[0:NS, 0:128], hT_[:, c2, :], ident[:])
        P.cp(htok[:, c2 * 128:(c2 + 1) * 128], pz[0:NS, 0:128], eng="act")
    P.dma(os_rgh[l], htok[:])
    Sst = lsb("Sst", [128, NS, 2, 64])
    for hh in range(2):
        P.dma(Sst[hh * 64:(hh + 1) * 64, :, :, :], shgs_d[l].r("s (c hh) k v -> hh k s c v", hh=2)[hh])
    fT = lsb("fT", [128, 2, NS])
    kT_ = lsb("kT_", [128, 2, NS])
    qT_ = lsb("qT_", [128, 2, NS])
    for c2 in range(2):
        P.act(fT[:, c2, :], Fs[:, 6 + c2, :], AF.Sigmoid)
        P.ts(fT[:, c2, :], fT[:, c2, :], oml[:, c2, l:l + 1], ALU.mult, lb[:, c2, l:l + 1], ALU.add)
        P.ts(kT_[:, c2, :], fT[:, c2, :], -1.0, ALU.mult, 1.0, ALU.add)
        P.act(qT_[:, c2, :], Fs[:, 4 + c2, :], AF.Sigmoid)
        P.tt(qT_[:, c2, :], qT_[:, c2, :], Fs[:, 4 + c2, :], ALU.mult)
    vdiag = lsb("vdiag", [NS, NS, 128])
    t2c = lsb("t2c", [128, 4, 2, 64])
    fb = fT[:].r("p c (s o) -> p s c o", o=1).b([128, NS, 2, 64])
    kb = kT_[:].r("p c (s o) -> p s c o", o=1).b([128, NS, 2, 64])
    P.tt(Sst[:], Sst[:], fb, ALU.mult)
    SPC = 4
    for hh in range(2):
        hp = slice(hh * 64, (hh + 1) * 64)
        v_hh = Ps[:, IN_OFF["hi"]:IN_OFF["hi"] + 256].r("p (c hh v) -> p hh c v", hh=2, v=64)[:, hh]
        P.tt(vdiag[:].r("p s (c v) -> p s c v", c=2), v_hh.r("p (o c) v -> p o c v", o=1).b([NS, NS, 2, 64]),
             ident[0:NS, 0:NS].r("p (s o t) -> p s o t", o=1, t=1).b([NS, NS, 2, 64]), ALU.mult)
        for q4 in range((NS + SPC - 1) // SPC):
            ns_ = min(SPC, NS - q4 * SPC)
            pz = nps()
            P.mm(pz[hp, 0:ns_ * 128], ones_f[0:NS, 0:64], vdiag[:, q4 * SPC:q4 * SPC + ns_, :].r("p s c -> p (s c)"))
            sl = slice(q4 * SPC, q4 * SPC + ns_)
            P.tt(t2c[hp, 0:ns_], pz[hp, 0:ns_ * 128].r("p (s c v) -> p s c v", s=ns_, c=2), kb[hp, sl, :, :], ALU.mult)
            P.tt(Sst[hp, sl, :, :], Sst[hp, sl, :, :], t2c[hp, 0:ns_], ALU.add)
    Snew = Sst
    for hh in range(2):
        P.dma(os_hgs[l].r("s (c hh) k v -> hh k s c v", hh=2)[hh], Snew[hh * 64:(hh + 1) * 64, :, :, :])
    pz = nps()
    for s in range(NS):
        for c2 in range(2):
            for hh in range(2):
                hp = slice(hh * 64, (hh + 1) * 64)
                P.mm(pz[hp, c2 * NS + s:c2 * NS + s + 1], Snew[hp, s, c2, :], qT_[hp, c2, s:s + 1])
    oTs = lsb("oTs", [128, 2, NS])
    P.cp(oTs[:].r("p c s -> p (c s)"), pz[:, 0:2 * NS], eng="act")
    osq_ = lsb("osq_", [128, NS], BF16)
    ors_ = lsb("ors_", [128, NS])
    for c2 in range(2):
        P.act(osq_[:], oTs[:, c2, :], AF.Square)
        pz = nps()
        P.mm(pz[:, 0:NS], C["blockones"][:], osq_[:])
        P.act(ors_[:], pz[:, 0:NS], AF.Sqrt, bias=EPS, scale=1.0 / 64)
        P.recip(ors_[:], ors_[:])
        P.tt(ors_[:], ors_[:], oTs[:, c2, :], ALU.mult)
        P.act(g1[:], Fs[:, 8 + c2, :], AF.Sigmoid)
        P.tt(g1[:], g1[:], Fs[:, 8 + c2, :], ALU.mult)
        P.stt(mixTs[:, 6 + c2, :], ors_[:], hgp[:, c2, 1, l:l + 1], g1[:], ALU.mult, ALU.mult)
    for dc in range(8):
        pz = nps()
        for k in range(8):
            P.mm(pz[:, 0:NS], Wout[:, k, dc * 128:(dc + 1) * 128], mixTs[:, k, :], start=(k == 0), stop=(k == 7))
        P.tt(xsT[:, dc, :], xsT[:, dc, :], pz[:, 0:NS], ALU.add)


def decode_ffn(l, lsb, Wup, Wdn):
    ysB = lsb("ysB", [128, 8, NS], BF16)
    HTs = lsb("HTs", [128, 32, NS], BF16)
    hrs = lsb("hrs", [128, NS])
    rmsnorm_T(xsT[:], normT[:, l, 1, :], ysB[:], NS)
    for f in range(32):
        pz = nps()
        for c in range(8):
            P.mm(pz[:, :NS], Wup[:, c, f * 128:(f + 1) * 128], ysB[:, c, :], start=(c == 0), stop=(c == 7))
        P.act(hrs[:], pz[:, :NS], AF.Relu)
        P.tt(HTs[:, f, :], hrs[:], hrs[:], ALU.mult, eng="pool")
    for dc in range(8):
        pz = nps()
        for f in range(32):
            P.mm(pz[:, :NS], Wdn[:, f, dc * 128:(dc + 1) * 128], HTs[:, f, :], start=(f == 0), stop=(f == 31))
        P.tt(xsT[:, dc, :], xsT[:, dc, :], pz[:, :NS], ALU.add)
    if l == L - 1:
        yfs = lsb("yfs", [128, 8, NS])
        ystok = lsb("ystok", [NS, D])
        sqv = sq[:, :, :NS]
        P.act(sqv, xsT[:], AF.Square)
        pz = nps()
        for c in range(8):
            P.mm(pz[:, :NS], ones_bf[:], sq[:, c, :NS], start=(c == 0), stop=(c == 7))
        P.act(rstd[:, :NS], pz[:, :NS], AF.Sqrt, bias=EPS, scale=1.0 / D)
        P.recip(rstd[:, :NS], rstd[:, :NS])
        P.tt(yfs[:], xsT[:], rstd[:, :NS].r("p (o t) -> p o t", o=1).b([128, 8, NS]), ALU.mult)
        P.tt(yfs[:], yfs[:], fnormT[:].r("p (c o) -> p c o", o=1).b([128, 8, NS]), ALU.mult)
        for c in range(8):
            pz = nps()
            P.tr(pz[0:NS, 0:128], yfs[:, c, :], ident[:])
            P.cp(ystok[:, c * 128:(c + 1) * 128], pz[0:NS, 0:128], eng="act")
        P.dma(ys_out[:], ystok[:])


QPERM = np.concatenate([np.arange(h * 64, (h + 1) * 64) for h in (0, 4, 1, 5, 2, 6, 3, 7)])


def host_inputs(cfg, inp, consts, b, core=0):
    L = cfg["L"]
    f = lambda a: np.ascontiguousarray(np.asarray(a, dtype=np.float32))
    m = {}
    m["x_prompt"] = f(inp["x_prompt"][b])
    w_in = np.asarray(inp["w_in"], np.float32).copy()
    w_in[:, :, 0:512] = w_in[:, :, QPERM]
    m["w_in"] = f(w_in)
    m["w_out"] = f(inp["w_out"])
    m["w_up"] = f(inp["w_up"])
    m["w_down"] = f(inp["w_down"])
    nm = np.stack([np.asarray(inp["norm_mix"]), np.asarray(inp["norm_ffn"])], 1)
    m["normT"] = f(nm.reshape(L, 2, 8, 128).transpose(3, 0, 1, 2))
    m["fnormT"] = f(np.asarray(inp["final_norm"]).reshape(8, 128).T)
    m["cmp_w1"] = f(inp["cmp_w1"])
    m["cmp_posT"] = f(np.asarray(inp["cmp_pos"]).reshape(L, 2, 16, 2, 64).transpose(3, 4, 0, 1, 2).reshape(128, L, 2, 16))
    m["cmp_b1T"] = f(np.asarray(inp["cmp_b1"]).reshape(L, 2, 2, 128).transpose(3, 0, 1, 2))
    m["cmp_w2"] = f(inp["cmp_w2"])
    m["cmp_b2"] = f(inp["cmp_b2"])
    cw = np.asarray(inp["rg_conv_w"])
    rp = np.zeros((L, 9, 256), np.float32)
    rp[:, 0:4] = cw
    rp[:, 4] = np.asarray(inp["rg_conv_b"])
    rp[:, 5] = np.asarray(inp["rg_ba"])
    rp[:, 6] = np.asarray(inp["rg_bx"])
    rp[:, 7] = np.asarray(inp["rg_lambda"])
    m["rg_pT"] = f(rp.reshape(L, 9, 2, 128).transpose(3, 0, 2, 1))
    m["rg_wa"] = f(inp["rg_wa"])
    m["rg_wx"] = f(inp["rg_wx"])
    hp = np.stack([np.asarray(inp["hg_lower_bounds"]), np.asarray(inp["hg_gain"])], 0)
    m["hg_pT"] = f(hp.reshape(2, L, 2, 128).transpose(3, 2, 0, 1))
    for k, v in consts.items():
        m["c_" + k] = f(v)
    NS = cfg.get("NS", 0)
    if NS:
        sl = slice(core * NS, (core + 1) * NS)
        past = cfg["past"]
        npg = past // 128
        wb = min(512, past)
        m["x_sample"] = f(np.asarray(inp["x_sample"])[sl, 0, :])
        m["page_table"] = np.ascontiguousarray(np.asarray(inp["page_table"])[sl].reshape(1, NS * npg).astype(np.int32))
        m["cache_cmp"] = f(inp["cache_nsa_cmp_kv"]).reshape(-1, 256)
        m["cache_sel"] = f(inp["cache_nsa_sel_kv"]).reshape(-1, 256)
        m["cache_win"] = f(np.asarray(inp["cache_nsa_win_kv"])[:, sl]).reshape(L, NS, wb, 256)
        m["st_rgh"] = f(np.asarray(inp["state_rglru_h"])[:, sl])
        m["st_rgc"] = f(np.asarray(inp["state_rglru_conv"])[:, sl])
        m["st_hgs"] = f(np.asarray(inp["state_hgrn_s"])[:, sl])
    return m


_CACHE = {}


def run(cfg, inp, ncores=8):
    key = tuple(sorted(cfg.items()))
    if key not in _CACHE:
        _CACHE[key] = build(cfg)
    nc, consts = _CACHE[key]
    B = np.asarray(inp["x_prompt"]).shape[0]
    maps = [host_inputs(cfg, inp, consts, c % B, c) for c in range(ncores)]
    res = run_bass_kernel_spmd(nc, maps, core_ids=list(range(ncores)))
    return res.results


def kernel(**inp):
    xp = np.asarray(inp["x_prompt"])
    B, Tn, _ = xp.shape
    L = np.asarray(inp["w_in"]).shape[0]
    NDEC = np.asarray(inp["x_sample"]).shape[0]
    npg = np.asarray(inp["page_table"]).shape[1]
    past = npg * 128
    ncores = 8
    NS = NDEC // ncores
    npool = np.asarray(inp["cache_nsa_cmp_kv"]).shape[1]
    cfg = dict(T=Tn, L=L, past=past, NS=NS, npool=npool)
    r = run(cfg, inp, ncores)
    wb = min(512, past)
    wt = min(512, Tn)
    f = np.float32
    y_prompt = np.stack([r[b]["y_prompt"] for b in range(B)], 0).astype(f)
    y_sample = np.concatenate([r[c]["y_sample"] for c in range(ncores)], 0).reshape(NDEC, 1, D).astype(f)

    def pst(name, shp):
        return np.stack([np.asarray(r[b][name]).reshape((L,) + shp) for b in range(B)], 1).astype(f)

    def sst(name, shp):
        return np.concatenate([np.asarray(r[c][name]).reshape((L, NS) + shp) for c in range(ncores)], 1).astype(f)

    return (y_prompt, y_sample,
            pst("p_cmp_kv", (Tn, 2, 2, 64)), pst("p_sel_kv", (Tn, 2, 2, 64)), pst("p_win_kv", (wt, 2, 2, 64)),
            pst("p_rg_h", (256,)), pst("p_rg_conv", (3, 256)), pst("p_hg_s", (4, 64, 64)),
            sst("s_cmp_kv", (1, 2, 2, 64)), sst("s_sel_kv", (1, 2, 2, 64)), sst("s_win_kv", (wb, 2, 2, 64)),
            sst("s_rg_h", (256,)), sst("s_rg_conv", (3, 256)), sst("s_hg_s", (4, 64, 64)))
```

```python
import contextlib
import numpy as np
import ml_dtypes
import concourse.bass as bass
import concourse.mybir as mybir
from concourse.bass_utils import run_bass_kernel_spmd

F32 = mybir.dt.float32
BF16 = mybir.dt.bfloat16
I32 = mybir.dt.int32
AF = mybir.ActivationFunctionType
ALU = mybir.AluOpType
AX = mybir.AxisListType

D = 1024
NQH = 8
EPS = 1e-6
IN_OFF = dict(q=0, kvs=512, gate=1280, rgx=1304, rgg=1560, hq=1816, hf=2072, hi=2328, hg=2584)
D_IN = 2840


class V:
    __slots__ = ("ap", "key")

    def __init__(self, ap, key):
        self.ap = ap
        self.key = key

    def r(self, pat, **kw):
        return V(self.ap.rearrange(pat, **kw), self.key)

    def b(self, shape):
        return V(self.ap.broadcast_to(shape), self.key)

    def __getitem__(self, idx):
        return V(self.ap[idx], self.key)


class _Sub:
    def __init__(self, t, k):
        self.t = t
        self.k = k

    def __getitem__(self, idx):
        return V(self.t.t[idx], (self.t.name, self.k))


class T:
    def __init__(self, name, t):
        self.name = name
        self.t = t

    def __getitem__(self, idx):
        return V(self.t[idx], (self.name, None))

    def s(self, k):
        return _Sub(self, k)


class StopBuild(Exception):
    pass


class Prog:
    SEM_CAP = 30000

    def __init__(self, nc, es):
        self.nc = nc
        self.es = es
        self.ops = []
        self.state = {}
        self.bar = set()
        self.eng = dict(pe=nc.tensor, act=nc.scalar, dve=nc.vector, pool=nc.gpsimd, sp=nc.sync)
        self.last = {}
        self.nosame = False
        self.dma_hist = {"sp": [], "pool": [], "act": []}

    def _touch(self, idx, key, write, deps):
        name, sub = key
        st = self.state.setdefault(name, {})
        if sub is None:
            ents = list(st.values())
        else:
            ents = [e for k, e in st.items() if k == sub or k is None]
        for e in ents:
            if e[0] is not None:
                deps.add(e[0])
            if write:
                deps.update(e[1])
        if write:
            if sub is None:
                st.clear()
                st[None] = [idx, []]
            else:
                st[sub] = [idx, []]
        else:
            st.setdefault(sub, [None, []])[1].append(idx)

    def add(self, eng, fn, r=(), w=(), dma=False, rg=None):
        idx = len(self.ops)
        deps = set(self.bar)
        for k in r:
            self._touch(idx, k, k[0].startswith("ps_"), deps)
        for k in w:
            self._touch(idx, k, True, deps)
        deps.discard(idx)
        self.ops.append(dict(eng=eng, fn=fn, deps=deps, dma=dma, rg=rg))
        self.last[eng] = idx
        if dma:
            self.dma_hist[eng].append(idx)
        return idx

    def barrier(self):
        b = set(self.last.values())
        for h in self.dma_hist.values():
            b.update(h[-32:])
        self.bar = b

    @staticmethod
    def _k(*vs):
        return [v.key for v in vs if isinstance(v, V)]

    @staticmethod
    def _a(v):
        return v.ap if isinstance(v, V) else v

    def mm(self, out, lhsT, rhs, start=True, stop=True):
        nc = self.nc
        self.add("pe", lambda: nc.tensor.matmul(out.ap, lhsT=lhsT.ap, rhs=rhs.ap, start=start, stop=stop,
                                                skip_group_check=True),
                 r=self._k(lhsT, rhs), w=[out.key], rg=lhsT.ap.start_partition())

    def tr(self, out, in_, ident):
        nc = self.nc
        self.add("pe", lambda: nc.tensor.transpose(out.ap, in_.ap, ident.ap), r=self._k(in_, ident), w=[out.key],
                 rg=in_.ap.start_partition())

    def act(self, out, in_, func, bias=None, scale=None, eng="act"):
        nc = self.nc
        kw = {}
        if bias is not None:
            kw["bias"] = self._a(bias)
        if scale is not None:
            kw["scale"] = self._a(scale)
        self.add("act", lambda: nc.scalar.activation(out.ap, in_.ap, func, **kw),
                 r=self._k(in_, bias, scale), w=[out.key])

    def tt(self, out, in0, in1, op, eng="dve"):
        e = self.eng[eng]
        self.add(eng, lambda: e.tensor_tensor(out.ap, in0.ap, in1.ap, op), r=self._k(in0, in1), w=[out.key])

    def ts(self, out, in0, s1, op0, s2=None, op1=None, eng="dve"):
        e = self.eng[eng]
        a1, a2 = self._a(s1), self._a(s2)
        if op1 is None:
            self.add(eng, lambda: e.tensor_scalar(out.ap, in0.ap, a1, None, op0), r=self._k(in0, s1), w=[out.key])
        else:
            self.add(eng, lambda: e.tensor_scalar(out.ap, in0.ap, a1, a2, op0, op1),
                     r=self._k(in0, s1, s2), w=[out.key])

    def stt(self, out, in0, scalar, in1, op0, op1):
        nc = self.nc
        sa = self._a(scalar)
        self.add("dve", lambda: nc.vector.scalar_tensor_tensor(out.ap, in0.ap, sa, in1.ap, op0, op1),
                 r=self._k(in0, scalar, in1), w=[out.key])

    def cp(self, out, in_, eng="dve"):
        if eng == "act":
            nc = self.nc
            self.add("act", lambda: nc.scalar.copy(out.ap, in_.ap), r=self._k(in_), w=[out.key])
        else:
            e = self.eng[eng]
            self.add(eng, lambda: e.tensor_copy(out.ap, in_.ap), r=self._k(in_), w=[out.key])

    def memset(self, out, val, eng="pool"):
        e = self.eng[eng]
        self.add(eng, lambda: e.memset(out.ap, val), w=[out.key])

    def recip(self, out, in_):
        nc = self.nc
        self.add("dve", lambda: nc.vector.reciprocal(out.ap, in_.ap), r=self._k(in_), w=[out.key])

    def reduce(self, out, in_, op, axis=AX.X):
        nc = self.nc
        self.add("dve", lambda: nc.vector.tensor_reduce(out.ap, in_.ap, axis, op), r=self._k(in_), w=[out.key])

    def scan(self, out, d0, d1, init, op0, op1):
        nc = self.nc
        ia = self._a(init)
        self.add("dve", lambda: nc.vector.tensor_tensor_scan(out.ap, d0.ap, d1.ap, ia, op0, op1),
                 r=self._k(d0, d1, init), w=[out.key])

    def max8(self, out, in_):
        nc = self.nc
        self.add("dve", lambda: nc.vector.max(out.ap, in_.ap), r=self._k(in_), w=[out.key])

    def match_replace(self, out, rep, vals, imm):
        nc = self.nc
        self.add("dve", lambda: nc.vector.match_replace(out.ap, rep.ap, vals.ap, imm), r=self._k(rep, vals),
                 w=[out.key])

    def dma(self, out, in_, eng="sp", **kw):
        e = self.eng[eng]
        self.add(eng, lambda: e.dma_start(out=out.ap, in_=in_.ap, **kw), r=self._k(in_), w=[out.key], dma=True)

    def idma(self, out, in_, idx_v, axis=0):
        nc = self.nc
        self.add("pool", lambda: nc.gpsimd.indirect_dma_start(
            out=out.ap, out_offset=None, in_=in_.ap,
            in_offset=bass.IndirectOffsetOnAxis(ap=idx_v.ap, axis=axis)),
            r=self._k(in_, idx_v), w=[out.key], dma=True)

    def emit(self):
        nc, es, ops = self.nc, self.es, self.ops

        nosame = self.nosame

        def skip(d, o):
            if d["dma"] or o["dma"] or d["eng"] != o["eng"]:
                return False
            if d["eng"] == "pe":
                return d["rg"] == o["rg"]
            return nosame

        need = set()
        for o in ops:
            for d in o["deps"]:
                if not skip(ops[d], o):
                    need.add(d)
        csem = {}
        ccount = {e: 0 for e in self.eng}
        RING = {"sp": 24, "pool": 12, "act": 4}
        rings = {e: [] for e in RING}
        rcount = {e: 0 for e in RING}
        rtot = {}
        known = {e: {} for e in self.eng}
        tok = [None] * len(ops)
        for i, o in enumerate(ops):
            e = o["eng"]
            E = self.eng[e]
            kn = known[e]
            waits = {}
            for d in o["deps"]:
                if skip(ops[d], o):
                    continue
                s_, v = tok[d]
                if waits.get(id(s_), (None, 0))[1] < v:
                    waits[id(s_)] = (s_, v)
            pre = None
            if o["dma"]:
                j = rcount[e]
                rcount[e] += 1
                slot = j % RING[e]
                if slot >= len(rings[e]):
                    rings[e].append(es.enter_context(nc.semaphore(f"r_{e}_{slot}")))
                rs = rings[e][slot]
                prev = rtot.get(id(rs), 0)
                if prev > 0 and waits.get(id(rs), (None, 0))[1] < prev:
                    waits[id(rs)] = (rs, prev)
                rtot[id(rs)] = prev + 16
                pre = (rs, prev + 16)
            for key, (s_, v) in waits.items():
                if kn.get(key, 0) >= v:
                    continue
                E.wait_ge(s_, v)
                kn[key] = v
            ins = o["fn"]()
            if o["dma"]:
                ins.then_inc(pre[0], 16)
                tok[i] = pre
            elif i in need:
                c = ccount[e]
                ep, val = c // self.SEM_CAP, c % self.SEM_CAP + 1
                if (e, ep) not in csem:
                    csem[(e, ep)] = es.enter_context(nc.semaphore(f"c_{e}_{ep}"))
                ins.then_inc(csem[(e, ep)], 1)
                ccount[e] = c + 1
                tok[i] = (csem[(e, ep)], val)
        for e in rings:
            for rs in rings[e]:
                nc.sync.wait_ge(rs, rtot[id(rs)])


def _fix_waits():
    pass


def rope_tab(pos):
    inv = 500000.0 ** (-np.arange(8, dtype=np.float32) * 2.0 / 16.0)
    ang = pos.astype(np.float32)[:, None] * inv.astype(np.float32)
    return np.cos(ang).astype(np.float32), np.sin(ang).astype(np.float32)


def make_consts(Tn, past, ns=0):
    NT = Tn // 128
    c = {}
    c["ident"] = np.eye(128, dtype=np.float32)
    cs, sn = rope_tab(np.arange(Tn))
    c["ropeP"] = np.stack([cs.reshape(NT, 128, 8).transpose(1, 0, 2), sn.reshape(NT, 128, 8).transpose(1, 0, 2)], 1)
    slot = np.arange(NT * 8)
    cs, sn = rope_tab(16 * slot + 15)
    c["ropeC"] = np.stack([cs.reshape(NT, 8, 8).transpose(1, 0, 2), sn.reshape(NT, 8, 8).transpose(1, 0, 2)], 1)
    m = np.zeros((128, 17, 128), np.float32)
    p = np.arange(128)[:, None]
    qi = np.arange(128)[None, :]
    for r in range(16):
        sp = p - 8 * r
        m[:, r, :] = np.where(sp < 0, 1.0, np.where(sp <= 7, (qi >= 16 * sp + 15), 0.0))
    m[:, 16, :] = 1.0
    c["cmpmask"] = m
    nslot = NT * 8
    KT = (nslot + 127) // 128
    sm = np.zeros((KT * 128, 128), np.float32)
    for s in range(1, nslot):
        c0 = 16 * (s - 1)
        for j in range(128):
            if c0 < 64 * j + 64 and c0 + 32 > 64 * j:
                sm[s, j] = 1.0
    c["smap"] = sm.reshape(KT, 128, 128).transpose(1, 0, 2).copy()
    A = np.zeros((128, 255), np.float32)
    M = np.zeros((128, 255), np.float32)
    for q in range(128):
        hi = 1 if q >= 64 else 0
        for ci in range(255):
            cc = ci - 127
            valid = cc <= hi
            forced = (cc == hi) or (cc == hi - 1)
            if not valid:
                A[q, ci] = -1e6
            elif forced:
                A[q, ci] = 1e6
            else:
                M[q, ci] = 1.0
    c["selA"] = A
    c["selM"] = M
    ki = np.arange(128)[:, None]
    c["tri"] = np.stack([(ki <= qi).astype(np.float32), (ki > qi).astype(np.float32)], 1)
    bo = np.zeros((128, 128), np.float32)
    bo[:64, :64] = 1.0
    bo[64:, 64:] = 1.0
    c["blockones"] = bo
    s64 = np.arange(64)[:, None]
    t64 = np.arange(64)[None, :]
    c["tri64"] = np.concatenate([(s64 <= t64).astype(np.float32)] * 2, 0)
    if ns:
        NCB = past // 16 - 1
        NSB = past // 64 + 1
        CUR = past // 64
        cs, sn = rope_tab(np.full((ns,), past))
        c["ropeS"] = np.stack([cs, sn], 1)
        cs, sn = rope_tab(16 * np.arange(NCB) + 31)
        c["ropeCS"] = np.stack([cs, sn], 1)
        sm = np.zeros((NCB, NSB), np.float32)
        for i in range(NCB):
            for j in range(NSB):
                if 16 * i < 64 * j + 64 and 16 * i + 32 > 64 * j:
                    sm[i, j] = 1.0
        c["smapS"] = sm
        A = np.zeros((2 * ns, 40), np.float32)
        M = np.zeros((2 * ns, 40), np.float32)
        assert NSB <= 40
        for j in range(40):
            if j >= NSB:
                A[:, j] = -1e6
            elif j in (0, CUR, CUR - 1):
                A[:, j] = 1e6
            else:
                M[:, j] = 1.0
        c["selAS"] = A
        c["selMS"] = M
        Dm = np.zeros((40, past), np.float32)
        u = np.arange(past)
        Dm[u // 64, u] = 1.0
        c["DS"] = Dm
        c["pcol"] = np.arange(128, dtype=np.float32).reshape(128, 1)
    return c


CONST_BF = ("cmpmask", "smap", "tri", "blockones", "tri64", "smapS", "DS")


def build(cfg):
    Tn, L = cfg["T"], cfg["L"]
    NT = Tn // 128
    KTC = (NT * 8 + 127) // 128
    WIN_T = min(4, NT)
    nc = bass.Bass("TRN2", target_bir_lowering=False)
    es = contextlib.ExitStack()
    P = Prog(nc, es)
    P.nosame = bool(cfg.get("nosame"))
    if cfg.get("dry"):
        P.add = lambda *a, **k: None
    NS = cfg.get("NS", 0)
    PAST = cfg.get("past", 0)
    SKIP = cfg.get("skip", ())
    consts = make_consts(Tn, PAST, NS)

    def dr_in(name, shape, dt=F32):
        return T(name, nc.dram_tensor(name, list(shape), dt, kind="ExternalInput"))

    def dr_out(name, shape, dt=F32):
        return T(name, nc.dram_tensor(name, list(shape), dt, kind="ExternalOutput"))

    def sb(name, shape, dt=F32):
        return T("s_" + name, es.enter_context(nc.sbuf_tensor("s_" + name, list(shape), dt)))

    def ps(name, shape=(128, 512), dt=F32):
        return T(name, es.enter_context(nc.psum_tensor(name, list(shape), dt)))

    x_in = dr_in("x_prompt", [Tn, D])
    w_in_d = dr_in("w_in", [L, D, D_IN])
    w_out_d = dr_in("w_out", [L, D, D])
    w_up_d = dr_in("w_up", [L, D, 4 * D])
    w_dn_d = dr_in("w_down", [L, 4 * D, D])
    normT_d = dr_in("normT", [128, L, 2, 8])
    fnormT_d = dr_in("fnormT", [128, 8])
    w1_d = dr_in("cmp_w1", [L, 2, 32, 64, 256])
    posT_d = dr_in("cmp_posT", [128, L, 2, 16])
    b1T_d = dr_in("cmp_b1T", [128, L, 2, 2])
    w2_d = dr_in("cmp_w2", [L, 2, 256, 64])
    b2_d = dr_in("cmp_b2", [L, 2, 64])
    rgpT_d = dr_in("rg_pT", [128, L, 2, 9])
    rgwa_d = dr_in("rg_wa", [L, 4, 64, 64])
    rgwx_d = dr_in("rg_wx", [L, 4, 64, 64])
    hgpT_d = dr_in("hg_pT", [128, 2, 2, L])
    cdr = {k: dr_in("c_" + k, v.shape) for k, v in consts.items()}

    y_out = dr_out("y_prompt", [Tn, D])
    o_cmp = dr_out("p_cmp_kv", [L, Tn, 256])
    o_sel = dr_out("p_sel_kv", [L, Tn, 256])
    o_win = dr_out("p_win_kv", [L, WIN_T * 128, 256])
    o_rgh = dr_out("p_rg_h", [L, 256])
    o_rgc = dr_out("p_rg_conv", [L, 3, 256])
    o_hgs = dr_out("p_hg_s", [L, 4, 64, 64])
    xs_d = T("xscr", nc.dram_tensor("xscr", [D, Tn], F32, kind="Internal"))
    if NS:
        NPG = PAST // 128
        WB = min(512, PAST)
        NPOOL = cfg["npool"]
        xsam_d = dr_in("x_sample", [NS, D])
        pt_d = dr_in("page_table", [1, NS * NPG], I32)
        ccmp_d = dr_in("cache_cmp", [L * NPOOL * 128, 256])
        csel_d = dr_in("cache_sel", [L * NPOOL * 128, 256])
        cwin_d = dr_in("cache_win", [L, NS, WB, 256])
        srgh_d = dr_in("st_rgh", [L, NS, 256])
        srgc_d = dr_in("st_rgc", [L, NS, 3, 256])
        shgs_d = dr_in("st_hgs", [L, NS, 4, 64, 64])
        ys_out = dr_out("y_sample", [NS, D])
        os_cmp = dr_out("s_cmp_kv", [L, NS, 256])
        os_sel = dr_out("s_sel_kv", [L, NS, 256])
        os_win = dr_out("s_win_kv", [L, NS, WB, 256])
        os_rgh = dr_out("s_rg_h", [L, NS, 256])
        os_rgc = dr_out("s_rg_conv", [L, NS, 3, 256])
        os_hgs = dr_out("s_hg_s", [L, NS, 4, 64, 64])

    C = {}
    for k, v in consts.items():
        shp = list(v.shape)
        if k in ("ropeP", "ropeC"):
            continue
        if k in CONST_BF:
            C[k] = sb("k_" + k, shp, BF16)
            P.dma(C[k][:], cdr[k][:], eng="pool")
        else:
            C[k] = sb("k_" + k, shp, F32)
            P.dma(C[k][:], cdr[k][:])
    ident = C["ident"]
    ident_bf = sb("ident_bf", [128, 128], BF16)
    P.cp(ident_bf[:], ident[:], eng="pool")
    ones_bf = sb("ones_bf", [128, 128], BF16)
    P.memset(ones_bf[:], 1.0)
    ones_f = sb("ones_f", [128, 128], F32)
    P.memset(ones_f[:], 1.0)
    normT = sb("normT", [128, L, 2, 8])
    P.dma(normT[:], normT_d[:])
    fnormT = sb("fnormT", [128, 8])
    P.dma(fnormT[:], fnormT_d[:])
    rgp = sb("rgp", [128, L, 2, 9])
    P.dma(rgp[:], rgpT_d[:])
    hgp = sb("hgp", [128, 2, 2, L])
    P.dma(hgp[:], hgpT_d[:])
    lbe = sb("lbe", [128, 2, L])
    lb = sb("lb", [128, 2, L])
    lbs = sb("lbs", [128, 2, 1])
    oml = sb("oml", [128, 2, L])
    P.act(lbe[:], hgp[:, :, 0, :], AF.Exp)
    P.reduce(lbs[:, :, 0], lbe[:], ALU.add)
    P.recip(lbs[:], lbs[:])
    P.tt(lbe[:], lbe[:], lbs[:].b([128, 2, L]), ALU.mult)
    P.memset(lb[:, :, 0:1], 0.0)
    for l in range(1, L):
        P.tt(lb[:, :, l:l + 1], lb[:, :, l - 1:l], lbe[:, :, l:l + 1], ALU.add)
    P.ts(oml[:], lb[:], -1.0, ALU.mult, 1.0, ALU.add)
    rgc = sb("rgc", [128, L, 2, 1])
    P.act(rgc[:], rgp[:, :, :, 7:8], AF.Exp, scale=-1.0)
    P.act(rgc[:], rgc[:], AF.Ln, bias=1.0)
    P.ts(rgc[:], rgc[:], -8.0, ALU.mult)
    rgc2 = sb("rgc2", [128, L, 2, 1])
    P.ts(rgc2[:], rgc[:], 2.0, ALU.mult)

    pa = [ps("ps_a%d" % i) for i in range(3)]
    pS = [ps("ps_s%d" % i) for i in range(2)]
    pM = ps("ps_m")
    pO = ps("ps_o")
    pI = ps("ps_i")
    rot = [0]

    def nps():
        rot[0] = (rot[0] + 1) % 3
        return pa[rot[0]]

    xT = sb("xT", [128, 8, 128])
    yT = sb("yT", [128, 8, 128], BF16)
    sq = sb("sq", [128, 8, 128], BF16)
    rstd = sb("rstd", [128, 128])

    def rmsnorm_T(xv, gv, out_bf, ntok):
        sqv = sq[:, :, :ntok]
        P.act(sqv, xv, AF.Square)
        pz = nps()
        for c in range(8):
            P.mm(pz[:, :ntok], ones_bf[:], sq[:, c, :ntok], start=(c == 0), stop=(c == 7))
        P.act(rstd[:, :ntok], pz[:, :ntok], AF.Sqrt, bias=EPS, scale=1.0 / D)
        P.recip(rstd[:, :ntok], rstd[:, :ntok])
        tmp = sb_tmpn[:, :, :ntok]
        P.tt(tmp, xv, rstd[:, :ntok].r("p (o t) -> p o t", o=1).b([128, 8, ntok]), ALU.mult)
        P.tt(out_bf, tmp, gv.r("p (c o) -> p c o", o=1).b([128, 8, ntok]), ALU.mult)

    sb_tmpn = sb("tmpn", [128, 8, 128])
    if NS:
        xsT = sb("xsT", [128, 8, NS])
        with nc.sbuf_tensor("s_xstok", [NS, D], F32) as _xt:
            xstok = T("s_xstok", _xt)
            P.dma(xstok[:], xsam_d[:])
            for c in range(8):
                pz = nps()
                P.tr(pz[:, 0:NS], xstok[:, c * 128:(c + 1) * 128], ident[0:NS, 0:NS])
                P.cp(xsT[:, c, :], pz[:, 0:NS], eng="act")
        P.barrier()
        ptb = sb("ptb", [128, NS * NPG], I32)
        idxt = sb("idxt", [128, NS * NPG], I32)
        P.dma(ptb[:], pt_d[:].b([128, NS * NPG]))
        P.ts(idxt[:], ptb[:], 128.0, ALU.mult, C["pcol"][:, 0:1], ALU.add)

    def chk(k):
        if cfg.get("stop") == k:
            raise StopBuild()

    try:
        _layers(locals())
    except StopBuild:
        pass
    P.emit()
    return nc, consts


def _layers(env):
    globals().update({k: v for k, v in env.items() if not k.startswith("__")})
    chk(1)
    for l in range(L):
        with contextlib.ExitStack() as les:
            cur = [les]

            def lsb(name, shape, dt=F32):
                return T("s_" + name + "_%d" % l, cur[0].enter_context(nc.sbuf_tensor("s_" + name + "_%d" % l, list(shape), dt)))

            P.barrier()
            Wout = lsb("Wout", [128, 8, D], BF16)
            for c in range(8):
                P.dma(Wout[:, c, :], w_out_d[l, c * 128:(c + 1) * 128, :], eng="pool", max_dma_last_dim=4096)
            W1 = lsb("W1", [128, 2, 16, 256], BF16)
            for kv in range(2):
                for h2 in range(2):
                    P.dma(W1[h2 * 64:(h2 + 1) * 64, kv, :, :], w1_d[l, kv].r("(s t) d h -> t d s h", t=2)[h2], eng="pool",
                          max_dma_last_dim=1024)
            W2 = lsb("W2", [128, 2, 2, 64], BF16)
            P.dma(W2[:].r("p k c d -> p (k c) d"), w2_d[l].r("k (c p) d -> p (k c) d", p=128), eng="pool")
            b2r = lsb("b2r", [1, 2, 64], BF16)
            P.dma(b2r[:], b2_d[l:l + 1, :, :], eng="pool")
            posT = lsb("posT", [128, 2, 16], BF16)
            P.dma(posT[:], posT_d[:, l, :, :], eng="pool")
            b1T = lsb("b1T", [128, 2, 2])
            P.dma(b1T[:], b1T_d[:, l, :, :])
            cb1 = lsb("cb1", [128, 2, 2])
            for kv in range(2):
                for hc in range(2):
                    pz = nps()
                    for s in range(16):
                        P.mm(pz[:, 0:1], W1[:, kv, s, hc * 128:(hc + 1) * 128], posT[:, kv, s:s + 1],
                             start=(s == 0), stop=(s == 15))
                    P.tt(cb1[:, kv, hc:hc + 1], pz[:, 0:1], b1T[:, kv, hc:hc + 1], ALU.add)
            BDf = lsb("BDf", [128, 2, 2, 128])
            BD = lsb("BD", [128, 2, 2, 128], BF16)
            P.memset(BDf[:], 0.0)
            for c2 in range(2):
                for hh in range(2):
                    blk = c2 * 2 + hh
                    P.dma(BDf[hh * 64:(hh + 1) * 64, c2, 0, hh * 64:(hh + 1) * 64], rgwa_d[l, blk])
                    P.dma(BDf[hh * 64:(hh + 1) * 64, c2, 1, hh * 64:(hh + 1) * 64], rgwx_d[l, blk])
            P.cp(BD[:], BDf[:], eng="pool")

            chk(2)
            pes = contextlib.ExitStack()
            cur[0] = pes
            Win = lsb("Win", [128, 8, D_IN], BF16)
            for c in range(8):
                P.dma(Win[:, c, :], w_in_d[l, c * 128:(c + 1) * 128, :], eng="pool", max_dma_last_dim=4096)
            KsT = lsb("KsT", [128, Tn], BF16)
            KwT = lsb("KwT", [128, 8 * 128], BF16)
            Vs = lsb("Vs", [128, NT, 2, 65], BF16)
            Vw = lsb("Vw", [128, 8, 2, 65], BF16)
            KcT = lsb("KcT", [128, KTC * 128], BF16)
            Vc = lsb("Vc", [128, KTC, 2, 65], BF16)
            P.memset(KcT[:], 0.0)
            P.memset(Vc[:], 0.0)
            P.memset(Vs[:, :, :, 64:65], 1.0)
            P.memset(Vw[:, :, :, 64:65], 1.0)
            rawT = lsb("rawT", [128, 2, 2, 144], BF16)
            rkv = lsb("rkv", [128, 256], BF16)
            P.memset(rawT[:], 0.0)
            xcat = lsb("xcat", [128, 2, 131])
            P.memset(xcat[:], 0.0)
            hst = lsb("hst", [128, 2, 1])
            P.memset(hst[:], 0.0)
            S32 = lsb("S32", [128, 2, 64])
            Sbf = lsb("Sbf", [128, 2, 64], BF16)
            P.memset(S32[:], 0.0)
            P.memset(Sbf[:], 0.0)

            Ptok = lsb("Ptok", [128, 1560])
            Rtok = lsb("Rtok", [128, 1280])
            rpt = lsb("rpt", [128, 2, 8])
            rct = lsb("rct", [8, 2, 8])
            ra = lsb("ra", [128, 8, 8])
            rb = lsb("rb", [128, 8, 8])
            QT = lsb("QT", [128, 512], BF16)
            Ffm = lsb("Ffm", [128, 10, 128])
            vtok = lsb("vtok", [128, 256], BF16)
            ET = lsb("ET", [128, 512], BF16)
            PT = lsb("PT", [128, 512], BF16)
            ET2 = lsb("ET2", [128, 512], BF16)
            PT2 = lsb("PT2", [128, 512], BF16)
            OTs = lsb("OTs", [65, 512])
            Otok = lsb("Otok", [128, 3, 8, 65])
            impT = lsb("impT", [128, 2, 128])
            zr = lsb("zr", [128, 512])
            score = lsb("score", [128, 128])
            sc2 = lsb("sc2", [128, 128])
            mx8 = lsb("mx8", [128, 8])
            thr = lsb("thr", [128, 1])
            selm = lsb("selm", [128, 2, 128])
            Xb = [lsb("Xb%d" % i, [128, 128]) for i in range(2)]
            gates = lsb("gates", [128, 24])
            coef = lsb("coef", [128, 3, 8])
            onsa = lsb("onsa", [128, 512])
            mixT = lsb("mixT", [128, 8, 128], BF16)
            hpre = lsb("hpre", [128, 16])
            hu = lsb("hu", [128, 16])
            hgl = lsb("hgl", [128, 2, 2, 16], BF16)
            cblk = lsb("cblk", [8, 2, 2, 64])
            cblkr = lsb("cblkr", [8, 2, 64])
            cv = lsb("cv", [8, 2, 65], BF16)
            P.memset(cv[:, :, 64:65], 1.0)
            xc = lsb("xc", [128, 2, 128])
            xcb = lsb("xcb", [128, 2, 128], BF16)
            rg_r = lsb("rg_r", [128, 128])
            rg_i = lsb("rg_i", [128, 128])
            rg_a = lsb("rg_a", [128, 128])
            rg_b = lsb("rg_b", [128, 128])
            rg_h = lsb("rg_h", [128, 2, 128])
            gl = lsb("gl", [128, 128])
            hg_f = lsb("hg_f", [128, 128])
            hg_g = lsb("hg_g", [128, 128])
            hg_k = lsb("hg_k", [128, 128])
            bc = lsb("bc", [128, 128])
            nbr = lsb("nbr", [128, 2, 3])
            ebl = lsb("ebl", [128, 2])
            qs = lsb("qs", [128, 128])
            ex = lsb("ex", [128, 128])
            qE = lsb("qE", [128, 2, 128], BF16)
            qe = lsb("qe", [128, 2, 128], BF16)
            ke = lsb("ke", [128, 2, 128], BF16)
            kdT = lsb("kdT", [128, 2, 128])
            kdtok = lsb("kdtok", [128, 2, 128], BF16)
            Am = lsb("Am", [128, 2, 64], BF16)
            oT = lsb("oT", [128, 2, 128])
            osq = lsb("osq", [128, 128], BF16)
            orstd = lsb("orstd", [128, 128])

            for n in range(NT):
                t0 = n * 128
                if l == 0:
                    xtok = Ptok
                    P.dma(Ptok[:, 0:1024], x_in[t0:t0 + 128, :])
                    for c in range(8):
                        pz = nps()
                        P.tr(pz[:, 0:128], Ptok[:, c * 128:(c + 1) * 128], ident[:])
                        P.cp(xT[:, c, :], pz[:, 0:128], eng="act")
                else:
                    P.dma(xT[:], xs_d[:, t0:t0 + 128].r("(c p) t -> p c t", p=128))
                rmsnorm_T(xT[:], normT[:, l, 0, :], yT[:], 128)
                tm_chunks = [(0, 512), (512, 512), (1024, 280), (IN_OFF["hi"], 256)]
                dst = 0
                for (c0, cw) in tm_chunks:
                    pz = nps()
                    for c in range(8):
                        P.mm(pz[:, :cw], yT[:, c, :], Win[:, c, c0:c0 + cw], start=(c == 0), stop=(c == 7))
                    P.cp(Ptok[:, dst:dst + cw], pz[:, :cw], eng="act")
                    dst += cw
                fm_cols = [IN_OFF["rgx"], IN_OFF["rgx"] + 128, IN_OFF["rgg"], IN_OFF["rgg"] + 128,
                           IN_OFF["hq"], IN_OFF["hq"] + 128, IN_OFF["hf"], IN_OFF["hf"] + 128,
                           IN_OFF["hg"], IN_OFF["hg"] + 128]
                for i, c0 in enumerate(fm_cols):
                    pz = nps()
                    for c in range(8):
                        P.mm(pz[:, :128], Win[:, c, c0:c0 + 128], yT[:, c, :], start=(c == 0), stop=(c == 7))
                    P.cp(Ffm[:, i, :], pz[:, :128], eng=("act" if i % 2 else "dve"))
                chk(3)
                P.cp(Rtok[:], Ptok[:, 0:1280], eng="pool")
                P.dma(rpt[:], cdr["ropeP"][:, :, n, :])
                P.dma(rct[:], cdr["ropeC"][:, :, n, :])
                cosv = rpt[:, 0, :]
                sinv = rpt[:, 1, :]
                for (h0, nh) in ((0, 8), (12, 2), (16, 2)):
                    src = Ptok[:, h0 * 64:(h0 + nh) * 64].r("p (h d) -> p h d", d=64)
                    dstv = Rtok[:, h0 * 64:(h0 + nh) * 64].r("p (h d) -> p h d", d=64)
                    cb = cosv.r("p (o e) -> p o e", o=1).b([128, nh, 8])
                    sbv = sinv.r("p (o e) -> p o e", o=1).b([128, nh, 8])
                    x1, x2 = src[:, :, 0:8], src[:, :, 8:16]
                    P.tt(ra[:, :nh, :], x1, cb, ALU.mult)
                    P.tt(rb[:, :nh, :], x2, sbv, ALU.mult)
                    P.tt(dstv[:, :, 0:8], ra[:, :nh, :], rb[:, :nh, :], ALU.subtract)
                    P.tt(ra[:, :nh, :], x2, cb, ALU.mult)
                    P.tt(rb[:, :nh, :], x1, sbv, ALU.mult)
                    P.tt(dstv[:, :, 8:16], ra[:, :nh, :], rb[:, :nh, :], ALU.add)
                chk(31)
                P.dma(o_cmp[l, t0:t0 + 128, :], Rtok[:, 512:768])
                P.dma(o_sel[l, t0:t0 + 128, :], Rtok[:, 768:1024])
                if n >= NT - WIN_T:
                    w0 = (n - (NT - WIN_T)) * 128
                    P.dma(o_win[l, w0:w0 + 128, :], Rtok[:, 1024:1280])
                chk(32)
                pz = nps()
                for j in range(4):
                    P.tr(pz[:, j * 128:(j + 1) * 128], Rtok[:, j * 128:(j + 1) * 128], ident[:])
                chk(321)
                P.cp(QT[:], pz[:], eng="act")
                chk(322)
                P.cp(rkv[:], Rtok[:, 512:768], eng="pool")
                pz = nps()
                for kv in range(2):
                    for g in range(2):
                        cs_ = rkv[:, kv * 128 + g * 64:kv * 128 + (g + 1) * 64]
                        for h2 in range(2):
                            P.mm(pz[h2 * 64:(h2 + 1) * 64, (kv * 2 + g) * 128:(kv * 2 + g + 1) * 128], cs_, ident_bf[:])
                P.cp(rawT[0:64, :, :, 16:144], pz[0:64, :].r("p (k g t) -> p k g t", k=2, g=2), eng="dve")
                P.cp(rawT[64:128, :, :, 15:143], pz[64:128, :].r("p (k g t) -> p k g t", k=2, g=2), eng="dve")
                pz = nps()
                P.tr(pz[:, 256:384], Rtok[:, 768:896], ident[:])
                P.tr(pz[:, 384:512], Rtok[:, 1024:1152], ident[:])
                P.cp(KsT[:, t0:t0 + 128], pz[:, 256:384], eng="dve")
                P.cp(KwT[:, (n % 8) * 128:(n % 8 + 1) * 128], pz[:, 384:512], eng="dve")
                chk(33)
                P.cp(Vs[:, n, :, 0:64], Rtok[:, 896:1024].r("p (g d) -> p g d", g=2), eng="pool")
                P.cp(Vw[:, n % 8, :, 0:64], Rtok[:, 1152:1280].r("p (g d) -> p g d", g=2), eng="pool")
                P.cp(vtok[:], Ptok[:, 1304:1560], eng="pool")
                chk(4)
                for kv in range(0 if "cpr" in SKIP else 2):
                    for hc in range(2):
                        pz = nps()
                        for g in range(2):
                            for s in range(16):
                                rhs = rawT[:, kv, g, 2 * s:2 * s + 113:16]
                                P.mm(pz[:, g * 8:(g + 1) * 8], W1[:, kv, s, hc * 128:(hc + 1) * 128],
                                     rhs, start=(s == 0), stop=(s == 15))
                        P.ts(hpre[:], pz[:, 0:16], cb1[:, kv, hc:hc + 1], ALU.add)
                        P.tt(hu[:], hpre[:], hpre[:], ALU.mult)
                        P.ts(hu[:], hu[:], 0.044715, ALU.mult, 1.0, ALU.add)
                        P.tt(hu[:], hu[:], hpre[:], ALU.mult)
                        P.act(hu[:], hu[:], AF.Sigmoid, scale=1.5957691216)
                        P.tt(hgl[:, kv, hc, :], hu[:], hpre[:], ALU.mult)
                chk(41)
                pz = nps()
                for kv in range(2):
                    for g in range(2):
                        o_ = pz[0:8, (kv * 2 + g) * 64:(kv * 2 + g + 1) * 64]
                        for hc in range(2):
                            P.mm(o_, hgl[:, kv, hc, g * 8:(g + 1) * 8], W2[:, kv, hc, :], start=(hc == 0), stop=False)
                        P.mm(o_, ones_bf[0:1, 0:8], b2r[0:1, kv, :], start=False, stop=True)
                P.cp(cblk[:].r("b k g d -> b (k g d)"), pz[0:8, 0:256], eng="act")
                chk(42)
                cC = rct[:, 0, :].r("p (o e) -> p o e", o=1).b([8, 2, 8])
                sC = rct[:, 1, :].r("p (o e) -> p o e", o=1).b([8, 2, 8])
                P.cp(cblkr[:], cblk[:, 0, :, :], eng="pool")
                x1, x2 = cblk[:, 0, :, 0:8], cblk[:, 0, :, 8:16]
                P.tt(ra[0:8, 0:2, :], x1, cC, ALU.mult)
                P.tt(rb[0:8, 0:2, :], x2, sC, ALU.mult)
                P.tt(cblkr[:, :, 0:8], ra[0:8, 0:2, :], rb[0:8, 0:2, :], ALU.subtract)
                P.tt(ra[0:8, 0:2, :], x2, cC, ALU.mult)
                P.tt(rb[0:8, 0:2, :], x1, sC, ALU.mult)
                P.tt(cblkr[:, :, 8:16], ra[0:8, 0:2, :], rb[0:8, 0:2, :], ALU.add)
                pz = nps()
                P.tr(pz[:, 0:8], cblkr[:].r("b g d -> b (g d)"), ident[0:8, 0:8])
                P.cp(KcT[:, n * 8:(n + 1) * 8], pz[:, 0:8], eng="act")
                chk(43)
                P.cp(cv[:, :, 0:64], cblk[:, 1, :, :], eng="pool")
                kt_n, po = (n * 8) // 128, (n * 8) % 128
                if n == 0:
                    P.dma(Vc[1:8, 0, :, :], cv[1:8, :, :])
                else:
                    P.dma(Vc[po:po + 8, kt_n, :, :], cv[:, :, :])
                chk(44)
                P.cp(rawT[0:64, :, :, 0:16], rawT[0:64, :, :, 128:144], eng="pool")
                P.cp(rawT[64:128, :, :, 0:15], rawT[64:128, :, :, 128:143], eng="pool")

                chk(5)
                def finish(br, g):
                    P.cp(OTs[:], pO[0:65, :], eng="act")
                    pz_ = nps()
                    for j in range(4):
                        P.tr(pz_[:, j * 65:(j + 1) * 65], OTs[:, j * 128:(j + 1) * 128], ident[0:65, 0:65])
                    P.cp(Otok[:, br, g * 4:(g + 1) * 4, :], pz_[:, 0:260].r("p (j e) -> p j e", e=65),
                         eng=("act" if g else "dve"))

                ETb = [ET, ET2]
                PTb = [PT, PT2]
                pMb = [pM, pI]

                def attend(br, g, items):
                    nI = len(items)

                    def s1(i):
                        it = items[i]
                        P.mm(pS[i % 2][:], it["kT"], QT[g * 64:(g + 1) * 64, :], start=True, stop=True)
                        if it.get("selkt") is not None:
                            kt_ = it["selkt"]
                            xb_ = Xb[i % 2]
                            P.cp(xb_[:].r("p (b o) -> p b o", o=64),
                                 selm[:, g, 2 * kt_:2 * kt_ + 2].r("p (b o) -> p b o", o=1).b([128, 2, 64]), eng="pool")
                            P.tr(pMb[i % 2][:, 0:128], xb_[:], ident[:])

                    def s2(i):
                        it = items[i]
                        et, pt = ETb[i % 2], PTb[i % 2]
                        P.act(et[:], pS[i % 2][:], AF.Exp, scale=0.125)
                        src = et
                        mk = it.get("mk")
                        if it.get("selkt") is not None:
                            mk = pMb[i % 2][:, 0:128]
                            if it.get("causal"):
                                P.tt(sc2[:], mk, C["tri"][:, 0, :], ALU.mult)
                                mk = sc2[:]
                        if mk is not None:
                            P.tt(pt[:].r("p (j q) -> p j q", j=4), et[:].r("p (j q) -> p j q", j=4),
                                 mk.r("p (o q) -> p o q", o=1).b([128, 4, 128]), ALU.mult)
                            src = pt
                        if it.get("z0"):
                            P.memset(src[0:1, :], 0.0, eng="dve")
                        P.mm(pO[0:65, :], it["va"], src[:], start=(i == 0), stop=(i == nI - 1))
                        if it.get("extra") is not None:
                            it["extra"](src, i == 0, i == nI - 1)

                    s1(0)
                    for i in range(nI):
                        if i + 1 < nI:
                            s1(i + 1)
                        s2(i)
                    finish(br, g)

                nkt = n // 16 + 1
                for g in range(0 if "cmp" in SKIP else 2):
                    lst = []
                    for kt in range(nkt):
                        mk = C["cmpmask"][:, n % 16, :] if kt == nkt - 1 else None

                        def extra(src, first, last_, kt=kt):
                            P.mm(pI[:], C["smap"][:, kt, :], src[:], start=first, stop=last_)
                            P.mm(pM[:], ones_bf[:], src[:], start=first, stop=last_)
                        lst.append(dict(kT=KcT[g * 64:(g + 1) * 64, kt * 128:(kt + 1) * 128], va=Vc[:, kt, g, :], mk=mk,
                                        z0=(kt == 0), extra=extra))
                    attend(0, g, lst)
                    P.ts(zr[:], pM[:], 1e-30, ALU.max)
                    P.recip(zr[:], zr[:])
                    P.tt(zr[:], pI[:], zr[:], ALU.mult)
                    P.reduce(impT[:, g, :], zr[:].r("p (j q) -> p q j", j=4), ALU.add)
                for g in range(2):
                    pz = nps()
                    P.tr(pz[:, 0:128], impT[:, g, :], ident[:])
                    P.tt(score[:], pz[:, 0:128], C["selM"][:, 127 - 2 * n:255 - 2 * n], ALU.mult)
                    P.tt(score[:], score[:], C["selA"][:, 127 - 2 * n:255 - 2 * n], ALU.add)
                    P.memset(score[:, 0:1], 1e6, eng="dve")
                    P.max8(mx8[:], score[:])
                    P.match_replace(sc2[:], mx8[:], score[:], -1e30)
                    P.max8(mx8[:], sc2[:])
                    P.ts(thr[:], mx8[:, 7:8], -1e5, ALU.max)
                    P.ts(selm[:, g, :], score[:], thr[:, 0:1], ALU.is_ge)
                chk(6)
                for g in range(0 if "sel" in SKIP else 2):
                    lst = []
                    for kt in range(n + 1):
                        lst.append(dict(kT=KsT[g * 64:(g + 1) * 64, kt * 128:(kt + 1) * 128], va=Vs[:, kt, g, :], selkt=kt,
                                        causal=(kt == n)))
                    attend(1, g, lst)
                chk(7)
                for g in range(0 if "win" in SKIP else 2):
                    lst = []
                    for kt in range(max(0, n - 4), n + 1):
                        if kt == n:
                            mk = C["tri"][:, 0, :]
                        elif kt == n - 4:
                            mk = C["tri"][:, 1, :]
                        else:
                            mk = None
                        lst.append(dict(kT=KwT[g * 64:(g + 1) * 64, (kt % 8) * 128:(kt % 8 + 1) * 128], va=Vw[:, kt % 8, g, :], mk=mk))
                    attend(2, g, lst)
                P.act(gates[:], Ptok[:, 1280:1304], AF.Sigmoid)
                P.ts(coef[:], Otok[:, :, :, 64], 1e-30, ALU.max)
                P.recip(coef[:], coef[:])
                P.tt(coef[:], coef[:], gates[:].r("p (h b) -> p b h", b=3), ALU.mult)
                for h in range(8):
                    ov = onsa[:, h * 64:(h + 1) * 64]
                    P.ts(ov, Otok[:, 0, h, 0:64], coef[:, 0, h:h + 1], ALU.mult)
                    P.stt(ov, Otok[:, 1, h, 0:64], coef[:, 1, h:h + 1], ov, ALU.mult, ALU.add)
                    P.stt(ov, Otok[:, 2, h, 0:64], coef[:, 2, h:h + 1], ov, ALU.mult, ALU.add)
                pz = nps()
                for j in range(4):
                    P.tr(pz[:, j * 128:(j + 1) * 128], onsa[:, j * 128:(j + 1) * 128], ident[:])
                P.cp(mixT[:, 0:4, :], pz[:].r("p (c t) -> p c t", c=4), eng="act")

                chk(8)
                for c2 in range(0 if "rg" in SKIP else 2):
                    pr = rgp[:, l, c2, :]
                    P.cp(xcat[:, c2, 3:131], Ffm[:, c2, :], eng="pool")
                    P.ts(xc[:, c2, :], xcat[:, c2, 0:128], pr[:, 0:1], ALU.mult, pr[:, 4:5], ALU.add)
                    for k in range(1, 4):
                        P.stt(xc[:, c2, :], xcat[:, c2, k:k + 128], pr[:, k:k + 1], xc[:, c2, :], ALU.mult, ALU.add)
                    P.cp(xcb[:, c2, :], xc[:, c2, :], eng="pool")
                    pz = nps()
                    P.mm(pz[:, 0:128], BD[:, c2, 0, :], xcb[:, c2, :])
                    P.act(rg_r[:], pz[:, 0:128], AF.Sigmoid, bias=pr[:, 5:6])
                    pz = nps()
                    P.mm(pz[:, 0:128], BD[:, c2, 1, :], xcb[:, c2, :])
                    P.act(rg_i[:], pz[:, 0:128], AF.Sigmoid, bias=pr[:, 6:7])
                    P.act(rg_a[:], rg_r[:], AF.Exp, scale=rgc[:, l, c2, :])
                    P.act(rg_b[:], rg_r[:], AF.Exp, scale=rgc2[:, l, c2, :])
                    P.ts(rg_b[:], rg_b[:], -1.0, ALU.mult, 1.0, ALU.add)
                    P.ts(rg_b[:], rg_b[:], 0.0, ALU.max)
                    P.act(rg_b[:], rg_b[:], AF.Sqrt)
                    P.tt(rg_i[:], rg_i[:], xc[:, c2, :], ALU.mult)
                    P.tt(rg_b[:], rg_b[:], rg_i[:], ALU.mult)
                    P.scan(rg_h[:, c2, :], rg_a[:], rg_b[:], hst[:, c2, :], ALU.mult, ALU.add)
                    P.cp(hst[:, c2, :], rg_h[:, c2, 127:128], eng="pool")
                    gx = Ffm[:, 2 + c2, :]
                    P.tt(gl[:], gx, gx, ALU.mult)
                    P.ts(gl[:], gl[:], 0.044715, ALU.mult, 1.0, ALU.add)
                    P.tt(gl[:], gl[:], gx, ALU.mult)
                    P.act(gl[:], gl[:], AF.Sigmoid, scale=1.5957691216)
                    P.tt(gl[:], gl[:], gx, ALU.mult)
                    P.tt(mixT[:, 4 + c2, :], gl[:], rg_h[:, c2, :], ALU.mult)
                    P.cp(xcat[:, c2, 0:3], xcat[:, c2, 128:131], eng="pool")
                if n == NT - 1:
                    for c2 in range(2):
                        P.dma(o_rgh[l, c2 * 128:(c2 + 1) * 128].r("(p o) -> p o", o=1), hst[:, c2, :])
                    pz = nps()
                    for c in range(8):
                        P.mm(pz[:, 0:256], yT[:, c, :], Win[:, c, IN_OFF["rgx"]:IN_OFF["rgx"] + 256], start=(c == 0),
                             stop=(c == 7))
                    P.cp(score[:, 0:128], pz[:, 0:128], eng="act")
                    P.cp(sc2[:, 0:128], pz[:, 128:256], eng="act")
                    P.dma(o_rgc[l, :, 0:128], score[125:128, 0:128])
                    P.dma(o_rgc[l, :, 128:256], sc2[125:128, 0:128])

                chk(9)
                for c2 in range(0 if "hg" in SKIP else 2):
                    hqv = Ffm[:, 4 + c2, :]
                    hfv = Ffm[:, 6 + c2, :]
                    P.act(hg_f[:], hfv, AF.Sigmoid)
                    P.ts(hg_f[:], hg_f[:], oml[:, c2, l:l + 1], ALU.mult, lb[:, c2, l:l + 1], ALU.add)
                    P.act(hg_g[:], hg_f[:], AF.Ln)
                    P.ts(hg_k[:], hg_f[:], -1.0, ALU.mult, 1.0, ALU.add)
                    P.act(qs[:], hqv, AF.Sigmoid)
                    P.tt(qs[:], qs[:], hqv, ALU.mult)
                    for ch in range(2):
                        sl = slice(ch * 64, (ch + 1) * 64)
                        P.scan(bc[:, sl], ones_f[:, 0:64], hg_g[:, sl], 0.0, ALU.mult, ALU.add)
                        P.cp(nbr[:, ch, 1:2], bc[:, ch * 64 + 31:ch * 64 + 32], eng="pool")
                        P.cp(nbr[:, ch, 2:3], bc[:, ch * 64 + 63:ch * 64 + 64], eng="pool")
                        P.ts(nbr[:, ch, 0:1], nbr[:, ch, 1:2], -1.0, ALU.mult)
                        P.act(ebl[:, ch:ch + 1], nbr[:, ch, 2:3], AF.Exp)
                        P.act(ex[:, sl], bc[:, sl], AF.Exp)
                        P.tt(qE[:, c2, sl], qs[:, sl], ex[:, sl], ALU.mult)
                        P.act(ex[:, sl], bc[:, sl], AF.Exp, bias=nbr[:, ch, 0:1])
                        P.tt(qe[:, c2, sl], qs[:, sl], ex[:, sl], ALU.mult)
                        P.act(ex[:, sl], bc[:, sl], AF.Exp, scale=-1.0, bias=nbr[:, ch, 1:2])
                        P.tt(ke[:, c2, sl], hg_k[:, sl], ex[:, sl], ALU.mult)
                        P.act(ex[:, sl], bc[:, sl], AF.Exp, scale=-1.0, bias=nbr[:, ch, 2:3])
                        P.tt(kdT[:, c2, sl], hg_k[:, sl], ex[:, sl], ALU.mult)
                    pz = nps()
                    P.tr(pz[:, 0:128], kdT[:, c2, :], ident[:])
                    P.cp(kdtok[:, c2, :], pz[:, 0:128], eng="act")
                    for ch in range(2):
                        sl = slice(ch * 64, (ch + 1) * 64)
                        pz = nps()
                        for hh in range(2):
                            hp = slice(hh * 64, (hh + 1) * 64)
                            P.mm(pz[sl, hh * 64:(hh + 1) * 64], ke[hp, c2, sl], qe[hp, c2, sl])
                        P.tt(Am[sl, :, :], pz[sl, 0:128].r("p (h t) -> p h t", h=2),
                             C["tri64"][sl, :].r("p (o t) -> p o t", o=1).b([64, 2, 64]), ALU.mult)
                        pz2 = nps()
                        for hh in range(2):
                            hp = slice(hh * 64, (hh + 1) * 64)
                            h = c2 * 2 + hh
                            P.mm(pz2[hp, 0:64], vtok[sl, h * 64:(h + 1) * 64], Am[sl, hh, :], start=True, stop=False)
                            P.mm(pz2[hp, 0:64], Sbf[hp, c2, :], qE[hp, c2, sl], start=False, stop=True)
                            P.mm(pz2[hp, 64:128], kdtok[sl, c2, hp], vtok[sl, h * 64:(h + 1) * 64])
                        P.cp(oT[:, c2, sl], pz2[:, 0:64], eng="act")
                        P.stt(S32[:, c2, :], S32[:, c2, :], ebl[:, ch:ch + 1], pz2[:, 64:128], ALU.mult, ALU.add)
                        P.cp(Sbf[:, c2, :], S32[:, c2, :], eng="pool")
                    P.act(osq[:], oT[:, c2, :], AF.Square)
                    pz = nps()
                    P.mm(pz[:, 0:128], C["blockones"][:], osq[:])
                    P.act(orstd[:], pz[:, 0:128], AF.Sqrt, bias=EPS, scale=1.0 / 64)
                    P.recip(orstd[:], orstd[:])
                    P.tt(orstd[:], orstd[:], oT[:, c2, :], ALU.mult)
                    hgv = Ffm[:, 8 + c2, :]
                    P.act(gl[:], hgv, AF.Sigmoid)
                    P.tt(gl[:], gl[:], hgv, ALU.mult)
                    P.stt(mixT[:, 6 + c2, :], orstd[:], hgp[:, c2, 1, l:l + 1], gl[:], ALU.mult, ALU.mult)
                if n == NT - 1:
                    for c2 in range(2):
                        P.dma(o_hgs[l, c2 * 2:c2 * 2 + 2].r("h k v -> (h k) v"), S32[:, c2, :])

                chk(10)
                for dc in range(8):
                    pz = nps()
                    for k in range(8):
                        P.mm(pz[:, 0:128], Wout[:, k, dc * 128:(dc + 1) * 128], mixT[:, k, :], start=(k == 0), stop=(k == 7))
                    P.tt(xT[:, dc, :], xT[:, dc, :], pz[:, 0:128], ALU.add)
                P.dma(xs_d[:, t0:t0 + 128].r("(c p) t -> p c t", p=128), xT[:])
                chk(11)
            pes.close()
            P.barrier()
            chk(12)
            if NS:
                des = contextlib.ExitStack()
                cur[0] = des
                decode_mixer(l, lsb, cur, Wout, W1, W2, b2r, cb1, BD)
                des.close()

        chk(13)
        with contextlib.ExitStack() as les:
            def lsb(name, shape, dt=F32):
                return T("s_" + name + "_f%d" % l, les.enter_context(nc.sbuf_tensor("s_" + name + "_f%d" % l, list(shape), dt)))
            P.barrier()
            Wup = lsb("Wup", [128, 8, 4 * D], BF16)
            Wdn = lsb("Wdn", [128, 32, D], BF16)
            for c in range(8):
                P.dma(Wup[:, c, :], w_up_d[l, c * 128:(c + 1) * 128, :], eng="pool", max_dma_last_dim=4096)
            for c in range(32):
                P.dma(Wdn[:, c, :], w_dn_d[l, c * 128:(c + 1) * 128, :], eng="pool", max_dma_last_dim=4096)
            FB = 256
            xB = lsb("xB", [128, 8, FB])
            yB = lsb("yB", [128, 8, FB], BF16)
            rsB = lsb("rsB", [128, FB])
            HT = lsb("HT", [128, 32, FB], BF16)
            tB = V(HT.t[:, 0:16, :].rearrange("p a b -> p (a b)").bitcast(F32).rearrange("p (c t) -> p c t", c=8), HT[:].key)
            sqB = V(HT.t[:, 16:24, :], HT[:].key)
            hr = lsb("hr", [128, FB])
            ytok = lsb("ytok", [128, D])
            for blk in range(Tn // FB):
                t0 = blk * FB
                P.dma(xB[:], xs_d[:, t0:t0 + FB].r("(c p) t -> p c t", p=128))

                def norm(gv, outv):
                    P.act(sqB, xB[:], AF.Square)
                    pz = nps()
                    for c in range(8):
                        P.mm(pz[:, :FB], ones_bf[:], sqB[:, c, :], start=(c == 0), stop=(c == 7))
                    P.act(rsB[:], pz[:, :FB], AF.Sqrt, bias=EPS, scale=1.0 / D)
                    P.recip(rsB[:], rsB[:])
                    P.tt(tB, xB[:], rsB[:].r("p (o t) -> p o t", o=1).b([128, 8, FB]), ALU.mult)
                    P.tt(outv, tB, gv.r("p (c o) -> p c o", o=1).b([128, 8, FB]), ALU.mult)
                norm(normT[:, l, 1, :], yB[:])
                for f in range(32):
                    pz = nps()
                    for c in range(8):
                        P.mm(pz[:, :FB], Wup[:, c, f * 128:(f + 1) * 128], yB[:, c, :], start=(c == 0), stop=(c == 7))
                    P.act(hr[:], pz[:, :FB], AF.Relu)
                    P.tt(HT[:, f, :], hr[:], hr[:], ALU.mult, eng="pool")
                for dc in range(8):
                    pz = nps()
                    for f in range(32):
                        P.mm(pz[:, :FB], Wdn[:, f, dc * 128:(dc + 1) * 128], HT[:, f, :], start=(f == 0), stop=(f == 31))
                    P.tt(xB[:, dc, :], xB[:, dc, :], pz[:, :FB], ALU.add)
                if l < L - 1:
                    P.dma(xs_d[:, t0:t0 + FB].r("(c p) t -> p c t", p=128), xB[:])
                else:
                    norm(fnormT[:], tB)
                    for tt_ in range(FB // 128):
                        for c in range(8):
                            pz = nps()
                            P.tr(pz[:, 0:128], tB[:, c, tt_ * 128:(tt_ + 1) * 128], ident[:])
                            P.cp(ytok[:, c * 128:(c + 1) * 128], pz[:, 0:128], eng=("act" if c % 2 else "dve"))
                        P.dma(y_out[t0 + tt_ * 128:t0 + (tt_ + 1) * 128, :], ytok[:])
            if NS:
                decode_ffn(l, lsb, Wup, Wdn)


FM_COLS = [IN_OFF["rgx"], IN_OFF["rgx"] + 128, IN_OFF["rgg"], IN_OFF["rgg"] + 128,
           IN_OFF["hq"], IN_OFF["hq"] + 128, IN_OFF["hf"], IN_OFF["hf"] + 128,
           IN_OFF["hg"], IN_OFF["hg"] + 128]


def rope_tok(src_t, dst_t, cosv, sinv, ra, rb, npart):
    for (h0, nh) in ((0, 8), (12, 2), (16, 2)):
        src = src_t[:, h0 * 64:(h0 + nh) * 64].r("p (h d) -> p h d", d=64)
        dstv = dst_t[:, h0 * 64:(h0 + nh) * 64].r("p (h d) -> p h d", d=64)
        cb = cosv.r("p (o e) -> p o e", o=1).b([npart, nh, 8])
        sbv = sinv.r("p (o e) -> p o e", o=1).b([npart, nh, 8])
        x1, x2 = src[:, :, 0:8], src[:, :, 8:16]
        P.tt(ra[:, :nh, :], x1, cb, ALU.mult)
        P.tt(rb[:, :nh, :], x2, sbv, ALU.mult)
        P.tt(dstv[:, :, 0:8], ra[:, :nh, :], rb[:, :nh, :], ALU.subtract)
        P.tt(ra[:, :nh, :], x2, cb, ALU.mult)
        P.tt(rb[:, :nh, :], x1, sbv, ALU.mult)
        P.tt(dstv[:, :, 8:16], ra[:, :nh, :], rb[:, :nh, :], ALU.add)


def gelu_tanh(out, x, tmp):
    P.tt(tmp, x, x, ALU.mult)
    P.ts(tmp, tmp, 0.044715, ALU.mult, 1.0, ALU.add)
    P.tt(tmp, tmp, x, ALU.mult)
    P.act(tmp, tmp, AF.Sigmoid, scale=1.5957691216)
    P.tt(out, tmp, x, ALU.mult)


def decode_mixer(l, lsb, cur, Wout, W1, W2, b2r, cb1, BD):
    NCB = PAST // 16 - 1
    NSB = PAST // 64 + 1
    WT = WB // 128
    ysT = lsb("ysT", [128, 8, NS], BF16)
    rmsnorm_T(xsT[:], normT[:, l, 0, :], ysT[:], NS)
    Ps = lsb("Ps", [NS, D_IN])
    Fs = lsb("Fs", [128, 10, NS])
    outer = cur[0]
    wes = contextlib.ExitStack()
    cur[0] = wes
    Win = lsb("WinD", [128, 8, D_IN], BF16)
    for c in range(8):
        P.dma(Win[:, c, :], w_in_d[l, c * 128:(c + 1) * 128, :], eng="pool", max_dma_last_dim=4096)
    for c0 in range(0, D_IN, 512):
        cw = min(512, D_IN - c0)
        pz = nps()
        for c in range(8):
            P.mm(pz[0:NS, :cw], ysT[:, c, :], Win[:, c, c0:c0 + cw], start=(c == 0), stop=(c == 7))
        P.cp(Ps[:, c0:c0 + cw], pz[0:NS, :cw], eng="act")
    for i, c0 in enumerate(FM_COLS):
        pz = nps()
        for c in range(8):
            P.mm(pz[:, :NS], Win[:, c, c0:c0 + 128], ysT[:, c, :], start=(c == 0), stop=(c == 7))
        P.cp(Fs[:, i, :], pz[:, :NS], eng="dve")
    wes.close()
    cur[0] = outer
    P.barrier()
    Rs = lsb("Rs", [NS, 1280])
    ras = lsb("ras", [NS, 8, 8])
    rbs = lsb("rbs", [NS, 8, 8])
    P.cp(Rs[:], Ps[:, 0:1280], eng="pool")
    rope_tok(Ps, Rs, C["ropeS"][:, 0, :], C["ropeS"][:, 1, :], ras, rbs, NS)
    P.dma(os_cmp[l], Rs[:, 512:768])
    P.dma(os_sel[l], Rs[:, 768:1024])
    P.dma(os_win.s("a")[l, :, 0:WB - 1, :], cwin_d[l, :, 1:WB, :])
    P.dma(os_win.s("b")[l, :, WB - 1, :], Rs[:, 1024:1280])
    QsT = lsb("QsT", [128, 4, NS], BF16)
    pz = nps()
    for j in range(4):
        P.tr(pz[:, j * NS:(j + 1) * NS], Rs[:, j * 128:(j + 1) * 128], ident[0:NS, 0:NS])
    P.cp(QsT[:].r("p j s -> p (j s)"), pz[:, 0:4 * NS], eng="act")

    idxl = lsb("idxl", [128, NS * NPG], I32)
    P.ts(idxl[:], idxt[:], float(l * NPOOL * 128), ALU.add)
    OTall = lsb("OTall", [65, 3, 2, 4, NS])
    impTall = lsb("impTall", [40, 2 * NS])
    P.memset(impTall[:], 0.0)
    pg = [lsb("pg0", [128, NPG, 256])] * 2
    rawS = lsb("rawS", [128, 2, 2, PAST], BF16)
    pgbf = lsb("pgbf", [128, 256], BF16)
    hp_ = lsb("hp_", [128, 2 * NCB])
    hu_ = lsb("hu_", [128, 2 * NCB])
    hglS = lsb("hglS", [128, 2, 2, 2 * NCB], BF16)
    cblkS = lsb("cblkS", [NCB, 2, 2, 64])
    cblkrS = lsb("cblkrS", [NCB, 2, 64])
    rcs = lsb("rcs", [NCB, 2, 8])
    rds = lsb("rds", [NCB, 2, 8])
    KcS = lsb("KcS", [128, NCB], BF16)
    cvS = lsb("cvS", [NCB, 2, 65], BF16)
    P.memset(cvS[:, :, 64:65], 1.0)
    ETs = lsb("ETs", [128, 4 * max(NPG, 4)], BF16)
    PTs = lsb("PTs", [128, 4 * max(NPG, 4)], BF16)
    zs = lsb("zs", [40, 4])
    zq = lsb("zq", [40, 4])
    for s in range(NS):
        pgb = pg[s % 2]
        for k in range(NPG):
            P.idma(pgb[:, k, :], ccmp_d[:], idxl[:, s * NPG + k:s * NPG + k + 1])
        for k in range(NPG):
            P.cp(pgbf[:], pgb[:, k, :], eng="pool")
            pz = nps()
            for kv in range(2):
                for g in range(2):
                    cs_ = pgbf[:, kv * 128 + g * 64:kv * 128 + (g + 1) * 64]
                    for h2 in range(2):
                        P.mm(pz[h2 * 64:(h2 + 1) * 64, (kv * 2 + g) * 128:(kv * 2 + g + 1) * 128], cs_, ident_bf[:])
            P.cp(rawS[0:64, :, :, k * 128:(k + 1) * 128], pz[0:64, :].r("p (k g t) -> p k g t", k=2, g=2), eng="act")
            if k == 0:
                P.cp(rawS[64:128, :, :, 0:127], pz[64:128, :].r("p (k g t) -> p k g t", k=2, g=2)[:, :, :, 1:128], eng="dve")
            else:
                P.cp(rawS[64:128, :, :, k * 128 - 1:(k + 1) * 128 - 1], pz[64:128, :].r("p (k g t) -> p k g t", k=2, g=2), eng="dve")
        for kv in range(2):
            for hc in range(2):
                pz = nps()
                for g in range(2):
                    for s16 in range(16):
                        rhs = rawS[:, kv, g, 2 * s16:2 * s16 + 16 * (NCB - 1) + 1:16]
                        P.mm(pz[:, g * NCB:(g + 1) * NCB], W1[:, kv, s16, hc * 128:(hc + 1) * 128],
                             rhs, start=(s16 == 0), stop=(s16 == 15))
                P.ts(hp_[:], pz[:, 0:2 * NCB], cb1[:, kv, hc:hc + 1], ALU.add)
                gelu_tanh(hglS[:, kv, hc, :], hp_[:], hu_[:])
        pz = nps()
        for kv in range(2):
            for g in range(2):
                o_ = pz[0:NCB, (kv * 2 + g) * 64:(kv * 2 + g + 1) * 64]
                for hc in range(2):
                    P.mm(o_, hglS[:, kv, hc, g * NCB:(g + 1) * NCB], W2[:, kv, hc, :], start=(hc == 0), stop=False)
                P.mm(o_, ones_bf[0:1, 0:NCB], b2r[0:1, kv, :], start=False, stop=True)
        P.cp(cblkS[:].r("b k g d -> b (k g d)"), pz[0:NCB, 0:256], eng="act")
        cC = C["ropeCS"][:, 0, :].r("p (o e) -> p o e", o=1).b([NCB, 2, 8])
        sC = C["ropeCS"][:, 1, :].r("p (o e) -> p o e", o=1).b([NCB, 2, 8])
        P.cp(cblkrS[:], cblkS[:, 0, :, :], eng="pool")
        x1, x2 = cblkS[:, 0, :, 0:8], cblkS[:, 0, :, 8:16]
        P.tt(rcs[:], x1, cC, ALU.mult)
        P.tt(rds[:], x2, sC, ALU.mult)
        P.tt(cblkrS[:, :, 0:8], rcs[:], rds[:], ALU.subtract)
        P.tt(rcs[:], x2, cC, ALU.mult)
        P.tt(rds[:], x1, sC, ALU.mult)
        P.tt(cblkrS[:, :, 8:16], rcs[:], rds[:], ALU.add)
        pz = nps()
        P.tr(pz[:, 0:NCB], cblkrS[:].r("b g d -> b (g d)"), ident[0:NCB, 0:NCB])
        P.cp(KcS[:], pz[:, 0:NCB], eng="act")
        P.cp(cvS[:, :, 0:64], cblkS[:, 1, :, :], eng="pool")
        for g in range(2):
            P.mm(pS[0][0:NCB, 0:4], KcS[g * 64:(g + 1) * 64, :], QsT[g * 64:(g + 1) * 64, :, s])
            P.act(ETs[0:NCB, 0:4], pS[0][0:NCB, 0:4], AF.Exp, scale=0.125)
            P.mm(pO[0:65, 0:4], cvS[:, g, :], ETs[0:NCB, 0:4])
            P.mm(pI[0:NSB, 0:4], C["smapS"][:, :], ETs[0:NCB, 0:4])
            P.mm(pM[0:NSB, 0:4], ones_bf[0:NCB, 0:NSB], ETs[0:NCB, 0:4])
            P.cp(OTall[:, 0, g, :, s], pO[0:65, 0:4], eng="act")
            P.ts(zs[0:NSB, :], pM[0:NSB, 0:4], 1e-30, ALU.max)
            P.recip(zs[0:NSB, :], zs[0:NSB, :])
            P.tt(zq[0:NSB, :], pI[0:NSB, 0:4], zs[0:NSB, :], ALU.mult)
            P.reduce(impTall[0:NSB, 2 * s + g:2 * s + g + 1], zq[0:NSB, :], ALU.add)
    scoreS = lsb("scoreS", [2 * NS, 40])
    sc2S = lsb("sc2S", [2 * NS, 40])
    mx8S = lsb("mx8S", [2 * NS, 8])
    thrS = lsb("thrS", [2 * NS, 1])
    selmS = lsb("selmS", [2 * NS, 40])
    selmST = lsb("selmST", [40, 2 * NS], BF16)
    pz = nps()
    P.tr(pz[0:2 * NS, 0:40], impTall[:], ident[0:40, 0:40])
    P.tt(scoreS[:], pz[0:2 * NS, 0:40], C["selMS"][:], ALU.mult)
    P.tt(scoreS[:], scoreS[:], C["selAS"][:], ALU.add)
    P.max8(mx8S[:], scoreS[:])
    P.match_replace(sc2S[:], mx8S[:], scoreS[:], -1e30)
    P.max8(mx8S[:], sc2S[:])
    P.ts(thrS[:], mx8S[:, 7:8], -1e5, ALU.max)
    P.ts(selmS[:], scoreS[:], thrS[:, 0:1], ALU.is_ge)
    pz = nps()
    P.tr(pz[0:40, 0:2 * NS], selmS[:], ident[0:2 * NS, 0:2 * NS])
    P.cp(selmST[:], pz[0:40, 0:2 * NS], eng="act")
    KsS = lsb("KsS", [128, PAST], BF16)
    VsS = lsb("VsS", [128, NPG, 2, 65], BF16)
    P.memset(VsS[:, :, :, 64:65], 1.0)
    wbuf = lsb("wbuf", [128, WT, 256])
    KwS = lsb("KwS", [128, WB], BF16)
    VwS = lsb("VwS", [128, WT, 2, 65], BF16)
    P.memset(VwS[:, :, :, 64:65], 1.0)
    for s in range(NS):
        pgb = pg[s % 2]
        for k in range(NPG):
            P.idma(pgb[:, k, :], csel_d[:], idxl[:, s * NPG + k:s * NPG + k + 1])
        P.dma(wbuf[:], cwin_d[l, s].r("(t p) c -> p t c", p=128))
        for k in range(NPG):
            pz = nps()
            P.tr(pz[:, 0:128], pgb[:, k, 0:128], ident[:])
            P.cp(KsS[:, k * 128:(k + 1) * 128], pz[:, 0:128], eng=("act" if k % 2 else "dve"))
            P.cp(VsS[:, k, :, 0:64], pgb[:, k, 128:256].r("p (g d) -> p g d", g=2), eng="pool")
        for t in range(WT):
            pz = nps()
            P.tr(pz[:, 0:128], wbuf[:, t, 0:128], ident[:])
            P.cp(KwS[:, t * 128:(t + 1) * 128], pz[:, 0:128], eng=("act" if t % 2 else "dve"))
            P.cp(VwS[:, t, :, 0:64], wbuf[:, t, 128:256].r("p (g d) -> p g d", g=2), eng="pool")
        for g in range(2):
            sp = pS[g]
            for k in range(NPG):
                P.mm(sp[:, k * 4:(k + 1) * 4], KsS[g * 64:(g + 1) * 64, k * 128:(k + 1) * 128], QsT[g * 64:(g + 1) * 64, :, s])
            for k in range(NPG):
                P.mm(pM[:, k:k + 1], C["DS"][:, k * 128:(k + 1) * 128], selmST[:, 2 * s + g:2 * s + g + 1])
            P.act(ETs[:, 0:4 * NPG], sp[:, 0:4 * NPG], AF.Exp, scale=0.125)
            P.tt(PTs[:, 0:4 * NPG].r("p (k h) -> p k h", h=4), ETs[:, 0:4 * NPG].r("p (k h) -> p k h", h=4),
                 pM[:, 0:NPG].r("p (k o) -> p k o", o=1).b([128, NPG, 4]), ALU.mult)
            for k in range(NPG):
                P.mm(pO[0:65, 0:4], VsS[:, k, g, :], PTs[:, k * 4:(k + 1) * 4], start=(k == 0), stop=(k == NPG - 1))
            P.cp(OTall[:, 1, g, :, s], pO[0:65, 0:4], eng="act")
            for t in range(WT):
                P.mm(sp[:, t * 4:(t + 1) * 4], KwS[g * 64:(g + 1) * 64, t * 128:(t + 1) * 128], QsT[g * 64:(g + 1) * 64, :, s])
            P.act(ETs[:, 0:4 * WT], sp[:, 0:4 * WT], AF.Exp, scale=0.125)
            if WB == 512:
                P.memset(ETs[0:1, 0:4], 0.0, eng="dve")
            for t in range(WT):
                P.mm(pO[0:65, 0:4], VwS[:, t, g, :], ETs[:, t * 4:(t + 1) * 4], start=(t == 0), stop=(t == WT - 1))
            P.cp(OTall[:, 2, g, :, s], pO[0:65, 0:4], eng="act")
    OtokS = lsb("OtokS", [NS, 3, 8, 65])
    for br in range(3):
        for g in range(2):
            pz = nps()
            for j in range(4):
                P.tr(pz[0:NS, j * 65:(j + 1) * 65], OTall[:, br, g, j, :], ident[0:65, 0:65])
            P.cp(OtokS[:, br, g * 4:(g + 1) * 4, :], pz[0:NS, 0:260].r("p (j e) -> p j e", e=65), eng="act")
    prod = lsb("prod", [NS, 4, 2, 64])
    dots = lsb("dots", [NS, 4, 2])
    enew = lsb("enew", [NS, 4, 2])
    tmpo = lsb("tmpo", [NS, 2, 4, 64])
    qv = Rs[:, 0:512].r("p (j g d) -> p j g d", j=4, g=2)
    for br, kc0, vc0 in ((1, 768, 896), (2, 1024, 1152)):
        kn = Rs[:, kc0:kc0 + 128].r("p (o g d) -> p o g d", o=1, g=2).b([NS, 4, 2, 64])
        P.tt(prod[:], qv, kn, ALU.mult)
        P.reduce(dots[:], prod[:], ALU.add)
        P.act(enew[:], dots[:], AF.Exp, scale=0.125)
        ev = enew[:].r("p j g -> p g j")
        vn = Rs[:, vc0:vc0 + 128].r("p (g o d) -> p g o d", g=2, o=1).b([NS, 2, 4, 64])
        P.tt(tmpo[:], vn, ev.r("p g (j o) -> p g j o", o=1).b([NS, 2, 4, 64]), ALU.mult)
        ob = OtokS[:, br, :, 0:64].r("p (g j) d -> p g j d", g=2)
        P.tt(ob, ob, tmpo[:], ALU.add)
        zb = OtokS[:, br, :, 64].r("p (g j) -> p g j", g=2)
        P.tt(zb, zb, ev, ALU.add)
    gatesS = lsb("gatesS", [NS, 24])
    coefS = lsb("coefS", [NS, 3, 8])
    onsaS = lsb("onsaS", [NS, 512])
    mixTs = lsb("mixTs", [128, 8, NS], BF16)
    P.act(gatesS[:], Ps[:, 1280:1304], AF.Sigmoid)
    P.ts(coefS[:], OtokS[:, :, :, 64], 1e-30, ALU.max)
    P.recip(coefS[:], coefS[:])
    P.tt(coefS[:], coefS[:], gatesS[:].r("p (h b) -> p b h", b=3), ALU.mult)
    for h in range(8):
        ov = onsaS[:, h * 64:(h + 1) * 64]
        P.ts(ov, OtokS[:, 0, h, 0:64], coefS[:, 0, h:h + 1], ALU.mult)
        P.stt(ov, OtokS[:, 1, h, 0:64], coefS[:, 1, h:h + 1], ov, ALU.mult, ALU.add)
        P.stt(ov, OtokS[:, 2, h, 0:64], coefS[:, 2, h:h + 1], ov, ALU.mult, ALU.add)
    pz = nps()
    for j in range(4):
        P.tr(pz[:, j * NS:(j + 1) * NS], onsaS[:, j * 128:(j + 1) * 128], ident[0:NS, 0:NS])
    P.cp(mixTs[:, 0:4, :].r("p c s -> p (c s)"), pz[:, 0:4 * NS], eng="act")
    rgct = lsb("rgct", [NS, 3, 256])
    rght = lsb("rght", [NS, 256])
    P.dma(rgct[:], srgc_d[l])
    P.dma(rght[:], srgh_d[l])
    P.dma(os_rgc.s("a")[l, :, 0:2, :], srgc_d[l, :, 1:3, :])
    P.dma(os_rgc.s("b")[l, :, 2, :], Ps[:, IN_OFF["rgx"]:IN_OFF["rgx"] + 256])
    xcs = lsb("xcs", [128, 2, 4, NS])
    h0T = lsb("h0T", [128, 2, NS])
    xcd = lsb("xcd", [128, NS])
    xcdb = lsb("xcdb", [128, NS], BF16)
    r_ = lsb("r_", [128, NS])
    i_ = lsb("i_", [128, NS])
    a_ = lsb("a_", [128, NS])
    b_ = lsb("b_", [128, NS])
    hT_ = lsb("hT_", [128, 2, NS])
    g1 = lsb("g1", [128, NS])
    g2 = lsb("g2", [128, NS])
    htok = lsb("htok", [NS, 256])
    for c2 in range(2):
        pz = nps()
        for k in range(3):
            P.tr(pz[:, k * NS:(k + 1) * NS], rgct[:, k, c2 * 128:(c2 + 1) * 128], ident[0:NS, 0:NS])
        P.tr(pz[:, 3 * NS:4 * NS], rght[:, c2 * 128:(c2 + 1) * 128], ident[0:NS, 0:NS])
        P.cp(xcs[:, c2, 0:3, :].r("p k s -> p (k s)"), pz[:, 0:3 * NS], eng="act")
        P.cp(h0T[:, c2, :], pz[:, 3 * NS:4 * NS], eng="act")
        P.cp(xcs[:, c2, 3, :], Fs[:, c2, :], eng="pool")
        pr = rgp[:, l, c2, :]
        P.ts(xcd[:], xcs[:, c2, 0, :], pr[:, 0:1], ALU.mult, pr[:, 4:5], ALU.add)
        for k in range(1, 4):
            P.stt(xcd[:], xcs[:, c2, k, :], pr[:, k:k + 1], xcd[:], ALU.mult, ALU.add)
        P.cp(xcdb[:], xcd[:], eng="pool")
        pz = nps()
        P.mm(pz[:, 0:NS], BD[:, c2, 0, :], xcdb[:])
        P.act(r_[:], pz[:, 0:NS], AF.Sigmoid, bias=pr[:, 5:6])
        pz = nps()
        P.mm(pz[:, 0:NS], BD[:, c2, 1, :], xcdb[:])
        P.act(i_[:], pz[:, 0:NS], AF.Sigmoid, bias=pr[:, 6:7])
        P.act(a_[:], r_[:], AF.Exp, scale=rgc[:, l, c2, :])
        P.act(b_[:], r_[:], AF.Exp, scale=rgc2[:, l, c2, :])
        P.ts(b_[:], b_[:], -1.0, ALU.mult, 1.0, ALU.add)
        P.ts(b_[:], b_[:], 0.0, ALU.max)
        P.act(b_[:], b_[:], AF.Sqrt)
        P.tt(i_[:], i_[:], xcd[:], ALU.mult)
        P.tt(b_[:], b_[:], i_[:], ALU.mult)
        P.tt(a_[:], a_[:], h0T[:, c2, :], ALU.mult)
        P.tt(hT_[:, c2, :], a_[:], b_[:], ALU.add)
        gelu_tanh(g2[:], Fs[:, 2 + c2, :], g1[:])
        P.tt(mixTs[:, 4 + c2, :], g2[:], hT_[:, c2, :], ALU.mult)
        pz = nps()
        P.tr(pz[0:NS, 0:128], hT_[:, c2, :], ident[:])
        P.cp(htok[:, c2 * 128:(c2 + 1) * 128], pz[0:NS, 0:128], eng="act")
    P.dma(os_rgh[l], htok[:])
    Sst = lsb("Sst", [128, NS, 2, 64])
    for hh in range(2):
        P.dma(Sst[hh * 64:(hh + 1) * 64, :, :, :], shgs_d[l].r("s (c hh) k v -> hh k s c v", hh=2)[hh])
    fT = lsb("fT", [128, 2, NS])
    kT_ = lsb("kT_", [128, 2, NS])
    qT_ = lsb("qT_", [128, 2, NS])
    for c2 in range(2):
        P.act(fT[:, c2, :], Fs[:, 6 + c2, :], AF.Sigmoid)
        P.ts(fT[:, c2, :], fT[:, c2, :], oml[:, c2, l:l + 1], ALU.mult, lb[:, c2, l:l + 1], ALU.add)
        P.ts(kT_[:, c2, :], fT[:, c2, :], -1.0, ALU.mult, 1.0, ALU.add)
        P.act(qT_[:, c2, :], Fs[:, 4 + c2, :], AF.Sigmoid)
        P.tt(qT_[:, c2, :], qT_[:, c2, :], Fs[:, 4 + c2, :], ALU.mult)
    vdiag = lsb("vdiag", [NS, NS, 128])
    t2c = lsb("t2c", [128, 4, 2, 64])
    fb = fT[:].r("p c (s o) -> p s c o", o=1).b([128, NS, 2, 64])
    kb = kT_[:].r("p c (s o) -> p s c o", o=1).b([128, NS, 2, 64])
    P.tt(Sst[:], Sst[:], fb, ALU.mult)
    SPC = 4
    for hh in range(2):
        hp = slice(hh * 64, (hh + 1) * 64)
        v_hh = Ps[:, IN_OFF["hi"]:IN_OFF["hi"] + 256].r("p (c hh v) -> p hh c v", hh=2, v=64)[:, hh]
        P.tt(vdiag[:].r("p s (c v) -> p s c v", c=2), v_hh.r("p (o c) v -> p o c v", o=1).b([NS, NS, 2, 64]),
             ident[0:NS, 0:NS].r("p (s o t) -> p s o t", o=1, t=1).b([NS, NS, 2, 64]), ALU.mult)
        for q4 in range((NS + SPC - 1) // SPC):
            ns_ = min(SPC, NS - q4 * SPC)
            pz = nps()
            P.mm(pz[hp, 0:ns_ * 128], ones_f[0:NS, 0:64], vdiag[:, q4 * SPC:q4 * SPC + ns_, :].r("p s c -> p (s c)"))
            sl = slice(q4 * SPC, q4 * SPC + ns_)
            P.tt(t2c[hp, 0:ns_], pz[hp, 0:ns_ * 128].r("p (s c v) -> p s c v", s=ns_, c=2), kb[hp, sl, :, :], ALU.mult)
            P.tt(Sst[hp, sl, :, :], Sst[hp, sl, :, :], t2c[hp, 0:ns_], ALU.add)
    Snew = Sst
    for hh in range(2):
        P.dma(os_hgs[l].r("s (c hh) k v -> hh k s c v", hh=2)[hh], Snew[hh * 64:(hh + 1) * 64, :, :, :])
    pz = nps()
    for s in range(NS):
        for c2 in range(2):
            for hh in range(2):
                hp = slice(hh * 64, (hh + 1) * 64)
                P.mm(pz[hp, c2 * NS + s:c2 * NS + s + 1], Snew[hp, s, c2, :], qT_[hp, c2, s:s + 1])
    oTs = lsb("oTs", [128, 2, NS])
    P.cp(oTs[:].r("p c s -> p (c s)"), pz[:, 0:2 * NS], eng="act")
    osq_ = lsb("osq_", [128, NS], BF16)
    ors_ = lsb("ors_", [128, NS])
    for c2 in range(2):
        P.act(osq_[:], oTs[:, c2, :], AF.Square)
        pz = nps()
        P.mm(pz[:, 0:NS], C["blockones"][:], osq_[:])
        P.act(ors_[:], pz[:, 0:NS], AF.Sqrt, bias=EPS, scale=1.0 / 64)
        P.recip(ors_[:], ors_[:])
        P.tt(ors_[:], ors_[:], oTs[:, c2, :], ALU.mult)
        P.act(g1[:], Fs[:, 8 + c2, :], AF.Sigmoid)
        P.tt(g1[:], g1[:], Fs[:, 8 + c2, :], ALU.mult)
        P.stt(mixTs[:, 6 + c2, :], ors_[:], hgp[:, c2, 1, l:l + 1], g1[:], ALU.mult, ALU.mult)
    for dc in range(8):
        pz = nps()
        for k in range(8):
            P.mm(pz[:, 0:NS], Wout[:, k, dc * 128:(dc + 1) * 128], mixTs[:, k, :], start=(k == 0), stop=(k == 7))
        P.tt(xsT[:, dc, :], xsT[:, dc, :], pz[:, 0:NS], ALU.add)


def decode_ffn(l, lsb, Wup, Wdn):
    ysB = lsb("ysB", [128, 8, NS], BF16)
    HTs = lsb("HTs", [128, 32, NS], BF16)
    hrs = lsb("hrs", [128, NS])
    rmsnorm_T(xsT[:], normT[:, l, 1, :], ysB[:], NS)
    for f in range(32):
        pz = nps()
        for c in range(8):
            P.mm(pz[:, :NS], Wup[:, c, f * 128:(f + 1) * 128], ysB[:, c, :], start=(c == 0), stop=(c == 7))
        P.act(hrs[:], pz[:, :NS], AF.Relu)
        P.tt(HTs[:, f, :], hrs[:], hrs[:], ALU.mult, eng="pool")
    for dc in range(8):
        pz = nps()
        for f in range(32):
            P.mm(pz[:, :NS], Wdn[:, f, dc * 128:(dc + 1) * 128], HTs[:, f, :], start=(f == 0), stop=(f == 31))
        P.tt(xsT[:, dc, :], xsT[:, dc, :], pz[:, :NS], ALU.add)
    if l == L - 1:
        yfs = lsb("yfs", [128, 8, NS])
        ystok = lsb("ystok", [NS, D])
        sqv = sq[:, :, :NS]
        P.act(sqv, xsT[:], AF.Square)
        pz = nps()
        for c in range(8):
            P.mm(pz[:, :NS], ones_bf[:], sq[:, c, :NS], start=(c == 0), stop=(c == 7))
        P.act(rstd[:, :NS], pz[:, :NS], AF.Sqrt, bias=EPS, scale=1.0 / D)
        P.recip(rstd[:, :NS], rstd[:, :NS])
        P.tt(yfs[:], xsT[:], rstd[:, :NS].r("p (o t) -> p o t", o=1).b([128, 8, NS]), ALU.mult)
        P.tt(yfs[:], yfs[:], fnormT[:].r("p (c o) -> p c o", o=1).b([128, 8, NS]), ALU.mult)
        for c in range(8):
            pz = nps()
            P.tr(pz[0:NS, 0:128], yfs[:, c, :], ident[:])
            P.cp(ystok[:, c * 128:(c + 1) * 128], pz[0:NS, 0:128], eng="act")
        P.dma(ys_out[:], ystok[:])


QPERM = np.concatenate([np.arange(h * 64, (h + 1) * 64) for h in (0, 4, 1, 5, 2, 6, 3, 7)])


def host_inputs(cfg, inp, consts, b, core=0):
    L = cfg["L"]
    f = lambda a: np.ascontiguousarray(np.asarray(a, dtype=np.float32))
    m = {}
    m["x_prompt"] = f(inp["x_prompt"][b])
    w_in = np.asarray(inp["w_in"], np.float32).copy()
    w_in[:, :, 0:512] = w_in[:, :, QPERM]
    m["w_in"] = f(w_in)
    m["w_out"] = f(inp["w_out"])
    m["w_up"] = f(inp["w_up"])
    m["w_down"] = f(inp["w_down"])
    nm = np.stack([np.asarray(inp["norm_mix"]), np.asarray(inp["norm_ffn"])], 1)
    m["normT"] = f(nm.reshape(L, 2, 8, 128).transpose(3, 0, 1, 2))
    m["fnormT"] = f(np.asarray(inp["final_norm"]).reshape(8, 128).T)
    m["cmp_w1"] = f(inp["cmp_w1"])
    m["cmp_posT"] = f(np.asarray(inp["cmp_pos"]).reshape(L, 2, 16, 2, 64).transpose(3, 4, 0, 1, 2).reshape(128, L, 2, 16))
    m["cmp_b1T"] = f(np.asarray(inp["cmp_b1"]).reshape(L, 2, 2, 128).transpose(3, 0, 1, 2))
    m["cmp_w2"] = f(inp["cmp_w2"])
    m["cmp_b2"] = f(inp["cmp_b2"])
    cw = np.asarray(inp["rg_conv_w"])
    rp = np.zeros((L, 9, 256), np.float32)
    rp[:, 0:4] = cw
    rp[:, 4] = np.asarray(inp["rg_conv_b"])
    rp[:, 5] = np.asarray(inp["rg_ba"])
    rp[:, 6] = np.asarray(inp["rg_bx"])
    rp[:, 7] = np.asarray(inp["rg_lambda"])
    m["rg_pT"] = f(rp.reshape(L, 9, 2, 128).transpose(3, 0, 2, 1))
    m["rg_wa"] = f(inp["rg_wa"])
    m["rg_wx"] = f(inp["rg_wx"])
    hp = np.stack([np.asarray(inp["hg_lower_bounds"]), np.asarray(inp["hg_gain"])], 0)
    m["hg_pT"] = f(hp.reshape(2, L, 2, 128).transpose(3, 2, 0, 1))
    for k, v in consts.items():
        m["c_" + k] = f(v)
    NS = cfg.get("NS", 0)
    if NS:
        sl = slice(core * NS, (core + 1) * NS)
        past = cfg["past"]
        npg = past // 128
        wb = min(512, past)
        m["x_sample"] = f(np.asarray(inp["x_sample"])[sl, 0, :])
        m["page_table"] = np.ascontiguousarray(np.asarray(inp["page_table"])[sl].reshape(1, NS * npg).astype(np.int32))
        m["cache_cmp"] = f(inp["cache_nsa_cmp_kv"]).reshape(-1, 256)
        m["cache_sel"] = f(inp["cache_nsa_sel_kv"]).reshape(-1, 256)
        m["cache_win"] = f(np.asarray(inp["cache_nsa_win_kv"])[:, sl]).reshape(L, NS, wb, 256)
        m["st_rgh"] = f(np.asarray(inp["state_rglru_h"])[:, sl])
        m["st_rgc"] = f(np.asarray(inp["state_rglru_conv"])[:, sl])
        m["st_hgs"] = f(np.asarray(inp["state_hgrn_s"])[:, sl])
    return m


_CACHE = {}


def run(cfg, inp, ncores=8):
    key = tuple(sorted(cfg.items()))
    if key not in _CACHE:
        _CACHE[key] = build(cfg)
    nc, consts = _CACHE[key]
    B = np.asarray(inp["x_prompt"]).shape[0]
    maps = [host_inputs(cfg, inp, consts, c % B, c) for c in range(ncores)]
    res = run_bass_kernel_spmd(nc, maps, core_ids=list(range(ncores)))
    return res.results


def kernel(**inp):
    xp = np.asarray(inp["x_prompt"])
    B, Tn, _ = xp.shape
    L = np.asarray(inp["w_in"]).shape[0]
    NDEC = np.asarray(inp["x_sample"]).shape[0]
    npg = np.asarray(inp["page_table"]).shape[1]
    past = npg * 128
    ncores = 8
    NS = NDEC // ncores
    npool = np.asarray(inp["cache_nsa_cmp_kv"]).shape[1]
    cfg = dict(T=Tn, L=L, past=past, NS=NS, npool=npool)
    r = run(cfg, inp, ncores)
    wb = min(512, past)
    wt = min(512, Tn)
    f = np.float32
    y_prompt = np.stack([r[b]["y_prompt"] for b in range(B)], 0).astype(f)
    y_sample = np.concatenate([r[c]["y_sample"] for c in range(ncores)], 0).reshape(NDEC, 1, D).astype(f)

    def pst(name, shp):
        return np.stack([np.asarray(r[b][name]).reshape((L,) + shp) for b in range(B)], 1).astype(f)

    def sst(name, shp):
        return np.concatenate([np.asarray(r[c][name]).reshape((L, NS) + shp) for c in range(ncores)], 1).astype(f)

    return (y_prompt, y_sample,
            pst("p_cmp_kv", (Tn, 2, 2, 64)), pst("p_sel_kv", (Tn, 2, 2, 64)), pst("p_win_kv", (wt, 2, 2, 64)),
            pst("p_rg_h", (256,)), pst("p_rg_conv", (3, 256)), pst("p_hg_s", (4, 64, 64)),
            sst("s_cmp_kv", (1, 2, 2, 64)), sst("s_sel_kv", (1, 2, 2, 64)), sst("s_win_kv", (wb, 2, 2, 64)),
            sst("s_rg_h", (256,)), sst("s_rg_conv", (3, 256)), sst("s_hg_s", (4, 64, 64)))
```

```python
import contextlib
import numpy as np
import ml_dtypes
import concourse.bass as bass
import concourse.mybir as mybir
from concourse.bass_utils import run_bass_kernel_spmd

F32 = mybir.dt.float32
BF16 = mybir.dt.bfloat16
I32 = mybir.dt.int32
AF = mybir.ActivationFunctionType
ALU = mybir.AluOpType
AX = mybir.AxisListType

D = 1024
NQH = 8
EPS = 1e-6
IN_OFF = dict(q=0, kvs=512, gate=1280, rgx=1304, rgg=1560, hq=1816, hf=2072, hi=2328, hg=2584)
D_IN = 2840


class V:
    __slots__ = ("ap", "key")

    def __init__(self, ap, key):
        self.ap = ap
        self.key = key

    def r(self, pat, **kw):
        return V(self.ap.rearrange(pat, **kw), self.key)

    def b(self, shape):
        return V(self.ap.broadcast_to(shape), self.key)

    def __getitem__(self, idx):
        return V(self.ap[idx], self.key)


class _Sub:
    def __init__(self, t, k):
        self.t = t
        self.k = k

    def __getitem__(self, idx):
        return V(self.t.t[idx], (self.t.name, self.k))


class T:
    def __init__(self, name, t):
        self.name = name
        self.t = t

    def __getitem__(self, idx):
        return V(self.t[idx], (self.name, None))

    def s(self, k):
        return _Sub(self, k)


class StopBuild(Exception):
    pass


class Prog:
    SEM_CAP = 30000

    def __init__(self, nc, es):
        self.nc = nc
        self.es = es
        self.ops = []
        self.state = {}
        self.bar = set()
        self.eng = dict(pe=nc.tensor, act=nc.scalar, dve=nc.vector, pool=nc.gpsimd, sp=nc.sync)
        self.last = {}
        self.nosame = False
        self.dma_hist = {"sp": [], "pool": [], "act": []}

    def _touch(self, idx, key, write, deps):
        name, sub = key
        st = self.state.setdefault(name, {})
        if sub is None:
            ents = list(st.values())
        else:
            ents = [e for k, e in st.items() if k == sub or k is None]
        for e in ents:
            if e[0] is not None:
                deps.add(e[0])
            if write:
                deps.update(e[1])
        if write:
            if sub is None:
                st.clear()
                st[None] = [idx, []]
            else:
                st[sub] = [idx, []]
        else:
            st.setdefault(sub, [None, []])[1].append(idx)

    def add(self, eng, fn, r=(), w=(), dma=False, rg=None):
        idx = len(self.ops)
        deps = set(self.bar)
        for k in r:
            self._touch(idx, k, k[0].startswith("ps_"), deps)
        for k in w:
            self._touch(idx, k, True, deps)
        deps.discard(idx)
        self.ops.append(dict(eng=eng, fn=fn, deps=deps, dma=dma, rg=rg))
        self.last[eng] = idx
        if dma:
            self.dma_hist[eng].append(idx)
        return idx

    def barrier(self):
        b = set(self.last.values())
        for h in self.dma_hist.values():
            b.update(h[-32:])
        self.bar = b

    @staticmethod
    def _k(*vs):
        return [v.key for v in vs if isinstance(v, V)]

    @staticmethod
    def _a(v):
        return v.ap if isinstance(v, V) else v

    def mm(self, out, lhsT, rhs, start=True, stop=True):
        nc = self.nc
        self.add("pe", lambda: nc.tensor.matmul(out.ap, lhsT=lhsT.ap, rhs=rhs.ap, start=start, stop=stop,
                                                skip_group_check=True),
                 r=self._k(lhsT, rhs), w=[out.key], rg=lhsT.ap.start_partition())

    def tr(self, out, in_, ident):
        nc = self.nc
        self.add("pe", lambda: nc.tensor.transpose(out.ap, in_.ap, ident.ap), r=self._k(in_, ident), w=[out.key],
                 rg=in_.ap.start_partition())

    def act(self, out, in_, func, bias=None, scale=None, eng="act"):
        nc = self.nc
        kw = {}
        if bias is not None:
            kw["bias"] = self._a(bias)
        if scale is not None:
            kw["scale"] = self._a(scale)
        self.add("act", lambda: nc.scalar.activation(out.ap, in_.ap, func, **kw),
                 r=self._k(in_, bias, scale), w=[out.key])

    def tt(self, out, in0, in1, op, eng="dve"):
        e = self.eng[eng]
        self.add(eng, lambda: e.tensor_tensor(out.ap, in0.ap, in1.ap, op), r=self._k(in0, in1), w=[out.key])

    def ts(self, out, in0, s1, op0, s2=None, op1=None, eng="dve"):
        e = self.eng[eng]
        a1, a2 = self._a(s1), self._a(s2)
        if op1 is None:
            self.add(eng, lambda: e.tensor_scalar(out.ap, in0.ap, a1, None, op0), r=self._k(in0, s1), w=[out.key])
        else:
            self.add(eng, lambda: e.tensor_scalar(out.ap, in0.ap, a1, a2, op0, op1),
                     r=self._k(in0, s1, s2), w=[out.key])

    def stt(self, out, in0, scalar, in1, op0, op1):
        nc = self.nc
        sa = self._a(scalar)
        self.add("dve", lambda: nc.vector.scalar_tensor_tensor(out.ap, in0.ap, sa, in1.ap, op0, op1),
                 r=self._k(in0, scalar, in1), w=[out.key])

    def cp(self, out, in_, eng="dve"):
        if eng == "act":
            nc = self.nc
            self.add("act", lambda: nc.scalar.copy(out.ap, in_.ap), r=self._k(in_), w=[out.key])
        else:
            e = self.eng[eng]
            self.add(eng, lambda: e.tensor_copy(out.ap, in_.ap), r=self._k(in_), w=[out.key])

    def memset(self, out, val, eng="pool"):
        e = self.eng[eng]
        self.add(eng, lambda: e.memset(out.ap, val), w=[out.key])

    def recip(self, out, in_):
        nc = self.nc
        self.add("dve", lambda: nc.vector.reciprocal(out.ap, in_.ap), r=self._k(in_), w=[out.key])

    def reduce(self, out, in_, op, axis=AX.X):
        nc = self.nc
        self.add("dve", lambda: nc.vector.tensor_reduce(out.ap, in_.ap, axis, op), r=self._k(in_), w=[out.key])

    def scan(self, out, d0, d1, init, op0, op1):
        nc = self.nc
        ia = self._a(init)
        self.add("dve", lambda: nc.vector.tensor_tensor_scan(out.ap, d0.ap, d1.ap, ia, op0, op1),
                 r=self._k(d0, d1, init), w=[out.key])

    def max8(self, out, in_):
        nc = self.nc
        self.add("dve", lambda: nc.vector.max(out.ap, in_.ap), r=self._k(in_), w=[out.key])

    def match_replace(self, out, rep, vals, imm):
        nc = self.nc
        self.add("dve", lambda: nc.vector.match_replace(out.ap, rep.ap, vals.ap, imm), r=self._k(rep, vals),
                 w=[out.key])

    def dma(self, out, in_, eng="sp", **kw):
        e = self.eng[eng]
        self.add(eng, lambda: e.dma_start(out=out.ap, in_=in_.ap, **kw), r=self._k(in_), w=[out.key], dma=True)

    def idma(self, out, in_, idx_v, axis=0):
        nc = self.nc
        self.add("pool", lambda: nc.gpsimd.indirect_dma_start(
            out=out.ap, out_offset=None, in_=in_.ap,
            in_offset=bass.IndirectOffsetOnAxis(ap=idx_v.ap, axis=axis)),
            r=self._k(in_, idx_v), w=[out.key], dma=True)

    def emit(self):
        nc, es, ops = self.nc, self.es, self.ops

        nosame = self.nosame

        def skip(d, o):
            if d["dma"] or o["dma"] or d["eng"] != o["eng"]:
                return False
            if d["eng"] == "pe":
                return d["rg"] == o["rg"]
            return nosame

        need = set()
        for o in ops:
            for d in o["deps"]:
                if not skip(ops[d], o):
                    need.add(d)
        csem = {}
        ccount = {e: 0 for e in self.eng}
        RING = {"sp": 24, "pool": 12, "act": 4}
        rings = {e: [] for e in RING}
        rcount = {e: 0 for e in RING}
        rtot = {}
        known = {e: {} for e in self.eng}
        tok = [None] * len(ops)
        for i, o in enumerate(ops):
            e = o["eng"]
            E = self.eng[e]
            kn = known[e]
            waits = {}
            for d in o["deps"]:
                if skip(ops[d], o):
                    continue
                s_, v = tok[d]
                if waits.get(id(s_), (None, 0))[1] < v:
                    waits[id(s_)] = (s_, v)
            pre = None
            if o["dma"]:
                j = rcount[e]
                rcount[e] += 1
                slot = j % RING[e]
                if slot >= len(rings[e]):
                    rings[e].append(es.enter_context(nc.semaphore(f"r_{e}_{slot}")))
                rs = rings[e][slot]
                prev = rtot.get(id(rs), 0)
                if prev > 0 and waits.get(id(rs), (None, 0))[1] < prev:
                    waits[id(rs)] = (rs, prev)
                rtot[id(rs)] = prev + 16
                pre = (rs, prev + 16)
            for key, (s_, v) in waits.items():
                if kn.get(key, 0) >= v:
                    continue
                E.wait_ge(s_, v)
                kn[key] = v
            ins = o["fn"]()
            if o["dma"]:
                ins.then_inc(pre[0], 16)
                tok[i] = pre
            elif i in need:
                c = ccount[e]
                ep, val = c // self.SEM_CAP, c % self.SEM_CAP + 1
                if (e, ep) not in csem:
                    csem[(e, ep)] = es.enter_context(nc.semaphore(f"c_{e}_{ep}"))
                ins.then_inc(csem[(e, ep)], 1)
                ccount[e] = c + 1
                tok[i] = (csem[(e, ep)], val)
        for e in rings:
            for rs in rings[e]:
                nc.sync.wait_ge(rs, rtot[id(rs)])


def _fix_waits():
    pass


def rope_tab(pos):
    inv = 500000.0 ** (-np.arange(8, dtype=np.float32) * 2.0 / 16.0)
    ang = pos.astype(np.float32)[:, None] * inv.astype(np.float32)
    return np.cos(ang).astype(np.float32), np.sin(ang).astype(np.float32)


def make_consts(Tn, past, ns=0):
    NT = Tn // 128
    c = {}
    c["ident"] = np.eye(128, dtype=np.float32)
    cs, sn = rope_tab(np.arange(Tn))
    c["ropeP"] = np.stack([cs.reshape(NT, 128, 8).transpose(1, 0, 2), sn.reshape(NT, 128, 8).transpose(1, 0, 2)], 1)
    slot = np.arange(NT * 8)
    cs, sn = rope_tab(16 * slot + 15)
    c["ropeC"] = np.stack([cs.reshape(NT, 8, 8).transpose(1, 0, 2), sn.reshape(NT, 8, 8).transpose(1, 0, 2)], 1)
    m = np.zeros((128, 17, 128), np.float32)
    p = np.arange(128)[:, None]
    qi = np.arange(128)[None, :]
    for r in range(16):
        sp = p - 8 * r
        m[:, r, :] = np.where(sp < 0, 1.0, np.where(sp <= 7, (qi >= 16 * sp + 15), 0.0))
    m[:, 16, :] = 1.0
    c["cmpmask"] = m
    nslot = NT * 8
    KT = (nslot + 127) // 128
    sm = np.zeros((KT * 128, 128), np.float32)
    for s in range(1, nslot):
        c0 = 16 * (s - 1)
        for j in range(128):
            if c0 < 64 * j + 64 and c0 + 32 > 64 * j:
                sm[s, j] = 1.0
    c["smap"] = sm.reshape(KT, 128, 128).transpose(1, 0, 2).copy()
    A = np.zeros((128, 255), np.float32)
    M = np.zeros((128, 255), np.float32)
    for q in range(128):
        hi = 1 if q >= 64 else 0
        for ci in range(255):
            cc = ci - 127
            valid = cc <= hi
            forced = (cc == hi) or (cc == hi - 1)
            if not valid:
                A[q, ci] = -1e6
            elif forced:
                A[q, ci] = 1e6
            else:
                M[q, ci] = 1.0
    c["selA"] = A
    c["selM"] = M
    ki = np.arange(128)[:, None]
    c["tri"] = np.stack([(ki <= qi).astype(np.float32), (ki > qi).astype(np.float32)], 1)
    bo = np.zeros((128, 128), np.float32)
    bo[:64, :64] = 1.0
    bo[64:, 64:] = 1.0
    c["blockones"] = bo
    s64 = np.arange(64)[:, None]
    t64 = np.arange(64)[None, :]
    c["tri64"] = np.concatenate([(s64 <= t64).astype(np.float32)] * 2, 0)
    if ns:
        NCB = past // 16 - 1
        NSB = past // 64 + 1
        CUR = past // 64
        cs, sn = rope_tab(np.full((ns,), past))
        c["ropeS"] = np.stack([cs, sn], 1)
        cs, sn = rope_tab(16 * np.arange(NCB) + 31)
        c["ropeCS"] = np.stack([cs, sn], 1)
        sm = np.zeros((NCB, NSB), np.float32)
        for i in range(NCB):
            for j in range(NSB):
                if 16 * i < 64 * j + 64 and 16 * i + 32 > 64 * j:
                    sm[i, j] = 1.0
        c["smapS"] = sm
        A = np.zeros((2 * ns, 40), np.float32)
        M = np.zeros((2 * ns, 40), np.float32)
        assert NSB <= 40
        for j in range(40):
            if j >= NSB:
                A[:, j] = -1e6
            elif j in (0, CUR, CUR - 1):
                A[:, j] = 1e6
            else:
                M[:, j] = 1.0
        c["selAS"] = A
        c["selMS"] = M
        Dm = np.zeros((40, past), np.float32)
        u = np.arange(past)
        Dm[u // 64, u] = 1.0
        c["DS"] = Dm
        c["pcol"] = np.arange(128, dtype=np.float32).reshape(128, 1)
    return c


CONST_BF = ("cmpmask", "smap", "tri", "blockones", "tri64", "smapS", "DS")


def build(cfg):
    Tn, L = cfg["T"], cfg["L"]
    NT = Tn // 128
    KTC = (NT * 8 + 127) // 128
    WIN_T = min(4, NT)
    nc = bass.Bass("TRN2", target_bir_lowering=False)
    es = contextlib.ExitStack()
    P = Prog(nc, es)
    P.nosame = bool(cfg.get("nosame"))
    if cfg.get("dry"):
        P.add = lambda *a, **k: None
    NS = cfg.get("NS", 0)
    PAST = cfg.get("past", 0)
    SKIP = cfg.get("skip", ())
    consts = make_consts(Tn, PAST, NS)

    def dr_in(name, shape, dt=F32):
        return T(name, nc.dram_tensor(name, list(shape), dt, kind="ExternalInput"))

    def dr_out(name, shape, dt=F32):
        return T(name, nc.dram_tensor(name, list(shape), dt, kind="ExternalOutput"))

    def sb(name, shape, dt=F32):
        return T("s_" + name, es.enter_context(nc.sbuf_tensor("s_" + name, list(shape), dt)))

    def ps(name, shape=(128, 512), dt=F32):
        return T(name, es.enter_context(nc.psum_tensor(name, list(shape), dt)))

    x_in = dr_in("x_prompt", [Tn, D])
    w_in_d = dr_in("w_in", [L, D, D_IN])
    w_out_d = dr_in("w_out", [L, D, D])
    w_up_d = dr_in("w_up", [L, D, 4 * D])
    w_dn_d = dr_in("w_down", [L, 4 * D, D])
    normT_d = dr_in("normT", [128, L, 2, 8])
    fnormT_d = dr_in("fnormT", [128, 8])
    w1_d = dr_in("cmp_w1", [L, 2, 32, 64, 256])
    posT_d = dr_in("cmp_posT", [128, L, 2, 16])
    b1T_d = dr_in("cmp_b1T", [128, L, 2, 2])
    w2_d = dr_in("cmp_w2", [L, 2, 256, 64])
    b2_d = dr_in("cmp_b2", [L, 2, 64])
    rgpT_d = dr_in("rg_pT", [128, L, 2, 9])
    rgwa_d = dr_in("rg_wa", [L, 4, 64, 64])
    rgwx_d = dr_in("rg_wx", [L, 4, 64, 64])
    hgpT_d = dr_in("hg_pT", [128, 2, 2, L])
    cdr = {k: dr_in("c_" + k, v.shape) for k, v in consts.items()}

    y_out = dr_out("y_prompt", [Tn, D])
    o_cmp = dr_out("p_cmp_kv", [L, Tn, 256])
    o_sel = dr_out("p_sel_kv", [L, Tn, 256])
    o_win = dr_out("p_win_kv", [L, WIN_T * 128, 256])
    o_rgh = dr_out("p_rg_h", [L, 256])
    o_rgc = dr_out("p_rg_conv", [L, 3, 256])
    o_hgs = dr_out("p_hg_s", [L, 4, 64, 64])
    xs_d = T("xscr", nc.dram_tensor("xscr", [D, Tn], F32, kind="Internal"))
    if NS:
        NPG = PAST // 128
        WB = min(512, PAST)
        NPOOL = cfg["npool"]
        xsam_d = dr_in("x_sample", [NS, D])
        pt_d = dr_in("page_table", [1, NS * NPG], I32)
        ccmp_d = dr_in("cache_cmp", [L * NPOOL * 128, 256])
        csel_d = dr_in("cache_sel", [L * NPOOL * 128, 256])
        cwin_d = dr_in("cache_win", [L, NS, WB, 256])
        srgh_d = dr_in("st_rgh", [L, NS, 256])
        srgc_d = dr_in("st_rgc", [L, NS, 3, 256])
        shgs_d = dr_in("st_hgs", [L, NS, 4, 64, 64])
        ys_out = dr_out("y_sample", [NS, D])
        os_cmp = dr_out("s_cmp_kv", [L, NS, 256])
        os_sel = dr_out("s_sel_kv", [L, NS, 256])
        os_win = dr_out("s_win_kv", [L, NS, WB, 256])
        os_rgh = dr_out("s_rg_h", [L, NS, 256])
        os_rgc = dr_out("s_rg_conv", [L, NS, 3, 256])
        os_hgs = dr_out("s_hg_s", [L, NS, 4, 64, 64])

    C = {}
    for k, v in consts.items():
        shp = list(v.shape)
        if k in ("ropeP", "ropeC"):
            continue
        if k in CONST_BF:
            C[k] = sb("k_" + k, shp, BF16)
            P.dma(C[k][:], cdr[k][:], eng="pool")
        else:
            C[k] = sb("k_" + k, shp, F32)
            P.dma(C[k][:], cdr[k][:])
    ident = C["ident"]
    ident_bf = sb("ident_bf", [128, 128], BF16)
    P.cp(ident_bf[:], ident[:], eng="pool")
    ones_bf = sb("ones_bf", [128, 128], BF16)
    P.memset(ones_bf[:], 1.0)
    ones_f = sb("ones_f", [128, 128], F32)
    P.memset(ones_f[:], 1.0)
    P.memset(ones_f[:, 64:65], 0.0)
    normT = sb("normT", [128, L, 2, 8])
    P.dma(normT[:], normT_d[:])
    fnormT = sb("fnormT", [128, 8])
    P.dma(fnormT[:], fnormT_d[:])
    rgp = sb("rgp", [128, L, 2, 9])
    P.dma(rgp[:], rgpT_d[:])
    hgp = sb("hgp", [128, 2, 2, L])
    P.dma(hgp[:], hgpT_d[:])
    lbe = sb("lbe", [128, 2, L])
    lb = sb("lb", [128, 2, L])
    lbs = sb("lbs", [128, 2, 1])
    oml = sb("oml", [128, 2, L])
    P.act(lbe[:], hgp[:, :, 0, :], AF.Exp)
    P.reduce(lbs[:, :, 0], lbe[:], ALU.add)
    P.recip(lbs[:], lbs[:])
    P.tt(lbe[:], lbe[:], lbs[:].b([128, 2, L]), ALU.mult)
    P.memset(lb[:, :, 0:1], 0.0)
    for l in range(1, L):
        P.tt(lb[:, :, l:l + 1], lb[:, :, l - 1:l], lbe[:, :, l:l + 1], ALU.add)
    P.ts(oml[:], lb[:], -1.0, ALU.mult, 1.0, ALU.add)
    rgc = sb("rgc", [128, L, 2, 1])
    P.act(rgc[:], rgp[:, :, :, 7:8], AF.Exp, scale=-1.0)
    P.act(rgc[:], rgc[:], AF.Ln, bias=1.0)
    P.ts(rgc[:], rgc[:], -8.0, ALU.mult)
    rgc2 = sb("rgc2", [128, L, 2, 1])
    P.ts(rgc2[:], rgc[:], 2.0, ALU.mult)

    pa = [ps("ps_a%d" % i) for i in range(3)]
    pS = [ps("ps_s%d" % i) for i in range(2)]
    pM = ps("ps_m")
    pO = ps("ps_o")
    pI = ps("ps_i")
    rot = [0]

    def nps():
        rot[0] = (rot[0] + 1) % 3
        return pa[rot[0]]

    xT = sb("xT", [128, 8, 128])
    yT = sb("yT", [128, 8, 128], BF16)
    sq = sb("sq", [128, 8, 128], BF16)
    rstd = sb("rstd", [128, 128])

    def rmsnorm_T(xv, gv, out_bf, ntok, tmp):
        sqv = sq[:, :, :ntok]
        P.act(sqv, xv, AF.Square)
        pz = nps()
        for c in range(8):
            P.mm(pz[:, :ntok], ones_bf[:], sq[:, c, :ntok], start=(c == 0), stop=(c == 7))
        P.act(rstd[:, :ntok], pz[:, :ntok], AF.Sqrt, bias=EPS, scale=1.0 / D)
        P.recip(rstd[:, :ntok], rstd[:, :ntok])
        P.tt(tmp, xv, rstd[:, :ntok].r("p (o t) -> p o t", o=1).b([128, 8, ntok]), ALU.mult)
        P.tt(out_bf, tmp, gv.r("p (c o) -> p c o", o=1).b([128, 8, ntok]), ALU.mult)

    if NS:
        xsT = sb("xsT", [128, 8, NS])
        with nc.sbuf_tensor("s_xstok", [NS, D], F32) as _xt:
            xstok = T("s_xstok", _xt)
            P.dma(xstok[:], xsam_d[:])
            for c in range(8):
                pz = nps()
                P.tr(pz[:, 0:NS], xstok[:, c * 128:(c + 1) * 128], ident[0:NS, 0:NS])
                P.cp(xsT[:, c, :], pz[:, 0:NS], eng="act")
        P.barrier()
        ptb = sb("ptb", [128, NS * NPG], I32)
        idxt = sb("idxt", [128, NS * NPG], I32)
        P.dma(ptb[:], pt_d[:].b([128, NS * NPG]))
        P.ts(idxt[:], ptb[:], 128.0, ALU.mult, C["pcol"][:, 0:1], ALU.add)

    def chk(k):
        if cfg.get("stop") == k:
            raise StopBuild()

    try:
        _layers(locals())
    except StopBuild:
        pass
    P.emit()
    return nc, consts


def _layers(env):
    globals().update({k: v for k, v in env.items() if not k.startswith("__")})
    chk(1)
    for l in range(L):
        with contextlib.ExitStack() as les:
            cur = [les]

            def lsb(name, shape, dt=F32):
                return T("s_" + name + "_%d" % l, cur[0].enter_context(nc.sbuf_tensor("s_" + name + "_%d" % l, list(shape), dt)))

            P.barrier()
            Wout = lsb("Wout", [128, 8, D], BF16)
            for c in range(8):
                P.dma(Wout[:, c, :], w_out_d[l, c * 128:(c + 1) * 128, :], eng="pool", max_dma_last_dim=4096)
            W1 = lsb("W1", [128, 2, 16, 256], BF16)
            for kv in range(2):
                for h2 in range(2):
                    P.dma(W1[h2 * 64:(h2 + 1) * 64, kv, :, :], w1_d[l, kv].r("(s t) d h -> t d s h", t=2)[h2], eng="pool",
                          max_dma_last_dim=1024)
            W2 = lsb("W2", [128, 2, 2, 64], BF16)
            P.dma(W2[:].r("p k c d -> p (k c) d"), w2_d[l].r("k (c p) d -> p (k c) d", p=128), eng="pool")
            b2r = lsb("b2r", [1, 2, 64], BF16)
            P.dma(b2r[:], b2_d[l:l + 1, :, :], eng="pool")
            posT = lsb("posT", [128, 2, 16], BF16)
            P.dma(posT[:], posT_d[:, l, :, :], eng="pool")
            b1T = lsb("b1T", [128, 2, 2])
            P.dma(b1T[:], b1T_d[:, l, :, :])
            cb1 = lsb("cb1", [128, 2, 2])
            for kv in range(2):
                for hc in range(2):
                    pz = nps()
                    for s in range(16):
                        P.mm(pz[:, 0:1], W1[:, kv, s, hc * 128:(hc + 1) * 128], posT[:, kv, s:s + 1],
                             start=(s == 0), stop=(s == 15))
                    P.tt(cb1[:, kv, hc:hc + 1], pz[:, 0:1], b1T[:, kv, hc:hc + 1], ALU.add)
            BDf = lsb("BDf", [128, 2, 2, 128])
            BD = lsb("BD", [128, 2, 2, 128], BF16)
            P.memset(BDf[:], 0.0)
            for c2 in range(2):
                for hh in range(2):
                    blk = c2 * 2 + hh
                    P.dma(BDf[hh * 64:(hh + 1) * 64, c2, 0, hh * 64:(hh + 1) * 64], rgwa_d[l, blk])
                    P.dma(BDf[hh * 64:(hh + 1) * 64, c2, 1, hh * 64:(hh + 1) * 64], rgwx_d[l, blk])
            P.cp(BD[:], BDf[:], eng="pool")

            chk(2)
            pes = contextlib.ExitStack()
            cur[0] = pes
            Win = lsb("Win", [128, 8, D_IN], BF16)
            for c in range(8):
                P.dma(Win[:, c, :], w_in_d[l, c * 128:(c + 1) * 128, :], eng="pool", max_dma_last_dim=4096)
            KsT = lsb("KsT", [128, Tn], BF16)
            KwT = lsb("KwT", [128, 8 * 128], BF16)
            Vs = lsb("Vs", [128, NT, 2, 65], BF16)
            Vw = lsb("Vw", [128, 8, 2, 65], BF16)
            KcT = lsb("KcT", [128, KTC * 128], BF16)
            Vc = lsb("Vc", [128, KTC, 2, 65], BF16)
            P.memset(KcT[:], 0.0)
            P.memset(Vc[:], 0.0)
            P.memset(Vs[:, :, :, 64:65], 1.0)
            P.memset(Vw[:, :, :, 64:65], 1.0)
            rawT = lsb("rawT", [128, 2, 2, 144], BF16)
            rkv = lsb("rkv", [128, 256], BF16)
            P.memset(rawT[:], 0.0)
            xcat = lsb("xcat", [128, 2, 131])
            P.memset(xcat[:], 0.0)
            hst = lsb("hst", [128, 2, 1])
            P.memset(hst[:], 0.0)
            S32 = lsb("S32", [128, 2, 64])
            Sbf = lsb("Sbf", [128, 2, 64], BF16)
            P.memset(S32[:], 0.0)
            P.memset(Sbf[:], 0.0)

            Ptok = lsb("Ptok", [128, 1560])
            rpt = lsb("rpt", [128, 2, 8])
            rct = lsb("rct", [8, 2, 8])
            ra = lsb("ra", [128, 8, 8])
            rb = lsb("rb", [128, 8, 8])
            rc = lsb("rc", [128, 8, 8])
            rd = lsb("rd", [128, 8, 8])
            QT = lsb("QT", [128, 512], BF16)
            Ffm = lsb("Ffm", [128, 10, 128])
            vtok = lsb("vtok", [128, 256], BF16)
            ET = lsb("ET", [128, 512], BF16)
            PT = lsb("PT", [128, 512], BF16)
            ET2 = lsb("ET2", [128, 512], BF16)
            PT2 = lsb("PT2", [128, 512], BF16)
            ET3 = lsb("ET3", [128, 512], BF16)
            PT3 = lsb("PT3", [128, 512], BF16)
            OTs = lsb("OTs", [65, 512])
            Otok = lsb("Otok", [128, 3, 8, 65])
            impT = lsb("impT", [128, 2, 128])
            zr = lsb("zr", [128, 512])
            score = lsb("score", [128, 128])
            sc2 = lsb("sc2", [128, 128])
            mx8 = lsb("mx8", [128, 8])
            thr = lsb("thr", [128, 1])
            selm = lsb("selm", [128, 2, 128])
            Xb = [lsb("Xb%d" % i, [128, 128]) for i in range(3)]
            gates = lsb("gates", [128, 24])
            coef = lsb("coef", [128, 3, 8])
            onsa = lsb("onsa", [128, 512])
            mixT = lsb("mixT", [128, 8, 128], BF16)
            hpre = lsb("hpre", [128, 16])
            hu = lsb("hu", [128, 16])
            hgl = lsb("hgl", [128, 2, 2, 16], BF16)
            cblk = lsb("cblk", [8, 2, 2, 64])
            cblkr = lsb("cblkr", [8, 2, 64])
            cv = lsb("cv", [8, 2, 65], BF16)
            P.memset(cv[:, :, 64:65], 1.0)
            xc = lsb("xc", [128, 2, 128])
            xcb = lsb("xcb", [128, 2, 128], BF16)
            rg_r = lsb("rg_r", [128, 128])
            rg_i = lsb("rg_i", [128, 128])
            rg_a = lsb("rg_a", [128, 128])
            rg_b = lsb("rg_b", [128, 128])
            rg_h = lsb("rg_h", [128, 2, 128])
            gl = lsb("gl", [128, 128])
            f2 = lsb("hgf2", [128, 2, 128])
            g2 = lsb("hgg2", [128, 2, 128])
            k2 = lsb("hgk2", [128, 2, 128])
            qs2 = lsb("qs2", [128, 2, 128])
            bc2 = lsb("bc2", [128, 2, 128])
            exA = lsb("exA", [128, 2, 128])
            exB = lsb("exB", [128, 2, 128])
            ebl2 = lsb("ebl2", [128, 2, 2])
            qE = lsb("qE", [128, 2, 128], BF16)
            qe = lsb("qe", [128, 2, 128], BF16)
            ke = lsb("ke", [128, 2, 128], BF16)
            kdT = lsb("kdT", [128, 2, 128])
            kdtok = lsb("kdtok", [128, 2, 128], BF16)
            Am = lsb("Am", [128, 2, 64], BF16)
            oT = lsb("oT", [128, 2, 128])
            osq2 = lsb("osq2", [128, 2, 128], BF16)

            for n in range(NT):
                t0 = n * 128
                if l == 0:
                    xtok = Ptok
                    P.dma(Ptok[:, 0:1024], x_in[t0:t0 + 128, :])
                    for c in range(8):
                        pz = nps()
                        P.tr(pz[:, 0:128], Ptok[:, c * 128:(c + 1) * 128], ident[:])
                        P.cp(xT[:, c, :], pz[:, 0:128], eng="act")
                else:
                    P.dma(xT[:], xs_d[:, t0:t0 + 128].r("(c p) t -> p c t", p=128))
                rmsnorm_T(xT[:], normT[:, l, 0, :], yT[:], 128, Ptok[:, 0:1024].r("p (c t) -> p c t", c=8))
                tm_chunks = [(0, 512), (512, 512), (1024, 280), (IN_OFF["hi"], 256)]
                dst = 0
                for (c0, cw) in tm_chunks:
                    pz = nps()
                    for c in range(8):
                        P.mm(pz[:, :cw], yT[:, c, :], Win[:, c, c0:c0 + cw], start=(c == 0), stop=(c == 7))
                    P.cp(Ptok[:, dst:dst + cw], pz[:, :cw], eng="act")
                    dst += cw
                fm_cols = [IN_OFF["rgx"], IN_OFF["rgx"] + 128, IN_OFF["rgg"], IN_OFF["rgg"] + 128,
                           IN_OFF["hq"], IN_OFF["hq"] + 128, IN_OFF["hf"], IN_OFF["hf"] + 128,
                           IN_OFF["hg"], IN_OFF["hg"] + 128]
                for i, c0 in enumerate(fm_cols):
                    pz = nps()
                    for c in range(8):
                        P.mm(pz[:, :128], Win[:, c, c0:c0 + 128], yT[:, c, :], start=(c == 0), stop=(c == 7))
                    P.cp(Ffm[:, i, :], pz[:, :128], eng=("act" if i % 2 else "dve"))
                chk(3)
                P.dma(rpt[:], cdr["ropeP"][:, :, n, :])
                P.dma(rct[:], cdr["ropeC"][:, :, n, :])
                cosv = rpt[:, 0, :]
                sinv = rpt[:, 1, :]
                for (h0, nh) in ((0, 8), (12, 2), (16, 2)):
                    src = Ptok[:, h0 * 64:(h0 + nh) * 64].r("p (h d) -> p h d", d=64)
                    cb = cosv.r("p (o e) -> p o e", o=1).b([128, nh, 8])
                    sbv = sinv.r("p (o e) -> p o e", o=1).b([128, nh, 8])
                    x1, x2 = src[:, :, 0:8], src[:, :, 8:16]
                    P.tt(ra[:, :nh, :], x1, cb, ALU.mult)
                    P.tt(rb[:, :nh, :], x2, sbv, ALU.mult)
                    P.tt(rc[:, :nh, :], x2, cb, ALU.mult)
                    P.tt(rd[:, :nh, :], x1, sbv, ALU.mult)
                    P.tt(src[:, :, 0:8], ra[:, :nh, :], rb[:, :nh, :], ALU.subtract)
                    P.tt(src[:, :, 8:16], rc[:, :nh, :], rd[:, :nh, :], ALU.add)
                chk(31)
                P.dma(o_cmp[l, t0:t0 + 128, :], Ptok[:, 512:768])
                P.dma(o_sel[l, t0:t0 + 128, :], Ptok[:, 768:1024])
                if n >= NT - WIN_T:
                    w0 = (n - (NT - WIN_T)) * 128
                    P.dma(o_win[l, w0:w0 + 128, :], Ptok[:, 1024:1280])
                chk(32)
                pz = nps()
                for j in range(4):
                    P.tr(pz[:, j * 128:(j + 1) * 128], Ptok[:, j * 128:(j + 1) * 128], ident[:])
                chk(321)
                P.cp(QT[:], pz[:], eng="act")
                chk(322)
                P.cp(rkv[:], Ptok[:, 512:768], eng="pool")
                pz = nps()
                for kv in range(2):
                    for g in range(2):
                        cs_ = rkv[:, kv * 128 + g * 64:kv * 128 + (g + 1) * 64]
                        for h2 in range(2):
                            P.mm(pz[h2 * 64:(h2 + 1) * 64, (kv * 2 + g) * 128:(kv * 2 + g + 1) * 128], cs_, ident_bf[:])
                P.cp(rawT[0:64, :, :, 16:144], pz[0:64, :].r("p (k g t) -> p k g t", k=2, g=2), eng="dve")
                P.cp(rawT[64:128, :, :, 15:143], pz[64:128, :].r("p (k g t) -> p k g t", k=2, g=2), eng="dve")
                pz = nps()
                P.tr(pz[:, 256:384], Ptok[:, 768:896], ident[:])
                P.tr(pz[:, 384:512], Ptok[:, 1024:1152], ident[:])
                P.cp(KsT[:, t0:t0 + 128], pz[:, 256:384], eng="dve")
                P.cp(KwT[:, (n % 8) * 128:(n % 8 + 1) * 128], pz[:, 384:512], eng="dve")
                chk(33)
                P.cp(Vs[:, n, :, 0:64], Ptok[:, 896:1024].r("p (g d) -> p g d", g=2), eng="pool")
                P.cp(Vw[:, n % 8, :, 0:64], Ptok[:, 1152:1280].r("p (g d) -> p g d", g=2), eng="pool")
                P.cp(vtok[:], Ptok[:, 1304:1560], eng="pool")
                chk(4)
                for kv in range(0 if "cpr" in SKIP else 2):
                    for hc in range(2):
                        pz = nps()
                        for g in range(2):
                            for s in range(16):
                                rhs = rawT[:, kv, g, 2 * s:2 * s + 113:16]
                                P.mm(pz[:, g * 8:(g + 1) * 8], W1[:, kv, s, hc * 128:(hc + 1) * 128],
                                     rhs, start=(s == 0), stop=(s == 15))
                        P.ts(hpre[:], pz[:, 0:16], cb1[:, kv, hc:hc + 1], ALU.add)
                        P.tt(hu[:], hpre[:], hpre[:], ALU.mult)
                        P.ts(hu[:], hu[:], 0.044715, ALU.mult, 1.0, ALU.add)
                        P.tt(hu[:], hu[:], hpre[:], ALU.mult)
                        P.act(hu[:], hu[:], AF.Sigmoid, scale=1.5957691216)
                        P.tt(hgl[:, kv, hc, :], hu[:], hpre[:], ALU.mult)
                chk(41)
                pz = nps()
                for kv in range(2):
                    for g in range(2):
                        o_ = pz[0:8, (kv * 2 + g) * 64:(kv * 2 + g + 1) * 64]
                        for hc in range(2):
                            P.mm(o_, hgl[:, kv, hc, g * 8:(g + 1) * 8], W2[:, kv, hc, :], start=(hc == 0), stop=False)
                        P.mm(o_, ones_bf[0:1, 0:8], b2r[0:1, kv, :], start=False, stop=True)
                P.cp(cblk[:].r("b k g d -> b (k g d)"), pz[0:8, 0:256], eng="act")
                chk(42)
                cC = rct[:, 0, :].r("p (o e) -> p o e", o=1).b([8, 2, 8])
                sC = rct[:, 1, :].r("p (o e) -> p o e", o=1).b([8, 2, 8])
                P.cp(cblkr[:], cblk[:, 0, :, :], eng="pool")
                x1, x2 = cblk[:, 0, :, 0:8], cblk[:, 0, :, 8:16]
                P.tt(ra[0:8, 0:2, :], x1, cC, ALU.mult)
                P.tt(rb[0:8, 0:2, :], x2, sC, ALU.mult)
                P.tt(cblkr[:, :, 0:8], ra[0:8, 0:2, :], rb[0:8, 0:2, :], ALU.subtract)
                P.tt(ra[0:8, 0:2, :], x2, cC, ALU.mult)
                P.tt(rb[0:8, 0:2, :], x1, sC, ALU.mult)
                P.tt(cblkr[:, :, 8:16], ra[0:8, 0:2, :], rb[0:8, 0:2, :], ALU.add)
                pz = nps()
                P.tr(pz[:, 0:8], cblkr[:].r("b g d -> b (g d)"), ident[0:8, 0:8])
                P.cp(KcT[:, n * 8:(n + 1) * 8], pz[:, 0:8], eng="act")
                chk(43)
                P.cp(cv[:, :, 0:64], cblk[:, 1, :, :], eng="pool")
                kt_n, po = (n * 8) // 128, (n * 8) % 128
                if n == 0:
                    P.dma(Vc[1:8, 0, :, :], cv[1:8, :, :])
                else:
                    P.dma(Vc[po:po + 8, kt_n, :, :], cv[:, :, :])
                chk(44)
                P.cp(rawT[0:64, :, :, 0:16], rawT[0:64, :, :, 128:144], eng="pool")
                P.cp(rawT[64:128, :, :, 0:15], rawT[64:128, :, :, 128:143], eng="pool")

                chk(5)
                def finish(br, g):
                    P.cp(OTs[:], pO[0:65, :], eng="act")
                    pz_ = pa[2]
                    for j in range(4):
                        P.tr(pz_[:, j * 65:(j + 1) * 65], OTs[:, j * 128:(j + 1) * 128], ident[0:65, 0:65])
                    P.cp(Otok[:, br, g * 4:(g + 1) * 4, :], pz_[:, 0:260].r("p (j e) -> p j e", e=65),
                         eng=("act" if g else "dve"))

                ETb = [ET, ET2, ET3]
                PTb = [PT, PT2, PT3]
                pSb = [pS[0], pS[1], pa[0]]
                pMb = [pM, pI, pa[1]]

                def attend(br, g, items):
                    nI = len(items)
                    NB = 3 if items[0].get("extra") is None else 2

                    def s1(i):
                        it = items[i]
                        P.mm(pSb[i % NB][:], it["kT"], QT[g * 64:(g + 1) * 64, :], start=True, stop=True)
                        if it.get("selkt") is not None:
                            kt_ = it["selkt"]
                            xb_ = Xb[i % 3]
                            P.cp(xb_[:].r("p (b o) -> p b o", o=64),
                                 selm[:, g, 2 * kt_:2 * kt_ + 2].r("p (b o) -> p b o", o=1).b([128, 2, 64]), eng="pool")
                            P.tr(pMb[i % NB][:, 0:128], xb_[:], ident[:])

                    def s2(i):
                        it = items[i]
                        et, pt = ETb[i % NB], PTb[i % NB]
                        P.act(et[:], pSb[i % NB][:], AF.Exp, scale=0.125)
                        src = et
                        mk = it.get("mk")
                        if it.get("selkt") is not None:
                            mk = pMb[i % NB][:, 0:128]
                            if it.get("causal"):
                                P.tt(sc2[:], mk, C["tri"][:, 0, :], ALU.mult)
                                mk = sc2[:]
                        if mk is not None:
                            P.tt(pt[:].r("p (j q) -> p j q", j=4), et[:].r("p (j q) -> p j q", j=4),
                                 mk.r("p (o q) -> p o q", o=1).b([128, 4, 128]), ALU.mult)
                            src = pt
                        if it.get("z0"):
                            P.memset(src[0:1, :], 0.0, eng="dve")
                        P.mm(pO[0:65, :], it["va"], src[:], start=(i == 0), stop=(i == nI - 1))
                        if it.get("extra") is not None:
                            it["extra"](src, i == 0, i == nI - 1)

                    for i in range(min(NB - 1, nI)):
                        s1(i)
                    for i in range(nI):
                        if i + NB - 1 < nI:
                            s1(i + NB - 1)
                        s2(i)
                    finish(br, g)

                nkt = n // 16 + 1
                for g in range(0 if "cmp" in SKIP else 2):
                    lst = []
                    for kt in range(nkt):
                        mk = C["cmpmask"][:, n % 16, :] if kt == nkt - 1 else None

                        def extra(src, first, last_, kt=kt):
                            P.mm(pI[:], C["smap"][:, kt, :], src[:], start=first, stop=last_)
                            P.mm(pM[:], ones_bf[:], src[:], start=first, stop=last_)
                        lst.append(dict(kT=KcT[g * 64:(g + 1) * 64, kt * 128:(kt + 1) * 128], va=Vc[:, kt, g, :], mk=mk,
                                        z0=(kt == 0), extra=extra))
                    attend(0, g, lst)
                    P.ts(zr[:], pM[:], 1e-30, ALU.max)
                    P.recip(zr[:], zr[:])
                    P.tt(zr[:], pI[:], zr[:], ALU.mult)
                    P.reduce(impT[:, g, :], zr[:].r("p (j q) -> p q j", j=4), ALU.add)
                for g in range(2):
                    pz = nps()
                    P.tr(pz[:, 0:128], impT[:, g, :], ident[:])
                    P.tt(score[:], pz[:, 0:128], C["selM"][:, 127 - 2 * n:255 - 2 * n], ALU.mult)
                    P.tt(score[:], score[:], C["selA"][:, 127 - 2 * n:255 - 2 * n], ALU.add)
                    P.memset(score[:, 0:1], 1e6, eng="dve")
                    P.max8(mx8[:], score[:])
                    P.match_replace(sc2[:], mx8[:], score[:], -1e30)
                    P.max8(mx8[:], sc2[:])
                    P.ts(thr[:], mx8[:, 7:8], -1e5, ALU.max)
                    P.ts(selm[:, g, :], score[:], thr[:, 0:1], ALU.is_ge)
                chk(6)
                for g in range(0 if "sel" in SKIP else 2):
                    lst = []
                    for kt in range(n + 1):
                        lst.append(dict(kT=KsT[g * 64:(g + 1) * 64, kt * 128:(kt + 1) * 128], va=Vs[:, kt, g, :], selkt=kt,
                                        causal=(kt == n)))
                    attend(1, g, lst)
                chk(7)
                for g in range(0 if "win" in SKIP else 2):
                    lst = []
                    for kt in range(max(0, n - 4), n + 1):
                        if kt == n:
                            mk = C["tri"][:, 0, :]
                        elif kt == n - 4:
                            mk = C["tri"][:, 1, :]
                        else:
                            mk = None
                        lst.append(dict(kT=KwT[g * 64:(g + 1) * 64, (kt % 8) * 128:(kt % 8 + 1) * 128], va=Vw[:, kt % 8, g, :], mk=mk))
                    attend(2, g, lst)
                P.act(gates[:], Ptok[:, 1280:1304], AF.Sigmoid)
                P.ts(coef[:], Otok[:, :, :, 64], 1e-30, ALU.max)
                P.recip(coef[:], coef[:])
                P.tt(coef[:], coef[:], gates[:].r("p (h b) -> p b h", b=3), ALU.mult)
                for h in range(8):
                    ov = onsa[:, h * 64:(h + 1) * 64]
                    P.ts(ov, Otok[:, 0, h, 0:64], coef[:, 0, h:h + 1], ALU.mult)
                    P.stt(ov, Otok[:, 1, h, 0:64], coef[:, 1, h:h + 1], ov, ALU.mult, ALU.add)
                    P.stt(ov, Otok[:, 2, h, 0:64], coef[:, 2, h:h + 1], ov, ALU.mult, ALU.add)
                pz = nps()
                for j in range(4):
                    P.tr(pz[:, j * 128:(j + 1) * 128], onsa[:, j * 128:(j + 1) * 128], ident[:])
                P.cp(mixT[:, 0:4, :], pz[:].r("p (c t) -> p c t", c=4), eng="act")

                chk(8)
                def rg_chain():
                    for c2 in range(0 if "rg" in SKIP else 2):
                        pr = rgp[:, l, c2, :]
                        P.cp(xcat[:, c2, 3:131], Ffm[:, c2, :], eng="pool"); yield
                        P.ts(xc[:, c2, :], xcat[:, c2, 0:128], pr[:, 0:1], ALU.mult, pr[:, 4:5], ALU.add); yield
                        for k in range(1, 4):
                            P.stt(xc[:, c2, :], xcat[:, c2, k:k + 128], pr[:, k:k + 1], xc[:, c2, :], ALU.mult, ALU.add); yield
                        P.cp(xcb[:, c2, :], xc[:, c2, :], eng="pool"); yield
                        pz = nps()
                        P.mm(pz[:, 0:128], BD[:, c2, 0, :], xcb[:, c2, :])
                        P.act(rg_r[:], pz[:, 0:128], AF.Sigmoid, bias=pr[:, 5:6]); yield
                        pz = nps()
                        P.mm(pz[:, 0:128], BD[:, c2, 1, :], xcb[:, c2, :])
                        P.act(rg_i[:], pz[:, 0:128], AF.Sigmoid, bias=pr[:, 6:7]); yield
                        P.act(rg_a[:], rg_r[:], AF.Exp, scale=rgc[:, l, c2, :]); yield
                        P.act(rg_b[:], rg_r[:], AF.Exp, scale=rgc2[:, l, c2, :]); yield
                        P.ts(rg_b[:], rg_b[:], -1.0, ALU.mult, 1.0, ALU.add); yield
                        P.ts(rg_b[:], rg_b[:], 0.0, ALU.max); yield
                        P.act(rg_b[:], rg_b[:], AF.Sqrt); yield
                        P.tt(rg_i[:], rg_i[:], xc[:, c2, :], ALU.mult); yield
                        P.tt(rg_b[:], rg_b[:], rg_i[:], ALU.mult); yield
                        P.scan(rg_h[:, c2, :], rg_a[:], rg_b[:], hst[:, c2, :], ALU.mult, ALU.add); yield
                        P.cp(hst[:, c2, :], rg_h[:, c2, 127:128], eng="pool"); yield
                        gx = Ffm[:, 2 + c2, :]
                        P.tt(gl[:], gx, gx, ALU.mult); yield
                        P.ts(gl[:], gl[:], 0.044715, ALU.mult, 1.0, ALU.add); yield
                        P.tt(gl[:], gl[:], gx, ALU.mult); yield
                        P.act(gl[:], gl[:], AF.Sigmoid, scale=1.5957691216); yield
                        P.tt(gl[:], gl[:], gx, ALU.mult); yield
                        P.tt(mixT[:, 4 + c2, :], gl[:], rg_h[:, c2, :], ALU.mult); yield
                        P.cp(xcat[:, c2, 0:3], xcat[:, c2, 128:131], eng="pool"); yield

                def hg_chain():
                    if "hg" in SKIP:
                        return
                    hq = Ffm[:, 4:6, :]
                    hf = Ffm[:, 6:8, :]
                    hgv = Ffm[:, 8:10, :]
                    P.act(f2[:], hf, AF.Sigmoid); yield
                    P.tt(f2[:], f2[:], oml[:, :, l:l + 1].b([128, 2, 128]), ALU.mult); yield
                    P.tt(f2[:], f2[:], lb[:, :, l:l + 1].b([128, 2, 128]), ALU.add); yield
                    P.act(g2[:], f2[:], AF.Ln); yield
                    P.ts(k2[:], f2[:], -1.0, ALU.mult, 1.0, ALU.add); yield
                    P.act(qs2[:], hq, AF.Sigmoid); yield
                    P.tt(qs2[:], qs2[:], hq, ALU.mult); yield
                    for c2 in range(2):
                        P.scan(bc2[:, c2, :], ones_f[:, :], g2[:, c2, :], 0.0, ALU.mult, ALU.add); yield
                    bcv = bc2[:].r("p c (h t) -> p c h t", h=2)
                    d1 = g2[:].r("p c (h t) -> p c h t", h=2)
                    d2 = f2[:].r("p c (h t) -> p c h t", h=2)
                    P.tt(d1, bcv, bcv[:, :, :, 31:32].b([128, 2, 2, 64]), ALU.subtract); yield
                    P.tt(d2, bcv[:, :, :, 63:64].b([128, 2, 2, 64]), bcv, ALU.subtract); yield
                    P.act(ebl2[:], bcv[:, :, :, 63], AF.Exp); yield
                    P.act(exA[:], bc2[:], AF.Exp); yield
                    P.tt(qE[:], qs2[:], exA[:], ALU.mult); yield
                    P.act(exB[:], g2[:], AF.Exp); yield
                    P.tt(qe[:], qs2[:], exB[:], ALU.mult); yield
                    P.act(exA[:], g2[:], AF.Exp, scale=-1.0); yield
                    P.tt(ke[:], k2[:], exA[:], ALU.mult); yield
                    P.act(exB[:], f2[:], AF.Exp); yield
                    P.tt(kdT[:], k2[:], exB[:], ALU.mult); yield
                    for c2 in range(2):
                        pz = nps()
                        P.tr(pz[:, 0:128], kdT[:, c2, :], ident[:])
                        P.cp(kdtok[:, c2, :], pz[:, 0:128], eng="act"); yield
                    for c2 in range(2):
                        for ch in range(2):
                            sl = slice(ch * 64, (ch + 1) * 64)
                            pz = nps()
                            for hh in range(2):
                                hp = slice(hh * 64, (hh + 1) * 64)
                                P.mm(pz[sl, hh * 64:(hh + 1) * 64], ke[hp, c2, sl], qe[hp, c2, sl])
                            P.tt(Am[sl, :, :], pz[sl, 0:128].r("p (h t) -> p h t", h=2),
                                 C["tri64"][sl, :].r("p (o t) -> p o t", o=1).b([64, 2, 64]), ALU.mult); yield
                            pz2 = nps()
                            for hh in range(2):
                                hp = slice(hh * 64, (hh + 1) * 64)
                                h = c2 * 2 + hh
                                P.mm(pz2[hp, 0:64], vtok[sl, h * 64:(h + 1) * 64], Am[sl, hh, :], start=True, stop=False)
                                P.mm(pz2[hp, 0:64], Sbf[hp, c2, :], qE[hp, c2, sl], start=False, stop=True)
                                P.mm(pz2[hp, 64:128], kdtok[sl, c2, hp], vtok[sl, h * 64:(h + 1) * 64])
                            P.cp(oT[:, c2, sl], pz2[:, 0:64], eng="act"); yield
                            P.stt(S32[:, c2, :], S32[:, c2, :], ebl2[:, c2, ch:ch + 1], pz2[:, 64:128], ALU.mult, ALU.add); yield
                            P.cp(Sbf[:, c2, :], S32[:, c2, :], eng="pool"); yield
                    P.act(osq2[:], oT[:], AF.Square); yield
                    pz = nps()
                    for c2 in range(2):
                        P.mm(pz[:, c2 * 128:(c2 + 1) * 128], C["blockones"][:], osq2[:, c2, :])
                    P.act(exB[:], pz[:, 0:256].r("p (c t) -> p c t", c=2), AF.Sqrt, bias=EPS, scale=1.0 / 64); yield
                    P.recip(exB[:], exB[:]); yield
                    P.tt(exB[:], exB[:], oT[:], ALU.mult); yield
                    P.tt(exB[:], exB[:], hgp[:, :, 1, l:l + 1].b([128, 2, 128]), ALU.mult); yield
                    P.act(exA[:], hgv, AF.Sigmoid); yield
                    P.tt(exA[:], exA[:], hgv, ALU.mult); yield
                    P.tt(mixT[:, 6:8, :], exB[:], exA[:], ALU.mult); yield

                gens = [rg_chain(), hg_chain()]
                while gens:
                    for gen in list(gens):
                        try:
                            next(gen)
                        except StopIteration:
                            gens.remove(gen)
                if n == NT - 1:
                    for c2 in range(2):
                        P.dma(o_rgh[l, c2 * 128:(c2 + 1) * 128].r("(p o) -> p o", o=1), hst[:, c2, :])
                    pz = nps()
                    for c in range(8):
                        P.mm(pz[:, 0:256], yT[:, c, :], Win[:, c, IN_OFF["rgx"]:IN_OFF["rgx"] + 256], start=(c == 0),
                             stop=(c == 7))
                    P.cp(score[:, 0:128], pz[:, 0:128], eng="act")
                    P.cp(sc2[:, 0:128], pz[:, 128:256], eng="act")
                    P.dma(o_rgc[l, :, 0:128], score[125:128, 0:128])
                    P.dma(o_rgc[l, :, 128:256], sc2[125:128, 0:128])

                chk(9)
                if n == NT - 1:
                    for c2 in range(2):
                        P.dma(o_hgs[l, c2 * 2:c2 * 2 + 2].r("h k v -> (h k) v"), S32[:, c2, :])

                chk(10)
                for dc in range(8):
                    pz = nps()
                    for k in range(8):
                        P.mm(pz[:, 0:128], Wout[:, k, dc * 128:(dc + 1) * 128], mixT[:, k, :], start=(k == 0), stop=(k == 7))
                    P.tt(xT[:, dc, :], xT[:, dc, :], pz[:, 0:128], ALU.add)
                P.dma(xs_d[:, t0:t0 + 128].r("(c p) t -> p c t", p=128), xT[:])
                chk(11)
            pes.close()
            P.barrier()
            chk(12)
            if NS:
                des = contextlib.ExitStack()
                cur[0] = des
                decode_mixer(l, lsb, cur, Wout, W1, W2, b2r, cb1, BD)
                des.close()

        chk(13)
        with contextlib.ExitStack() as les:
            def lsb(name, shape, dt=F32):
                return T("s_" + name + "_f%d" % l, les.enter_context(nc.sbuf_tensor("s_" + name + "_f%d" % l, list(shape), dt)))
            P.barrier()
            Wup = lsb("Wup", [128, 8, 4 * D], BF16)
            Wdn = lsb("Wdn", [128, 32, D], BF16)
            for c in range(8):
                P.dma(Wup[:, c, :], w_up_d[l, c * 128:(c + 1) * 128, :], eng="pool", max_dma_last_dim=4096)
            for c in range(32):
                P.dma(Wdn[:, c, :], w_dn_d[l, c * 128:(c + 1) * 128, :], eng="pool", max_dma_last_dim=4096)
            FB = 256
            xB = lsb("xB", [128, 8, FB])
            yB = lsb("yB", [128, 8, FB], BF16)
            rsB = lsb("rsB", [128, FB])
            HT = lsb("HT", [128, 32, FB], BF16)
            tB = V(HT.t[:, 0:16, :].rearrange("p a b -> p (a b)").bitcast(F32).rearrange("p (c t) -> p c t", c=8), HT[:].key)
            sqB = V(HT.t[:, 16:24, :], HT[:].key)
            hr = lsb("hr", [128, FB])
            ytok = lsb("ytok", [128, D])
            for blk in range(Tn // FB):
                t0 = blk * FB
                P.dma(xB[:], xs_d[:, t0:t0 + FB].r("(c p) t -> p c t", p=128))

                def norm(gv, outv):
                    P.act(sqB, xB[:], AF.Square)
                    pz = nps()
                    for c in range(8):
                        P.mm(pz[:, :FB], ones_bf[:], sqB[:, c, :], start=(c == 0), stop=(c == 7))
                    P.act(rsB[:], pz[:, :FB], AF.Sqrt, bias=EPS, scale=1.0 / D)
                    P.recip(rsB[:], rsB[:])
                    P.tt(tB, xB[:], rsB[:].r("p (o t) -> p o t", o=1).b([128, 8, FB]), ALU.mult)
                    P.tt(outv, tB, gv.r("p (c o) -> p c o", o=1).b([128, 8, FB]), ALU.mult)
                norm(normT[:, l, 1, :], yB[:])
                for f in range(32):
                    pz = nps()
                    for c in range(8):
                        P.mm(pz[:, :FB], Wup[:, c, f * 128:(f + 1) * 128], yB[:, c, :], start=(c == 0), stop=(c == 7))
                    P.act(hr[:], pz[:, :FB], AF.Relu)
                    P.tt(HT[:, f, :], hr[:], hr[:], ALU.mult, eng="pool")
                for dc in range(8):
                    pz = nps()
                    for f in range(32):
                        P.mm(pz[:, :FB], Wdn[:, f, dc * 128:(dc + 1) * 128], HT[:, f, :], start=(f == 0), stop=(f == 31))
                    P.tt(xB[:, dc, :], xB[:, dc, :], pz[:, :FB], ALU.add)
                if l < L - 1:
                    P.dma(xs_d[:, t0:t0 + FB].r("(c p) t -> p c t", p=128), xB[:])
                else:
                    norm(fnormT[:], tB)
                    for tt_ in range(FB // 128):
                        for c in range(8):
                            pz = nps()
                            P.tr(pz[:, 0:128], tB[:, c, tt_ * 128:(tt_ + 1) * 128], ident[:])
                            P.cp(ytok[:, c * 128:(c + 1) * 128], pz[:, 0:128], eng=("act" if c % 2 else "dve"))
                        P.dma(y_out[t0 + tt_ * 128:t0 + (tt_ + 1) * 128, :], ytok[:])
            if NS:
                decode_ffn(l, lsb, Wup, Wdn)


FM_COLS = [IN_OFF["rgx"], IN_OFF["rgx"] + 128, IN_OFF["rgg"], IN_OFF["rgg"] + 128,
           IN_OFF["hq"], IN_OFF["hq"] + 128, IN_OFF["hf"], IN_OFF["hf"] + 128,
           IN_OFF["hg"], IN_OFF["hg"] + 128]


def rope_tok(src_t, dst_t, cosv, sinv, ra, rb, npart):
    for (h0, nh) in ((0, 8), (12, 2), (16, 2)):
        src = src_t[:, h0 * 64:(h0 + nh) * 64].r("p (h d) -> p h d", d=64)
        dstv = dst_t[:, h0 * 64:(h0 + nh) * 64].r("p (h d) -> p h d", d=64)
        cb = cosv.r("p (o e) -> p o e", o=1).b([npart, nh, 8])
        sbv = sinv.r("p (o e) -> p o e", o=1).b([npart, nh, 8])
        x1, x2 = src[:, :, 0:8], src[:, :, 8:16]
        P.tt(ra[:, :nh, :], x1, cb, ALU.mult)
        P.tt(rb[:, :nh, :], x2, sbv, ALU.mult)
        P.tt(dstv[:, :, 0:8], ra[:, :nh, :], rb[:, :nh, :], ALU.subtract)
        P.tt(ra[:, :nh, :], x2, cb, ALU.mult)
        P.tt(rb[:, :nh, :], x1, sbv, ALU.mult)
        P.tt(dstv[:, :, 8:16], ra[:, :nh, :], rb[:, :nh, :], ALU.add)


def gelu_tanh(out, x, tmp):
    P.tt(tmp, x, x, ALU.mult)
    P.ts(tmp, tmp, 0.044715, ALU.mult, 1.0, ALU.add)
    P.tt(tmp, tmp, x, ALU.mult)
    P.act(tmp, tmp, AF.Sigmoid, scale=1.5957691216)
    P.tt(out, tmp, x, ALU.mult)


def decode_mixer(l, lsb, cur, Wout, W1, W2, b2r, cb1, BD):
    NCB = PAST // 16 - 1
    NSB = PAST // 64 + 1
    WT = WB // 128
    ysT = lsb("ysT", [128, 8, NS], BF16)
    ntmp = lsb("ntmp", [128, 8, NS])
    rmsnorm_T(xsT[:], normT[:, l, 0, :], ysT[:], NS, ntmp[:])
    Ps = lsb("Ps", [NS, D_IN])
    Fs = lsb("Fs", [128, 10, NS])
    outer = cur[0]
    wes = contextlib.ExitStack()
    cur[0] = wes
    Win = lsb("WinD", [128, 8, D_IN], BF16)
    for c in range(8):
        P.dma(Win[:, c, :], w_in_d[l, c * 128:(c + 1) * 128, :], eng="pool", max_dma_last_dim=4096)
    for c0 in range(0, D_IN, 512):
        cw = min(512, D_IN - c0)
        pz = nps()
        for c in range(8):
            P.mm(pz[0:NS, :cw], ysT[:, c, :], Win[:, c, c0:c0 + cw], start=(c == 0), stop=(c == 7))
        P.cp(Ps[:, c0:c0 + cw], pz[0:NS, :cw], eng="act")
    for i, c0 in enumerate(FM_COLS):
        pz = nps()
        for c in range(8):
            P.mm(pz[:, :NS], Win[:, c, c0:c0 + 128], ysT[:, c, :], start=(c == 0), stop=(c == 7))
        P.cp(Fs[:, i, :], pz[:, :NS], eng="dve")
    wes.close()
    cur[0] = outer
    P.barrier()
    Rs = lsb("Rs", [NS, 1280])
    ras = lsb("ras", [NS, 8, 8])
    rbs = lsb("rbs", [NS, 8, 8])
    P.cp(Rs[:], Ps[:, 0:1280], eng="pool")
    rope_tok(Ps, Rs, C["ropeS"][:, 0, :], C["ropeS"][:, 1, :], ras, rbs, NS)
    P.dma(os_cmp[l], Rs[:, 512:768])
    P.dma(os_sel[l], Rs[:, 768:1024])
    P.dma(os_win.s("a")[l, :, 0:WB - 1, :], cwin_d[l, :, 1:WB, :])
    P.dma(os_win.s("b")[l, :, WB - 1, :], Rs[:, 1024:1280])
    QsT = lsb("QsT", [128, 4, NS], BF16)
    pz = nps()
    for j in range(4):
        P.tr(pz[:, j * NS:(j + 1) * NS], Rs[:, j * 128:(j + 1) * 128], ident[0:NS, 0:NS])
    P.cp(QsT[:].r("p j s -> p (j s)"), pz[:, 0:4 * NS], eng="act")

    idxl = lsb("idxl", [128, NS * NPG], I32)
    P.ts(idxl[:], idxt[:], float(l * NPOOL * 128), ALU.add)
    OTall = lsb("OTall", [65, 3, 2, 4, NS])
    impTall = lsb("impTall", [40, 2 * NS])
    P.memset(impTall[:], 0.0)
    pg = [lsb("pg%d" % i, [128, NPG, 256]) for i in range(2)]
    rawS = lsb("rawS", [128, 2, 2, PAST], BF16)
    pgbf = lsb("pgbf", [128, 256], BF16)
    hp_ = lsb("hp_", [128, 2 * NCB])
    hu_ = lsb("hu_", [128, 2 * NCB])
    hglS = lsb("hglS", [128, 2, 2, 2 * NCB], BF16)
    cblkS = lsb("cblkS", [NCB, 2, 2, 64])
    cblkrS = lsb("cblkrS", [NCB, 2, 64])
    rcs = lsb("rcs", [NCB, 2, 8])
    rds = lsb("rds", [NCB, 2, 8])
    KcS = lsb("KcS", [128, NCB], BF16)
    cvS = lsb("cvS", [NCB, 2, 65], BF16)
    P.memset(cvS[:, :, 64:65], 1.0)
    ETs = lsb("ETs", [128, 4 * max(NPG, 4)], BF16)
    PTs = lsb("PTs", [128, 4 * max(NPG, 4)], BF16)
    zs = lsb("zs", [40, 4])
    zq = lsb("zq", [40, 4])
    def fetch1(s_):
        for k in range(NPG):
            P.idma(pg[s_ % 2][:, k, :], ccmp_d[:], idxl[:, s_ * NPG + k:s_ * NPG + k + 1])

    fetch1(0)
    for s in range(NS):
        pgb = pg[s % 2]
        if s + 1 < NS:
            fetch1(s + 1)
        for k in range(NPG):
            P.cp(pgbf[:], pgb[:, k, :], eng="pool")
            pz = nps()
            for kv in range(2):
                for g in range(2):
                    cs_ = pgbf[:, kv * 128 + g * 64:kv * 128 + (g + 1) * 64]
                    for h2 in range(2):
                        P.mm(pz[h2 * 64:(h2 + 1) * 64, (kv * 2 + g) * 128:(kv * 2 + g + 1) * 128], cs_, ident_bf[:])
            P.cp(rawS[0:64, :, :, k * 128:(k + 1) * 128], pz[0:64, :].r("p (k g t) -> p k g t", k=2, g=2), eng="act")
            if k == 0:
                P.cp(rawS[64:128, :, :, 0:127], pz[64:128, :].r("p (k g t) -> p k g t", k=2, g=2)[:, :, :, 1:128], eng="dve")
            else:
                P.cp(rawS[64:128, :, :, k * 128 - 1:(k + 1) * 128 - 1], pz[64:128, :].r("p (k g t) -> p k g t", k=2, g=2), eng="dve")
        for kv in range(2):
            for hc in range(2):
                pz = nps()
                for g in range(2):
                    for s16 in range(16):
                        rhs = rawS[:, kv, g, 2 * s16:2 * s16 + 16 * (NCB - 1) + 1:16]
                        P.mm(pz[:, g * NCB:(g + 1) * NCB], W1[:, kv, s16, hc * 128:(hc + 1) * 128],
                             rhs, start=(s16 == 0), stop=(s16 == 15))
                P.ts(hp_[:], pz[:, 0:2 * NCB], cb1[:, kv, hc:hc + 1], ALU.add)
                gelu_tanh(hglS[:, kv, hc, :], hp_[:], hu_[:])
        pz = nps()
        for kv in range(2):
            for g in range(2):
                o_ = pz[0:NCB, (kv * 2 + g) * 64:(kv * 2 + g + 1) * 64]
                for hc in range(2):
                    P.mm(o_, hglS[:, kv, hc, g * NCB:(g + 1) * NCB], W2[:, kv, hc, :], start=(hc == 0), stop=False)
                P.mm(o_, ones_bf[0:1, 0:NCB], b2r[0:1, kv, :], start=False, stop=True)
        P.cp(cblkS[:].r("b k g d -> b (k g d)"), pz[0:NCB, 0:256], eng="act")
        cC = C["ropeCS"][:, 0, :].r("p (o e) -> p o e", o=1).b([NCB, 2, 8])
        sC = C["ropeCS"][:, 1, :].r("p (o e) -> p o e", o=1).b([NCB, 2, 8])
        P.cp(cblkrS[:], cblkS[:, 0, :, :], eng="pool")
        x1, x2 = cblkS[:, 0, :, 0:8], cblkS[:, 0, :, 8:16]
        P.tt(rcs[:], x1, cC, ALU.mult)
        P.tt(rds[:], x2, sC, ALU.mult)
        P.tt(cblkrS[:, :, 0:8], rcs[:], rds[:], ALU.subtract)
        P.tt(rcs[:], x2, cC, ALU.mult)
        P.tt(rds[:], x1, sC, ALU.mult)
        P.tt(cblkrS[:, :, 8:16], rcs[:], rds[:], ALU.add)
        pz = nps()
        P.tr(pz[:, 0:NCB], cblkrS[:].r("b g d -> b (g d)"), ident[0:NCB, 0:NCB])
        P.cp(KcS[:], pz[:, 0:NCB], eng="act")
        P.cp(cvS[:, :, 0:64], cblkS[:, 1, :, :], eng="pool")
        for g in range(2):
            P.mm(pS[0][0:NCB, 0:4], KcS[g * 64:(g + 1) * 64, :], QsT[g * 64:(g + 1) * 64, :, s])
            P.act(ETs[0:NCB, 0:4], pS[0][0:NCB, 0:4], AF.Exp, scale=0.125)
            P.mm(pO[0:65, 0:4], cvS[:, g, :], ETs[0:NCB, 0:4])
            P.mm(pI[0:NSB, 0:4], C["smapS"][:, :], ETs[0:NCB, 0:4])
            P.mm(pM[0:NSB, 0:4], ones_bf[0:NCB, 0:NSB], ETs[0:NCB, 0:4])
            P.cp(OTall[:, 0, g, :, s], pO[0:65, 0:4], eng="act")
            P.ts(zs[0:NSB, :], pM[0:NSB, 0:4], 1e-30, ALU.max)
            P.recip(zs[0:NSB, :], zs[0:NSB, :])
            P.tt(zq[0:NSB, :], pI[0:NSB, 0:4], zs[0:NSB, :], ALU.mult)
            P.reduce(impTall[0:NSB, 2 * s + g:2 * s + g + 1], zq[0:NSB, :], ALU.add)
    scoreS = lsb("scoreS", [2 * NS, 40])
    sc2S = lsb("sc2S", [2 * NS, 40])
    mx8S = lsb("mx8S", [2 * NS, 8])
    thrS = lsb("thrS", [2 * NS, 1])
    selmS = lsb("selmS", [2 * NS, 40])
    selmST = lsb("selmST", [40, 2 * NS], BF16)
    pz = nps()
    P.tr(pz[0:2 * NS, 0:40], impTall[:], ident[0:40, 0:40])
    P.tt(scoreS[:], pz[0:2 * NS, 0:40], C["selMS"][:], ALU.mult)
    P.tt(scoreS[:], scoreS[:], C["selAS"][:], ALU.add)
    P.max8(mx8S[:], scoreS[:])
    P.match_replace(sc2S[:], mx8S[:], scoreS[:], -1e30)
    P.max8(mx8S[:], sc2S[:])
    P.ts(thrS[:], mx8S[:, 7:8], -1e5, ALU.max)
    P.ts(selmS[:], scoreS[:], thrS[:, 0:1], ALU.is_ge)
    pz = nps()
    P.tr(pz[0:40, 0:2 * NS], selmS[:], ident[0:2 * NS, 0:2 * NS])
    P.cp(selmST[:], pz[0:40, 0:2 * NS], eng="act")
    KsS = lsb("KsS", [128, PAST], BF16)
    VsS = lsb("VsS", [128, NPG, 2, 65], BF16)
    P.memset(VsS[:, :, :, 64:65], 1.0)
    wbuf = lsb("wbuf", [128, WT, 256])
    KwS = lsb("KwS", [128, WB], BF16)
    VwS = lsb("VwS", [128, WT, 2, 65], BF16)
    P.memset(VwS[:, :, :, 64:65], 1.0)
    def fetch2(s_):
        for k in range(NPG):
            P.idma(pg[s_ % 2][:, k, :], csel_d[:], idxl[:, s_ * NPG + k:s_ * NPG + k + 1])

    fetch2(0)
    for s in range(NS):
        pgb = pg[s % 2]
        if s + 1 < NS:
            fetch2(s + 1)
        P.dma(wbuf[:], cwin_d[l, s].r("(t p) c -> p t c", p=128))
        for k in range(NPG):
            pz = nps()
            P.tr(pz[:, 0:128], pgb[:, k, 0:128], ident[:])
            P.cp(KsS[:, k * 128:(k + 1) * 128], pz[:, 0:128], eng=("act" if k % 2 else "dve"))
            P.cp(VsS[:, k, :, 0:64], pgb[:, k, 128:256].r("p (g d) -> p g d", g=2), eng="pool")
        for t in range(WT):
            pz = nps()
            P.tr(pz[:, 0:128], wbuf[:, t, 0:128], ident[:])
            P.cp(KwS[:, t * 128:(t + 1) * 128], pz[:, 0:128], eng=("act" if t % 2 else "dve"))
            P.cp(VwS[:, t, :, 0:64], wbuf[:, t, 128:256].r("p (g d) -> p g d", g=2), eng="pool")
        for g in range(2):
            sp = pS[g]
            for k in range(NPG):
                P.mm(sp[:, k * 4:(k + 1) * 4], KsS[g * 64:(g + 1) * 64, k * 128:(k + 1) * 128], QsT[g * 64:(g + 1) * 64, :, s])
            for k in range(NPG):
                P.mm(pM[:, k:k + 1], C["DS"][:, k * 128:(k + 1) * 128], selmST[:, 2 * s + g:2 * s + g + 1])
            P.act(ETs[:, 0:4 * NPG], sp[:, 0:4 * NPG], AF.Exp, scale=0.125)
            P.tt(PTs[:, 0:4 * NPG].r("p (k h) -> p k h", h=4), ETs[:, 0:4 * NPG].r("p (k h) -> p k h", h=4),
                 pM[:, 0:NPG].r("p (k o) -> p k o", o=1).b([128, NPG, 4]), ALU.mult)
            for k in range(NPG):
                P.mm(pO[0:65, 0:4], VsS[:, k, g, :], PTs[:, k * 4:(k + 1) * 4], start=(k == 0), stop=(k == NPG - 1))
            P.cp(OTall[:, 1, g, :, s], pO[0:65, 0:4], eng="act")
            for t in range(WT):
                P.mm(sp[:, t * 4:(t + 1) * 4], KwS[g * 64:(g + 1) * 64, t * 128:(t + 1) * 128], QsT[g * 64:(g + 1) * 64, :, s])
            P.act(ETs[:, 0:4 * WT], sp[:, 0:4 * WT], AF.Exp, scale=0.125)
            if WB == 512:
                P.memset(ETs[0:1, 0:4], 0.0, eng="dve")
            for t in range(WT):
                P.mm(pO[0:65, 0:4], VwS[:, t, g, :], ETs[:, t * 4:(t + 1) * 4], start=(t == 0), stop=(t == WT - 1))
            P.cp(OTall[:, 2, g, :, s], pO[0:65, 0:4], eng="act")
    OtokS = lsb("OtokS", [NS, 3, 8, 65])
    for br in range(3):
        for g in range(2):
            pz = nps()
            for j in range(4):
                P.tr(pz[0:NS, j * 65:(j + 1) * 65], OTall[:, br, g, j, :], ident[0:65, 0:65])
            P.cp(OtokS[:, br, g * 4:(g + 1) * 4, :], pz[0:NS, 0:260].r("p (j e) -> p j e", e=65), eng="act")
    prod = lsb("prod", [NS, 4, 2, 64])
    dots = lsb("dots", [NS, 4, 2])
    enew = lsb("enew", [NS, 4, 2])
    tmpo = lsb("tmpo", [NS, 2, 4, 64])
    qv = Rs[:, 0:512].r("p (j g d) -> p j g d", j=4, g=2)
    for br, kc0, vc0 in ((1, 768, 896), (2, 1024, 1152)):
        kn = Rs[:, kc0:kc0 + 128].r("p (o g d) -> p o g d", o=1, g=2).b([NS, 4, 2, 64])
        P.tt(prod[:], qv, kn, ALU.mult)
        P.reduce(dots[:], prod[:], ALU.add)
        P.act(enew[:], dots[:], AF.Exp, scale=0.125)
        ev = enew[:].r("p j g -> p g j")
        vn = Rs[:, vc0:vc0 + 128].r("p (g o d) -> p g o d", g=2, o=1).b([NS, 2, 4, 64])
        P.tt(tmpo[:], vn, ev.r("p g (j o) -> p g j o", o=1).b([NS, 2, 4, 64]), ALU.mult)
        ob = OtokS[:, br, :, 0:64].r("p (g j) d -> p g j d", g=2)
        P.tt(ob, ob, tmpo[:], ALU.add)
        zb = OtokS[:, br, :, 64].r("p (g j) -> p g j", g=2)
        P.tt(zb, zb, ev, ALU.add)
    gatesS = lsb("gatesS", [NS, 24])
    coefS = lsb("coefS", [NS, 3, 8])
    onsaS = lsb("onsaS", [NS, 512])
    mixTs = lsb("mixTs", [128, 8, NS], BF16)
    P.act(gatesS[:], Ps[:, 1280:1304], AF.Sigmoid)
    P.ts(coefS[:], OtokS[:, :, :, 64], 1e-30, ALU.max)
    P.recip(coefS[:], coefS[:])
    P.tt(coefS[:], coefS[:], gatesS[:].r("p (h b) -> p b h", b=3), ALU.mult)
    for h in range(8):
        ov = onsaS[:, h * 64:(h + 1) * 64]
        P.ts(ov, OtokS[:, 0, h, 0:64], coefS[:, 0, h:h + 1], ALU.mult)
        P.stt(ov, OtokS[:, 1, h, 0:64], coefS[:, 1, h:h + 1], ov, ALU.mult, ALU.add)
        P.stt(ov, OtokS[:, 2, h, 0:64], coefS[:, 2, h:h + 1], ov, ALU.mult, ALU.add)
    pz = nps()
    for j in range(4):
        P.tr(pz[:, j * NS:(j + 1) * NS], onsaS[:, j * 128:(j + 1) * 128], ident[0:NS, 0:NS])
    P.cp(mixTs[:, 0:4, :].r("p c s -> p (c s)"), pz[:, 0:4 * NS], eng="act")
    rgct = lsb("rgct", [NS, 3, 256])
    rght = lsb("rght", [NS, 256])
    P.dma(rgct[:], srgc_d[l])
    P.dma(rght[:], srgh_d[l])
    P.dma(os_rgc.s("a")[l, :, 0:2, :], srgc_d[l, :, 1:3, :])
    P.dma(os_rgc.s("b")[l, :, 2, :], Ps[:, IN_OFF["rgx"]:IN_OFF["rgx"] + 256])
    xcs = lsb("xcs", [128, 2, 4, NS])
    h0T = lsb("h0T", [128, 2, NS])
    xcd = lsb("xcd", [128, NS])
    xcdb = lsb("xcdb", [128, NS], BF16)
    r_ = lsb("r_", [128, NS])
    i_ = lsb("i_", [128, NS])
    a_ = lsb("a_", [128, NS])
    b_ = lsb("b_", [128, NS])
    hT_ = lsb("hT_", [128, 2, NS])
    g1 = lsb("g1", [128, NS])
    g2 = lsb("g2", [128, NS])
    htok = lsb("htok", [NS, 256])
    for c2 in range(2):
        pz = nps()
        for k in range(3):
            P.tr(pz[:, k * NS:(k + 1) * NS], rgct[:, k, c2 * 128:(c2 + 1) * 128], ident[0:NS, 0:NS])
        P.tr(pz[:, 3 * NS:4 * NS], rght[:, c2 * 128:(c2 + 1) * 128], ident[0:NS, 0:NS])
        P.cp(xcs[:, c2, 0:3, :].r("p k s -> p (k s)"), pz[:, 0:3 * NS], eng="act")
        P.cp(h0T[:, c2, :], pz[:, 3 * NS:4 * NS], eng="act")
        P.cp(xcs[:, c2, 3, :], Fs[:, c2, :], eng="pool")
        pr = rgp[:, l, c2, :]
        P.ts(xcd[:], xcs[:, c2, 0, :], pr[:, 0:1], ALU.mult, pr[:, 4:5], ALU.add)
        for k in range(1, 4):
            P.stt(xcd[:], xcs[:, c2, k, :], pr[:, k:k + 1], xcd[:], ALU.mult, ALU.add)
        P.cp(xcdb[:], xcd[:], eng="pool")
        pz = nps()
        P.mm(pz[:, 0:NS], BD[:, c2, 0, :], xcdb[:])
        P.act(r_[:], pz[:, 0:NS], AF.Sigmoid, bias=pr[:, 5:6])
        pz = nps()
        P.mm(pz[:, 0:NS], BD[:, c2, 1, :], xcdb[:])
        P.act(i_[:], pz[:, 0:NS], AF.Sigmoid, bias=pr[:, 6:7])
        P.act(a_[:], r_[:], AF.Exp, scale=rgc[:, l, c2, :])
        P.act(b_[:], r_[:], AF.Exp, scale=rgc2[:, l, c2, :])
        P.ts(b_[:], b_[:], -1.0, ALU.mult, 1.0, ALU.add)
        P.ts(b_[:], b_[:], 0.0, ALU.max)
        P.act(b_[:], b_[:], AF.Sqrt)
        P.tt(i_[:], i_[:], xcd[:], ALU.mult)
        P.tt(b_[:], b_[:], i_[:], ALU.mult)
        P.tt(a_[:], a_[:], h0T[:, c2, :], ALU.mult)
        P.tt(hT_[:, c2, :], a_[:], b_[:], ALU.add)
        gelu_tanh(g2[:], Fs[:, 2 + c2, :], g1[:])
        P.tt(mixTs[:, 4 + c2, :], g2[:], hT_[:, c2, :], ALU.mult)
        pz = nps()
        P.tr(pz[0:NS, 0:128], hT_[:, c2, :], ident[:])
        P.cp(htok[:, c2 * 128:(c2 + 1) * 128], pz[0:NS, 0:128], eng="act")
    P.dma(os_rgh[l], htok[:])
    Sst = lsb("Sst", [128, NS, 2, 64])
    for hh in range(2):
        P.dma(Sst[hh * 64:(hh + 1) * 64, :, :, :], shgs_d[l].r("s (c hh) k v -> hh k s c v", hh=2)[hh])
    fT = lsb("fT", [128, 2, NS])
    kT_ = lsb("kT_", [128, 2, NS])
    qT_ = lsb("qT_", [128, 2, NS])
    for c2 in range(2):
        P.act(fT[:, c2, :], Fs[:, 6 + c2, :], AF.Sigmoid)
        P.ts(fT[:, c2, :], fT[:, c2, :], oml[:, c2, l:l + 1], ALU.mult, lb[:, c2, l:l + 1], ALU.add)
        P.ts(kT_[:, c2, :], fT[:, c2, :], -1.0, ALU.mult, 1.0, ALU.add)
        P.act(qT_[:, c2, :], Fs[:, 4 + c2, :], AF.Sigmoid)
        P.tt(qT_[:, c2, :], qT_[:, c2, :], Fs[:, 4 + c2, :], ALU.mult)
    vdiag = lsb("vdiag", [NS, NS, 128])
    t2c = lsb("t2c", [128, 4, 2, 64])
    fb = fT[:].r("p c (s o) -> p s c o", o=1).b([128, NS, 2, 64])
    kb = kT_[:].r("p c (s o) -> p s c o", o=1).b([128, NS, 2, 64])
    P.tt(Sst[:], Sst[:], fb, ALU.mult)
    SPC = 4
    for hh in range(2):
        hp = slice(hh * 64, (hh + 1) * 64)
        v_hh = Ps[:, IN_OFF["hi"]:IN_OFF["hi"] + 256].r("p (c hh v) -> p hh c v", hh=2, v=64)[:, hh]
        P.tt(vdiag[:].r("p s (c v) -> p s c v", c=2), v_hh.r("p (o c) v -> p o c v", o=1).b([NS, NS, 2, 64]),
             ident[0:NS, 0:NS].r("p (s o t) -> p s o t", o=1, t=1).b([NS, NS, 2, 64]), ALU.mult)
        for q4 in range((NS + SPC - 1) // SPC):
            ns_ = min(SPC, NS - q4 * SPC)
            pz = nps()
            P.mm(pz[hp, 0:ns_ * 128], ones_f[0:NS, 0:64], vdiag[:, q4 * SPC:q4 * SPC + ns_, :].r("p s c -> p (s c)"))
            sl = slice(q4 * SPC, q4 * SPC + ns_)
            P.tt(t2c[hp, 0:ns_], pz[hp, 0:ns_ * 128].r("p (s c v) -> p s c v", s=ns_, c=2), kb[hp, sl, :, :], ALU.mult)
            P.tt(Sst[hp, sl, :, :], Sst[hp, sl, :, :], t2c[hp, 0:ns_], ALU.add)
    Snew = Sst
    for hh in range(2):
        P.dma(os_hgs[l].r("s (c hh) k v -> hh k s c v", hh=2)[hh], Snew[hh * 64:(hh + 1) * 64, :, :, :])
    pz = nps()
    for s in range(NS):
        for c2 in range(2):
            for hh in range(2):
                hp = slice(hh * 64, (hh + 1) * 64)
                P.mm(pz[hp, c2 * NS + s:c2 * NS + s + 1], Snew[hp, s, c2, :], qT_[hp, c2, s:s + 1])
    oTs = lsb("oTs", [128, 2, NS])
    P.cp(oTs[:].r("p c s -> p (c s)"), pz[:, 0:2 * NS], eng="act")
    osq_ = lsb("osq_", [128, NS], BF16)
    ors_ = lsb("ors_", [128, NS])
    for c2 in range(2):
        P.act(osq_[:], oTs[:, c2, :], AF.Square)
        pz = nps()
        P.mm(pz[:, 0:NS], C["blockones"][:], osq_[:])
        P.act(ors_[:], pz[:, 0:NS], AF.Sqrt, bias=EPS, scale=1.0 / 64)
        P.recip(ors_[:], ors_[:])
        P.tt(ors_[:], ors_[:], oTs[:, c2, :], ALU.mult)
        P.act(g1[:], Fs[:, 8 + c2, :], AF.Sigmoid)
        P.tt(g1[:], g1[:], Fs[:, 8 + c2, :], ALU.mult)
        P.stt(mixTs[:, 6 + c2, :], ors_[:], hgp[:, c2, 1, l:l + 1], g1[:], ALU.mult, ALU.mult)
    for dc in range(8):
        pz = nps()
        for k in range(8):
            P.mm(pz[:, 0:NS], Wout[:, k, dc * 128:(dc + 1) * 128], mixTs[:, k, :], start=(k == 0), stop=(k == 7))
        P.tt(xsT[:, dc, :], xsT[:, dc, :], pz[:, 0:NS], ALU.add)


def decode_ffn(l, lsb, Wup, Wdn):
    ysB = lsb("ysB", [128, 8, NS], BF16)
    HTs = lsb("HTs", [128, 32, NS], BF16)
    hrs = lsb("hrs", [128, NS])
    ntmpf = lsb("ntmpf", [128, 8, NS])
    rmsnorm_T(xsT[:], normT[:, l, 1, :], ysB[:], NS, ntmpf[:])
    for f in range(32):
        pz = nps()
        for c in range(8):
            P.mm(pz[:, :NS], Wup[:, c, f * 128:(f + 1) * 128], ysB[:, c, :], start=(c == 0), stop=(c == 7))
        P.act(hrs[:], pz[:, :NS], AF.Relu)
        P.tt(HTs[:, f, :], hrs[:], hrs[:], ALU.mult, eng="pool")
    for dc in range(8):
        pz = nps()
        for f in range(32):
            P.mm(pz[:, :NS], Wdn[:, f, dc * 128:(dc + 1) * 128], HTs[:, f, :], start=(f == 0), stop=(f == 31))
        P.tt(xsT[:, dc, :], xsT[:, dc, :], pz[:, :NS], ALU.add)
    if l == L - 1:
        yfs = lsb("yfs", [128, 8, NS])
        ystok = lsb("ystok", [NS, D])
        sqv = sq[:, :, :NS]
        P.act(sqv, xsT[:], AF.Square)
        pz = nps()
        for c in range(8):
            P.mm(pz[:, :NS], ones_bf[:], sq[:, c, :NS], start=(c == 0), stop=(c == 7))
        P.act(rstd[:, :NS], pz[:, :NS], AF.Sqrt, bias=EPS, scale=1.0 / D)
        P.recip(rstd[:, :NS], rstd[:, :NS])
        P.tt(yfs[:], xsT[:], rstd[:, :NS].r("p (o t) -> p o t", o=1).b([128, 8, NS]), ALU.mult)
        P.tt(yfs[:], yfs[:], fnormT[:].r("p (c o) -> p c o", o=1).b([128, 8, NS]), ALU.mult)
        for c in range(8):
            pz = nps()
            P.tr(pz[0:NS, 0:128], yfs[:, c, :], ident[:])
            P.cp(ystok[:, c * 128:(c + 1) * 128], pz[0:NS, 0:128], eng="act")
        P.dma(ys_out[:], ystok[:])


QPERM = np.concatenate([np.arange(h * 64, (h + 1) * 64) for h in (0, 4, 1, 5, 2, 6, 3, 7)])


def host_inputs(cfg, inp, consts, b, core=0):
    L = cfg["L"]
    f = lambda a: np.ascontiguousarray(np.asarray(a, dtype=np.float32))
    m = {}
    m["x_prompt"] = f(inp["x_prompt"][b])
    w_in = np.asarray(inp["w_in"], np.float32).copy()
    w_in[:, :, 0:512] = w_in[:, :, QPERM]
    m["w_in"] = f(w_in)
    m["w_out"] = f(inp["w_out"])
    m["w_up"] = f(inp["w_up"])
    m["w_down"] = f(inp["w_down"])
    nm = np.stack([np.asarray(inp["norm_mix"]), np.asarray(inp["norm_ffn"])], 1)
    m["normT"] = f(nm.reshape(L, 2, 8, 128).transpose(3, 0, 1, 2))
    m["fnormT"] = f(np.asarray(inp["final_norm"]).reshape(8, 128).T)
    m["cmp_w1"] = f(inp["cmp_w1"])
    m["cmp_posT"] = f(np.asarray(inp["cmp_pos"]).reshape(L, 2, 16, 2, 64).transpose(3, 4, 0, 1, 2).reshape(128, L, 2, 16))
    m["cmp_b1T"] = f(np.asarray(inp["cmp_b1"]).reshape(L, 2, 2, 128).transpose(3, 0, 1, 2))
    m["cmp_w2"] = f(inp["cmp_w2"])
    m["cmp_b2"] = f(inp["cmp_b2"])
    cw = np.asarray(inp["rg_conv_w"])
    rp = np.zeros((L, 9, 256), np.float32)
    rp[:, 0:4] = cw
    rp[:, 4] = np.asarray(inp["rg_conv_b"])
    rp[:, 5] = np.asarray(inp["rg_ba"])
    rp[:, 6] = np.asarray(inp["rg_bx"])
    rp[:, 7] = np.asarray(inp["rg_lambda"])
    m["rg_pT"] = f(rp.reshape(L, 9, 2, 128).transpose(3, 0, 2, 1))
    m["rg_wa"] = f(inp["rg_wa"])
    m["rg_wx"] = f(inp["rg_wx"])
    hp = np.stack([np.asarray(inp["hg_lower_bounds"]), np.asarray(inp["hg_gain"])], 0)
    m["hg_pT"] = f(hp.reshape(2, L, 2, 128).transpose(3, 2, 0, 1))
    for k, v in consts.items():
        m["c_" + k] = f(v)
    NS = cfg.get("NS", 0)
    if NS:
        sl = slice(core * NS, (core + 1) * NS)
        past = cfg["past"]
        npg = past // 128
        wb = min(512, past)
        m["x_sample"] = f(np.asarray(inp["x_sample"])[sl, 0, :])
        m["page_table"] = np.ascontiguousarray(np.asarray(inp["page_table"])[sl].reshape(1, NS * npg).astype(np.int32))
        m["cache_cmp"] = f(inp["cache_nsa_cmp_kv"]).reshape(-1, 256)
        m["cache_sel"] = f(inp["cache_nsa_sel_kv"]).reshape(-1, 256)
        m["cache_win"] = f(np.asarray(inp["cache_nsa_win_kv"])[:, sl]).reshape(L, NS, wb, 256)
        m["st_rgh"] = f(np.asarray(inp["state_rglru_h"])[:, sl])
        m["st_rgc"] = f(np.asarray(inp["state_rglru_conv"])[:, sl])
        m["st_hgs"] = f(np.asarray(inp["state_hgrn_s"])[:, sl])
    return m


_CACHE = {}


def run(cfg, inp, ncores=8):
    key = tuple(sorted(cfg.items()))
    if key not in _CACHE:
        _CACHE[key] = build(cfg)
    nc, consts = _CACHE[key]
    B = np.asarray(inp["x_prompt"]).shape[0]
    maps = [host_inputs(cfg, inp, consts, c % B, c) for c in range(ncores)]
    res = run_bass_kernel_spmd(nc, maps, core_ids=list(range(ncores)))
    return res.results


def kernel(**inp):
    xp = np.asarray(inp["x_prompt"])
    B, Tn, _ = xp.shape
    L = np.asarray(inp["w_in"]).shape[0]
    NDEC = np.asarray(inp["x_sample"]).shape[0]
    npg = np.asarray(inp["page_table"]).shape[1]
    past = npg * 128
    ncores = 8
    NS = NDEC // ncores
    npool = np.asarray(inp["cache_nsa_cmp_kv"]).shape[1]
    cfg = dict(T=Tn, L=L, past=past, NS=NS, npool=npool)
    r = run(cfg, inp, ncores)
    wb = min(512, past)
    wt = min(512, Tn)
    f = np.float32
    y_prompt = np.stack([r[b]["y_prompt"] for b in range(B)], 0).astype(f)
    y_sample = np.concatenate([r[c]["y_sample"] for c in range(ncores)], 0).reshape(NDEC, 1, D).astype(f)

    def pst(name, shp):
        return np.stack([np.asarray(r[b][name]).reshape((L,) + shp) for b in range(B)], 1).astype(f)

    def sst(name, shp):
        return np.concatenate([np.asarray(r[c][name]).reshape((L, NS) + shp) for c in range(ncores)], 1).astype(f)

    return (y_prompt, y_sample,
            pst("p_cmp_kv", (Tn, 2, 2, 64)), pst("p_sel_kv", (Tn, 2, 2, 64)), pst("p_win_kv", (wt, 2, 2, 64)),
            pst("p_rg_h", (256,)), pst("p_rg_conv", (3, 256)), pst("p_hg_s", (4, 64, 64)),
            sst("s_cmp_kv", (1, 2, 2, 64)), sst("s_sel_kv", (1, 2, 2, 64)), sst("s_win_kv", (wb, 2, 2, 64)),
            sst("s_rg_h", (256,)), sst("s_rg_conv", (3, 256)), sst("s_hg_s", (4, 64, 64)))
```

```python
import contextlib
import numpy as np
import ml_dtypes
import concourse.bass as bass
import concourse.mybir as mybir
from concourse.bass_utils import run_bass_kernel_spmd

F32 = mybir.dt.float32
BF16 = mybir.dt.bfloat16
I32 = mybir.dt.int32
AF = mybir.ActivationFunctionType
ALU = mybir.AluOpType
AX = mybir.AxisListType

D = 1024
NQH = 8
EPS = 1e-6
IN_OFF = dict(q=0, kvs=512, gate=1280, rgx=1304, rgg=1560, hq=1816, hf=2072, hi=2328, hg=2584)
D_IN = 2840


class V:
    __slots__ = ("ap", "key")

    def __init__(self, ap, key):
        self.ap = ap
        self.key = key

    def r(self, pat, **kw):
        return V(self.ap.rearrange(pat, **kw), self.key)

    def b(self, shape):
        return V(self.ap.broadcast_to(shape), self.key)

    def __getitem__(self, idx):
        return V(self.ap[idx], self.key)


class _Sub:
    def __init__(self, t, k):
        self.t = t
        self.k = k

    def __getitem__(self, idx):
        return V(self.t.t[idx], (self.t.name, self.k))


class T:
    def __init__(self, name, t):
        self.name = name
        self.t = t

    def __getitem__(self, idx):
        return V(self.t[idx], (self.name, None))

    def s(self, k):
        return _Sub(self, k)


class StopBuild(Exception):
    pass


class Prog:
    SEM_CAP = 30000

    def __init__(self, nc, es):
        self.nc = nc
        self.es = es
        self.ops = []
        self.state = {}
        self.bar = set()
        self.eng = dict(pe=nc.tensor, act=nc.scalar, dve=nc.vector, pool=nc.gpsimd, sp=nc.sync)
        self.last = {}
        self.nosame = False
        self.dma_hist = {"sp": [], "pool": [], "act": []}

    def _touch(self, idx, key, write, deps):
        name, sub = key
        st = self.state.setdefault(name, {})
        if sub is None:
            ents = list(st.values())
        else:
            ents = [e for k, e in st.items() if k == sub or k is None]
        for e in ents:
            if e[0] is not None:
                deps.add(e[0])
            if write:
                deps.update(e[1])
        if write:
            if sub is None:
                st.clear()
                st[None] = [idx, []]
            else:
                st[sub] = [idx, []]
        else:
            st.setdefault(sub, [None, []])[1].append(idx)

    def add(self, eng, fn, r=(), w=(), dma=False, rg=None):
        idx = len(self.ops)
        deps = set(self.bar)
        for k in r:
            self._touch(idx, k, k[0].startswith("ps_"), deps)
        for k in w:
            self._touch(idx, k, True, deps)
        deps.discard(idx)
        self.ops.append(dict(eng=eng, fn=fn, deps=deps, dma=dma, rg=rg))
        self.last[eng] = idx
        if dma:
            self.dma_hist[eng].append(idx)
        return idx

    def barrier(self):
        b = set(self.last.values())
        for h in self.dma_hist.values():
            b.update(h[-32:])
        self.bar = b

    @staticmethod
    def _k(*vs):
        return [v.key for v in vs if isinstance(v, V)]

    @staticmethod
    def _a(v):
        return v.ap if isinstance(v, V) else v

    def mm(self, out, lhsT, rhs, start=True, stop=True):
        nc = self.nc
        self.add("pe", lambda: nc.tensor.matmul(out.ap, lhsT=lhsT.ap, rhs=rhs.ap, start=start, stop=stop,
                                                skip_group_check=True),
                 r=self._k(lhsT, rhs), w=[out.key], rg=lhsT.ap.start_partition())

    def tr(self, out, in_, ident):
        nc = self.nc
        self.add("pe", lambda: nc.tensor.transpose(out.ap, in_.ap, ident.ap), r=self._k(in_, ident), w=[out.key],
                 rg=in_.ap.start_partition())

    def act(self, out, in_, func, bias=None, scale=None, eng="act"):
        nc = self.nc
        kw = {}
        if bias is not None:
            kw["bias"] = self._a(bias)
        if scale is not None:
            kw["scale"] = self._a(scale)
        self.add("act", lambda: nc.scalar.activation(out.ap, in_.ap, func, **kw),
                 r=self._k(in_, bias, scale), w=[out.key])

    def tt(self, out, in0, in1, op, eng="dve"):
        e = self.eng[eng]
        self.add(eng, lambda: e.tensor_tensor(out.ap, in0.ap, in1.ap, op), r=self._k(in0, in1), w=[out.key])

    def ts(self, out, in0, s1, op0, s2=None, op1=None, eng="dve"):
        e = self.eng[eng]
        a1, a2 = self._a(s1), self._a(s2)
        if op1 is None:
            self.add(eng, lambda: e.tensor_scalar(out.ap, in0.ap, a1, None, op0), r=self._k(in0, s1), w=[out.key])
        else:
            self.add(eng, lambda: e.tensor_scalar(out.ap, in0.ap, a1, a2, op0, op1),
                     r=self._k(in0, s1, s2), w=[out.key])

    def stt(self, out, in0, scalar, in1, op0, op1):
        nc = self.nc
        sa = self._a(scalar)
        self.add("dve", lambda: nc.vector.scalar_tensor_tensor(out.ap, in0.ap, sa, in1.ap, op0, op1),
                 r=self._k(in0, scalar, in1), w=[out.key])

    def cp(self, out, in_, eng="dve"):
        if eng == "act":
            nc = self.nc
            self.add("act", lambda: nc.scalar.copy(out.ap, in_.ap), r=self._k(in_), w=[out.key])
        else:
            e = self.eng[eng]
            self.add(eng, lambda: e.tensor_copy(out.ap, in_.ap), r=self._k(in_), w=[out.key])

    def memset(self, out, val, eng="pool"):
        e = self.eng[eng]
        self.add(eng, lambda: e.memset(out.ap, val), w=[out.key])

    def recip(self, out, in_):
        nc = self.nc
        self.add("dve", lambda: nc.vector.reciprocal(out.ap, in_.ap), r=self._k(in_), w=[out.key])

    def reduce(self, out, in_, op, axis=AX.X):
        nc = self.nc
        self.add("dve", lambda: nc.vector.tensor_reduce(out.ap, in_.ap, axis, op), r=self._k(in_), w=[out.key])

    def scan(self, out, d0, d1, init, op0, op1):
        nc = self.nc
        ia = self._a(init)
        self.add("dve", lambda: nc.vector.tensor_tensor_scan(out.ap, d0.ap, d1.ap, ia, op0, op1),
                 r=self._k(d0, d1, init), w=[out.key])

    def max8(self, out, in_):
        nc = self.nc
        self.add("dve", lambda: nc.vector.max(out.ap, in_.ap), r=self._k(in_), w=[out.key])

    def match_replace(self, out, rep, vals, imm):
        nc = self.nc
        self.add("dve", lambda: nc.vector.match_replace(out.ap, rep.ap, vals.ap, imm), r=self._k(rep, vals),
                 w=[out.key])

    def dma(self, out, in_, eng="sp", **kw):
        e = self.eng[eng]
        self.add(eng, lambda: e.dma_start(out=out.ap, in_=in_.ap, **kw), r=self._k(in_), w=[out.key], dma=True)

    def idma(self, out, in_, idx_v, axis=0):
        nc = self.nc
        self.add("pool", lambda: nc.gpsimd.indirect_dma_start(
            out=out.ap, out_offset=None, in_=in_.ap,
            in_offset=bass.IndirectOffsetOnAxis(ap=idx_v.ap, axis=axis)),
            r=self._k(in_, idx_v), w=[out.key], dma=True)

    def emit(self):
        nc, es, ops = self.nc, self.es, self.ops

        nosame = self.nosame

        def skip(d, o):
            if d["dma"] or o["dma"] or d["eng"] != o["eng"]:
                return False
            if d["eng"] == "pe":
                return d["rg"] == o["rg"]
            return nosame

        need = set()
        for o in ops:
            for d in o["deps"]:
                if not skip(ops[d], o):
                    need.add(d)
        csem = {}
        ccount = {e: 0 for e in self.eng}
        RING = {"sp": 24, "pool": 12, "act": 4}
        rings = {e: [] for e in RING}
        rcount = {e: 0 for e in RING}
        rtot = {}
        known = {e: {} for e in self.eng}
        tok = [None] * len(ops)
        for i, o in enumerate(ops):
            e = o["eng"]
            E = self.eng[e]
            kn = known[e]
            waits = {}
            for d in o["deps"]:
                if skip(ops[d], o):
                    continue
                s_, v = tok[d]
                if waits.get(id(s_), (None, 0))[1] < v:
                    waits[id(s_)] = (s_, v)
            pre = None
            if o["dma"]:
                j = rcount[e]
                rcount[e] += 1
                slot = j % RING[e]
                if slot >= len(rings[e]):
                    rings[e].append(es.enter_context(nc.semaphore(f"r_{e}_{slot}")))
                rs = rings[e][slot]
                prev = rtot.get(id(rs), 0)
                if prev > 0 and waits.get(id(rs), (None, 0))[1] < prev:
                    waits[id(rs)] = (rs, prev)
                rtot[id(rs)] = prev + 16
                pre = (rs, prev + 16)
            for key, (s_, v) in waits.items():
                if kn.get(key, 0) >= v:
                    continue
                E.wait_ge(s_, v)
                kn[key] = v
            ins = o["fn"]()
            if o["dma"]:
                ins.then_inc(pre[0], 16)
                tok[i] = pre
            elif i in need:
                c = ccount[e]
                ep, val = c // self.SEM_CAP, c % self.SEM_CAP + 1
                if (e, ep) not in csem:
                    csem[(e, ep)] = es.enter_context(nc.semaphore(f"c_{e}_{ep}"))
                ins.then_inc(csem[(e, ep)], 1)
                ccount[e] = c + 1
                tok[i] = (csem[(e, ep)], val)
        for e in rings:
            for rs in rings[e]:
                nc.sync.wait_ge(rs, rtot[id(rs)])


def _fix_waits():
    pass


def rope_tab(pos):
    inv = 500000.0 ** (-np.arange(8, dtype=np.float32) * 2.0 / 16.0)
    ang = pos.astype(np.float32)[:, None] * inv.astype(np.float32)
    return np.cos(ang).astype(np.float32), np.sin(ang).astype(np.float32)


def make_consts(Tn, past, ns=0):
    NT = Tn // 128
    c = {}
    c["ident"] = np.eye(128, dtype=np.float32)
    cs, sn = rope_tab(np.arange(Tn))
    c["ropeP"] = np.stack([cs.reshape(NT, 128, 8).transpose(1, 0, 2), sn.reshape(NT, 128, 8).transpose(1, 0, 2)], 1)
    slot = np.arange(NT * 8)
    cs, sn = rope_tab(16 * slot + 15)
    c["ropeC"] = np.stack([cs.reshape(NT, 8, 8).transpose(1, 0, 2), sn.reshape(NT, 8, 8).transpose(1, 0, 2)], 1)
    m = np.zeros((128, 17, 128), np.float32)
    p = np.arange(128)[:, None]
    qi = np.arange(128)[None, :]
    for r in range(16):
        sp = p - 8 * r
        m[:, r, :] = np.where(sp < 0, 1.0, np.where(sp <= 7, (qi >= 16 * sp + 15), 0.0))
    m[:, 16, :] = 1.0
    c["cmpmask"] = m
    nslot = NT * 8
    KT = (nslot + 127) // 128
    sm = np.zeros((KT * 128, 128), np.float32)
    for s in range(1, nslot):
        c0 = 16 * (s - 1)
        for j in range(128):
            if c0 < 64 * j + 64 and c0 + 32 > 64 * j:
                sm[s, j] = 1.0
    c["smap"] = sm.reshape(KT, 128, 128).transpose(1, 0, 2).copy()
    A = np.zeros((128, 255), np.float32)
    M = np.zeros((128, 255), np.float32)
    for q in range(128):
        hi = 1 if q >= 64 else 0
        for ci in range(255):
            cc = ci - 127
            valid = cc <= hi
            forced = (cc == hi) or (cc == hi - 1)
            if not valid:
                A[q, ci] = -1e6
            elif forced:
                A[q, ci] = 1e6
            else:
                M[q, ci] = 1.0
    c["selA"] = A
    c["selM"] = M
    ki = np.arange(128)[:, None]
    c["tri"] = np.stack([(ki <= qi).astype(np.float32), (ki > qi).astype(np.float32)], 1)
    bo = np.zeros((128, 128), np.float32)
    bo[:64, :64] = 1.0
    bo[64:, 64:] = 1.0
    c["blockones"] = bo
    s64 = np.arange(64)[:, None]
    t64 = np.arange(64)[None, :]
    c["tri64"] = np.concatenate([(s64 <= t64).astype(np.float32)] * 2, 0)
    if ns:
        NCB = past // 16 - 1
        NSB = past // 64 + 1
        CUR = past // 64
        cs, sn = rope_tab(np.full((ns,), past))
        c["ropeS"] = np.stack([cs, sn], 1)
        cs, sn = rope_tab(16 * np.arange(NCB) + 31)
        c["ropeCS"] = np.stack([cs, sn], 1)
        sm = np.zeros((NCB, NSB), np.float32)
        for i in range(NCB):
            for j in range(NSB):
                if 16 * i < 64 * j + 64 and 16 * i + 32 > 64 * j:
                    sm[i, j] = 1.0
        c["smapS"] = sm
        A = np.zeros((2 * ns, 40), np.float32)
        M = np.zeros((2 * ns, 40), np.float32)
        assert NSB <= 40
        for j in range(40):
            if j >= NSB:
                A[:, j] = -1e6
            elif j in (0, CUR, CUR - 1):
                A[:, j] = 1e6
            else:
                M[:, j] = 1.0
        c["selAS"] = A
        c["selMS"] = M
        Dm = np.zeros((40, past), np.float32)
        u = np.arange(past)
        Dm[u // 64, u] = 1.0
        c["DS"] = Dm
        c["pcol"] = np.arange(128, dtype=np.float32).reshape(128, 1)
    return c


CONST_BF = ("cmpmask", "smap", "tri", "blockones", "tri64", "smapS", "DS")


def build(cfg):
    Tn, L = cfg["T"], cfg["L"]
    NT = Tn // 128
    KTC = (NT * 8 + 127) // 128
    WIN_T = min(4, NT)
    nc = bass.Bass("TRN2", target_bir_lowering=False)
    es = contextlib.ExitStack()
    P = Prog(nc, es)
    P.nosame = bool(cfg.get("nosame"))
    if cfg.get("dry"):
        P.add = lambda *a, **k: None
    NS = cfg.get("NS", 0)
    PAST = cfg.get("past", 0)
    SKIP = cfg.get("skip", ())
    consts = make_consts(Tn, PAST, NS)

    def dr_in(name, shape, dt=F32):
        return T(name, nc.dram_tensor(name, list(shape), dt, kind="ExternalInput"))

    def dr_out(name, shape, dt=F32):
        return T(name, nc.dram_tensor(name, list(shape), dt, kind="ExternalOutput"))

    def sb(name, shape, dt=F32):
        return T("s_" + name, es.enter_context(nc.sbuf_tensor("s_" + name, list(shape), dt)))

    def ps(name, shape=(128, 512), dt=F32):
        return T(name, es.enter_context(nc.psum_tensor(name, list(shape), dt)))

    x_in = dr_in("x_prompt", [Tn, D])
    w_in_d = dr_in("w_in", [L, D, D_IN])
    w_out_d = dr_in("w_out", [L, D, D])
    w_up_d = dr_in("w_up", [L, D, 4 * D])
    w_dn_d = dr_in("w_down", [L, 4 * D, D])
    normT_d = dr_in("normT", [128, L, 2, 8])
    fnormT_d = dr_in("fnormT", [128, 8])
    w1_d = dr_in("cmp_w1", [L, 2, 32, 64, 256])
    posT_d = dr_in("cmp_posT", [128, L, 2, 16])
    b1T_d = dr_in("cmp_b1T", [128, L, 2, 2])
    w2_d = dr_in("cmp_w2", [L, 2, 256, 64])
    b2_d = dr_in("cmp_b2", [L, 2, 64])
    rgpT_d = dr_in("rg_pT", [128, L, 2, 9])
    rgwa_d = dr_in("rg_wa", [L, 4, 64, 64])
    rgwx_d = dr_in("rg_wx", [L, 4, 64, 64])
    hgpT_d = dr_in("hg_pT", [128, 2, 2, L])
    cdr = {k: dr_in("c_" + k, v.shape) for k, v in consts.items()}

    y_out = dr_out("y_prompt", [Tn, D])
    o_cmp = dr_out("p_cmp_kv", [L, Tn, 256])
    o_sel = dr_out("p_sel_kv", [L, Tn, 256])
    o_win = dr_out("p_win_kv", [L, WIN_T * 128, 256])
    o_rgh = dr_out("p_rg_h", [L, 256])
    o_rgc = dr_out("p_rg_conv", [L, 3, 256])
    o_hgs = dr_out("p_hg_s", [L, 4, 64, 64])
    xs_d = T("xscr", nc.dram_tensor("xscr", [D, Tn], F32, kind="Internal"))
    if NS:
        NPG = PAST // 128
        WB = min(512, PAST)
        NPOOL = cfg["npool"]
        xsam_d = dr_in("x_sample", [NS, D])
        pt_d = dr_in("page_table", [1, NS * NPG], I32)
        ccmp_d = dr_in("cache_cmp", [L * NPOOL * 128, 256])
        csel_d = dr_in("cache_sel", [L * NPOOL * 128, 256])
        cwin_d = dr_in("cache_win", [L, NS, WB, 256])
        srgh_d = dr_in("st_rgh", [L, NS, 256])
        srgc_d = dr_in("st_rgc", [L, NS, 3, 256])
        shgs_d = dr_in("st_hgs", [L, NS, 4, 64, 64])
        ys_out = dr_out("y_sample", [NS, D])
        os_cmp = dr_out("s_cmp_kv", [L, NS, 256])
        os_sel = dr_out("s_sel_kv", [L, NS, 256])
        os_win = dr_out("s_win_kv", [L, NS, WB, 256])
        os_rgh = dr_out("s_rg_h", [L, NS, 256])
        os_rgc = dr_out("s_rg_conv", [L, NS, 3, 256])
        os_hgs = dr_out("s_hg_s", [L, NS, 4, 64, 64])

    C = {}
    for k, v in consts.items():
        shp = list(v.shape)
        if k in ("ropeP", "ropeC"):
            continue
        if k in CONST_BF:
            C[k] = sb("k_" + k, shp, BF16)
            P.dma(C[k][:], cdr[k][:], eng="pool")
        else:
            C[k] = sb("k_" + k, shp, F32)
            P.dma(C[k][:], cdr[k][:])
    ident = C["ident"]
    ident_bf = sb("ident_bf", [128, 128], BF16)
    P.cp(ident_bf[:], ident[:], eng="pool")
    ones_bf = sb("ones_bf", [128, 128], BF16)
    P.memset(ones_bf[:], 1.0)
    ones_f = sb("ones_f", [128, 128], F32)
    P.memset(ones_f[:], 1.0)
    P.memset(ones_f[:, 64:65], 0.0)
    normT = sb("normT", [128, L, 2, 8])
    P.dma(normT[:], normT_d[:])
    fnormT = sb("fnormT", [128, 8])
    P.dma(fnormT[:], fnormT_d[:])
    rgp = sb("rgp", [128, L, 2, 9])
    P.dma(rgp[:], rgpT_d[:])
    hgp = sb("hgp", [128, 2, 2, L])
    P.dma(hgp[:], hgpT_d[:])
    lbe = sb("lbe", [128, 2, L])
    lb = sb("lb", [128, 2, L])
    lbs = sb("lbs", [128, 2, 1])
    oml = sb("oml", [128, 2, L])
    P.act(lbe[:], hgp[:, :, 0, :], AF.Exp)
    P.reduce(lbs[:, :, 0], lbe[:], ALU.add)
    P.recip(lbs[:], lbs[:])
    P.tt(lbe[:], lbe[:], lbs[:].b([128, 2, L]), ALU.mult)
    P.memset(lb[:, :, 0:1], 0.0)
    for l in range(1, L):
        P.tt(lb[:, :, l:l + 1], lb[:, :, l - 1:l], lbe[:, :, l:l + 1], ALU.add)
    P.ts(oml[:], lb[:], -1.0, ALU.mult, 1.0, ALU.add)
    rgc = sb("rgc", [128, L, 2, 1])
    P.act(rgc[:], rgp[:, :, :, 7:8], AF.Exp, scale=-1.0)
    P.act(rgc[:], rgc[:], AF.Ln, bias=1.0)
    P.ts(rgc[:], rgc[:], -8.0, ALU.mult)
    rgc2 = sb("rgc2", [128, L, 2, 1])
    P.ts(rgc2[:], rgc[:], 2.0, ALU.mult)

    pa = [ps("ps_a%d" % i) for i in range(3)]
    pS = [ps("ps_s%d" % i) for i in range(2)]
    pM = ps("ps_m")
    pO = ps("ps_o")
    pI = ps("ps_i")
    rot = [0]

    def nps():
        rot[0] = (rot[0] + 1) % 3
        return pa[rot[0]]

    xT = sb("xT", [128, 8, 128])
    yT = sb("yT", [128, 8, 128], BF16)
    sq = sb("sq", [128, 8, 128], BF16)
    rstd = sb("rstd", [128, 128])

    def rmsnorm_T(xv, gv, out_bf, ntok, tmp):
        sqv = sq[:, :, :ntok]
        P.act(sqv, xv, AF.Square)
        pz = nps()
        for c in range(8):
            P.mm(pz[:, :ntok], ones_bf[:], sq[:, c, :ntok], start=(c == 0), stop=(c == 7))
        P.act(rstd[:, :ntok], pz[:, :ntok], AF.Sqrt, bias=EPS, scale=1.0 / D)
        P.recip(rstd[:, :ntok], rstd[:, :ntok])
        P.tt(tmp, xv, rstd[:, :ntok].r("p (o t) -> p o t", o=1).b([128, 8, ntok]), ALU.mult)
        P.tt(out_bf, tmp, gv.r("p (c o) -> p c o", o=1).b([128, 8, ntok]), ALU.mult)

    if NS:
        xsT = sb("xsT", [128, 8, NS])
        with nc.sbuf_tensor("s_xstok", [NS, D], F32) as _xt:
            xstok = T("s_xstok", _xt)
            P.dma(xstok[:], xsam_d[:])
            for c in range(8):
                pz = nps()
                P.tr(pz[:, 0:NS], xstok[:, c * 128:(c + 1) * 128], ident[0:NS, 0:NS])
                P.cp(xsT[:, c, :], pz[:, 0:NS], eng="act")
        P.barrier()
        ptb = sb("ptb", [128, NS * NPG], I32)
        idxt = sb("idxt", [128, NS * NPG], I32)
        P.dma(ptb[:], pt_d[:].b([128, NS * NPG]))
        P.ts(idxt[:], ptb[:], 128.0, ALU.mult, C["pcol"][:, 0:1], ALU.add)

    def chk(k):
        if cfg.get("stop") == k:
            raise StopBuild()

    try:
        _layers(locals())
    except StopBuild:
        pass
    P.emit()
    return nc, consts


def _layers(env):
    globals().update({k: v for k, v in env.items() if not k.startswith("__")})
    chk(1)
    for l in range(L):
        with contextlib.ExitStack() as les:
            cur = [les]

            def lsb(name, shape, dt=F32):
                return T("s_" + name + "_%d" % l, cur[0].enter_context(nc.sbuf_tensor("s_" + name + "_%d" % l, list(shape), dt)))

            P.barrier()
            Wout = lsb("Wout", [128, 8, D], BF16)
            for c in range(8):
                P.dma(Wout[:, c, :], w_out_d[l, c * 128:(c + 1) * 128, :], eng="pool", max_dma_last_dim=4096)
            W1 = lsb("W1", [128, 2, 16, 256], BF16)
            for kv in range(2):
                for h2 in range(2):
                    P.dma(W1[h2 * 64:(h2 + 1) * 64, kv, :, :], w1_d[l, kv].r("(s t) d h -> t d s h", t=2)[h2], eng="pool",
                          max_dma_last_dim=1024)
            W2 = lsb("W2", [128, 2, 2, 64], BF16)
            P.dma(W2[:].r("p k c d -> p (k c) d"), w2_d[l].r("k (c p) d -> p (k c) d", p=128), eng="pool")
            b2r = lsb("b2r", [1, 2, 64], BF16)
            P.dma(b2r[:], b2_d[l:l + 1, :, :], eng="pool")
            posT = lsb("posT", [128, 2, 16], BF16)
            P.dma(posT[:], posT_d[:, l, :, :], eng="pool")
            b1T = lsb("b1T", [128, 2, 2])
            P.dma(b1T[:], b1T_d[:, l, :, :])
            cb1 = lsb("cb1", [128, 2, 2])
            for kv in range(2):
                for hc in range(2):
                    pz = nps()
                    for s in range(16):
                        P.mm(pz[:, 0:1], W1[:, kv, s, hc * 128:(hc + 1) * 128], posT[:, kv, s:s + 1],
                             start=(s == 0), stop=(s == 15))
                    P.tt(cb1[:, kv, hc:hc + 1], pz[:, 0:1], b1T[:, kv, hc:hc + 1], ALU.add)
            BDf = lsb("BDf", [128, 2, 2, 128])
            BD = lsb("BD", [128, 2, 2, 128], BF16)
            P.memset(BDf[:], 0.0)
            for c2 in range(2):
                for hh in range(2):
                    blk = c2 * 2 + hh
                    P.dma(BDf[hh * 64:(hh + 1) * 64, c2, 0, hh * 64:(hh + 1) * 64], rgwa_d[l, blk])
                    P.dma(BDf[hh * 64:(hh + 1) * 64, c2, 1, hh * 64:(hh + 1) * 64], rgwx_d[l, blk])
            P.cp(BD[:], BDf[:], eng="pool")

            chk(2)
            pes = contextlib.ExitStack()
            cur[0] = pes
            Win = lsb("Win", [128, 8, D_IN], BF16)
            for c in range(8):
                P.dma(Win[:, c, :], w_in_d[l, c * 128:(c + 1) * 128, :], eng="pool", max_dma_last_dim=4096)
            KsT = lsb("KsT", [128, Tn], BF16)
            KwT = lsb("KwT", [128, 8 * 128], BF16)
            Vs = lsb("Vs", [128, NT, 2, 65], BF16)
            Vw = lsb("Vw", [128, 8, 2, 65], BF16)
            KcT = lsb("KcT", [128, KTC * 128], BF16)
            Vc = lsb("Vc", [128, KTC, 2, 65], BF16)
            P.memset(KcT[:], 0.0)
            P.memset(Vc[:], 0.0)
            P.memset(Vs[:, :, :, 64:65], 1.0)
            P.memset(Vw[:, :, :, 64:65], 1.0)
            rawT = lsb("rawT", [128, 2, 2, 144], BF16)
            rkv = lsb("rkv", [128, 256], BF16)
            P.memset(rawT[:], 0.0)
            xcat = lsb("xcat", [128, 2, 131])
            P.memset(xcat[:], 0.0)
            hst = lsb("hst", [128, 2, 1])
            P.memset(hst[:], 0.0)
            S32 = lsb("S32", [128, 2, 64])
            Sbf = lsb("Sbf", [128, 2, 64], BF16)
            P.memset(S32[:], 0.0)
            P.memset(Sbf[:], 0.0)

            Ptok = lsb("Ptok", [128, 1560])
            rpt = lsb("rpt", [128, 2, 8])
            rct = lsb("rct", [8, 2, 8])
            ra = lsb("ra", [128, 8, 8])
            rb = lsb("rb", [128, 8, 8])
            rc = lsb("rc", [128, 8, 8])
            rd = lsb("rd", [128, 8, 8])
            QT = lsb("QT", [128, 512], BF16)
            Ffm = lsb("Ffm", [128, 10, 128])
            vtok = lsb("vtok", [128, 256], BF16)
            ET = lsb("ET", [128, 512], BF16)
            PT = lsb("PT", [128, 512], BF16)
            ET2 = lsb("ET2", [128, 512], BF16)
            PT2 = lsb("PT2", [128, 512], BF16)
            ET3 = lsb("ET3", [128, 512], BF16)
            PT3 = lsb("PT3", [128, 512], BF16)
            OTs = lsb("OTs", [65, 512])
            Otok = lsb("Otok", [128, 3, 8, 65])
            impT = lsb("impT", [128, 2, 128])
            zr = lsb("zr", [128, 512])
            score = lsb("score", [128, 128])
            sc2 = lsb("sc2", [128, 128])
            mx8 = lsb("mx8", [128, 8])
            thr = lsb("thr", [128, 1])
            selm = lsb("selm", [128, 2, 128])
            Xb = [lsb("Xb%d" % i, [128, 128]) for i in range(3)]
            gates = lsb("gates", [128, 24])
            coef = lsb("coef", [128, 3, 8])
            onsa = lsb("onsa", [128, 512])
            mixT = lsb("mixT", [128, 8, 128], BF16)
            hpre = lsb("hpre", [128, 16])
            hu = lsb("hu", [128, 16])
            hgl = lsb("hgl", [128, 2, 2, 16], BF16)
            cblk = lsb("cblk", [8, 2, 2, 64])
            cblkr = lsb("cblkr", [8, 2, 64])
            cv = lsb("cv", [8, 2, 65], BF16)
            P.memset(cv[:, :, 64:65], 1.0)
            xc = lsb("xc", [128, 2, 128])
            xcb = lsb("xcb", [128, 2, 128], BF16)
            rg_r = lsb("rg_r", [128, 128])
            rg_i = lsb("rg_i", [128, 128])
            rg_a = lsb("rg_a", [128, 128])
            rg_b = lsb("rg_b", [128, 128])
            rg_h = lsb("rg_h", [128, 2, 128])
            gl = lsb("gl", [128, 128])
            f2 = lsb("hgf2", [128, 2, 128])
            g2 = lsb("hgg2", [128, 2, 128])
            k2 = lsb("hgk2", [128, 2, 128])
            qs2 = lsb("qs2", [128, 2, 128])
            bc2 = lsb("bc2", [128, 2, 128])
            exA = lsb("exA", [128, 2, 128])
            exB = lsb("exB", [128, 2, 128])
            ebl2 = lsb("ebl2", [128, 2, 2])
            qE = lsb("qE", [128, 2, 128], BF16)
            qe = lsb("qe", [128, 2, 128], BF16)
            ke = lsb("ke", [128, 2, 128], BF16)
            kdT = lsb("kdT", [128, 2, 128])
            kdtok = lsb("kdtok", [128, 2, 128], BF16)
            Am = lsb("Am", [128, 2, 64], BF16)
            oT = lsb("oT", [128, 2, 128])
            osq2 = lsb("osq2", [128, 2, 128], BF16)

            for n in range(NT):
                t0 = n * 128
                if l == 0:
                    xtok = Ptok
                    P.dma(Ptok[:, 0:1024], x_in[t0:t0 + 128, :])
                    for c in range(8):
                        pz = nps()
                        P.tr(pz[:, 0:128], Ptok[:, c * 128:(c + 1) * 128], ident[:])
                        P.cp(xT[:, c, :], pz[:, 0:128], eng="act")
                else:
                    P.dma(xT[:], xs_d[:, t0:t0 + 128].r("(c p) t -> p c t", p=128))
                rmsnorm_T(xT[:], normT[:, l, 0, :], yT[:], 128, Ptok[:, 0:1024].r("p (c t) -> p c t", c=8))
                tm_chunks = [(0, 512), (512, 512), (1024, 280), (IN_OFF["hi"], 256)]
                dst = 0
                for (c0, cw) in tm_chunks:
                    pz = nps()
                    for c in range(8):
                        P.mm(pz[:, :cw], yT[:, c, :], Win[:, c, c0:c0 + cw], start=(c == 0), stop=(c == 7))
                    P.cp(Ptok[:, dst:dst + cw], pz[:, :cw], eng="act")
                    dst += cw
                fm_cols = [IN_OFF["rgx"], IN_OFF["rgx"] + 128, IN_OFF["rgg"], IN_OFF["rgg"] + 128,
                           IN_OFF["hq"], IN_OFF["hq"] + 128, IN_OFF["hf"], IN_OFF["hf"] + 128,
                           IN_OFF["hg"], IN_OFF["hg"] + 128]
                for i, c0 in enumerate(fm_cols):
                    pz = nps()
                    for c in range(8):
                        P.mm(pz[:, :128], Win[:, c, c0:c0 + 128], yT[:, c, :], start=(c == 0), stop=(c == 7))
                    P.cp(Ffm[:, i, :], pz[:, :128], eng=("act" if i % 2 else "dve"))
                chk(3)
                P.dma(rpt[:], cdr["ropeP"][:, :, n, :])
                P.dma(rct[:], cdr["ropeC"][:, :, n, :])
                cosv = rpt[:, 0, :]
                sinv = rpt[:, 1, :]
                for (h0, nh) in ((0, 8), (12, 2), (16, 2)):
                    src = Ptok[:, h0 * 64:(h0 + nh) * 64].r("p (h d) -> p h d", d=64)
                    cb = cosv.r("p (o e) -> p o e", o=1).b([128, nh, 8])
                    sbv = sinv.r("p (o e) -> p o e", o=1).b([128, nh, 8])
                    x1, x2 = src[:, :, 0:8], src[:, :, 8:16]
                    P.tt(ra[:, :nh, :], x1, cb, ALU.mult)
                    P.tt(rb[:, :nh, :], x2, sbv, ALU.mult)
                    P.tt(rc[:, :nh, :], x2, cb, ALU.mult)
                    P.tt(rd[:, :nh, :], x1, sbv, ALU.mult)
                    P.tt(src[:, :, 0:8], ra[:, :nh, :], rb[:, :nh, :], ALU.subtract)
                    P.tt(src[:, :, 8:16], rc[:, :nh, :], rd[:, :nh, :], ALU.add)
                chk(31)
                P.dma(o_cmp[l, t0:t0 + 128, :], Ptok[:, 512:768])
                P.dma(o_sel[l, t0:t0 + 128, :], Ptok[:, 768:1024])
                if n >= NT - WIN_T:
                    w0 = (n - (NT - WIN_T)) * 128
                    P.dma(o_win[l, w0:w0 + 128, :], Ptok[:, 1024:1280])
                chk(32)
                pz = nps()
                for j in range(4):
                    P.tr(pz[:, j * 128:(j + 1) * 128], Ptok[:, j * 128:(j + 1) * 128], ident[:])
                chk(321)
                P.cp(QT[:], pz[:], eng="act")
                chk(322)
                P.cp(rkv[:], Ptok[:, 512:768], eng="pool")
                pz = nps()
                for kv in range(2):
                    for g in range(2):
                        cs_ = rkv[:, kv * 128 + g * 64:kv * 128 + (g + 1) * 64]
                        for h2 in range(2):
                            P.mm(pz[h2 * 64:(h2 + 1) * 64, (kv * 2 + g) * 128:(kv * 2 + g + 1) * 128], cs_, ident_bf[:])
                P.cp(rawT[0:64, :, :, 16:144], pz[0:64, :].r("p (k g t) -> p k g t", k=2, g=2), eng="dve")
                P.cp(rawT[64:128, :, :, 15:143], pz[64:128, :].r("p (k g t) -> p k g t", k=2, g=2), eng="dve")
                pz = nps()
                P.tr(pz[:, 256:384], Ptok[:, 768:896], ident[:])
                P.tr(pz[:, 384:512], Ptok[:, 1024:1152], ident[:])
                P.cp(KsT[:, t0:t0 + 128], pz[:, 256:384], eng="dve")
                P.cp(KwT[:, (n % 8) * 128:(n % 8 + 1) * 128], pz[:, 384:512], eng="dve")
                chk(33)
                P.cp(Vs[:, n, :, 0:64], Ptok[:, 896:1024].r("p (g d) -> p g d", g=2), eng="pool")
                P.cp(Vw[:, n % 8, :, 0:64], Ptok[:, 1152:1280].r("p (g d) -> p g d", g=2), eng="pool")
                P.cp(vtok[:], Ptok[:, 1304:1560], eng="pool")
                chk(4)
                for kv in range(0 if "cpr" in SKIP else 2):
                    for hc in range(2):
                        pz = nps()
                        for g in range(2):
                            for s in range(16):
                                rhs = rawT[:, kv, g, 2 * s:2 * s + 113:16]
                                P.mm(pz[:, g * 8:(g + 1) * 8], W1[:, kv, s, hc * 128:(hc + 1) * 128],
                                     rhs, start=(s == 0), stop=(s == 15))
                        P.ts(hpre[:], pz[:, 0:16], cb1[:, kv, hc:hc + 1], ALU.add)
                        P.tt(hu[:], hpre[:], hpre[:], ALU.mult)
                        P.ts(hu[:], hu[:], 0.044715, ALU.mult, 1.0, ALU.add)
                        P.tt(hu[:], hu[:], hpre[:], ALU.mult)
                        P.act(hu[:], hu[:], AF.Sigmoid, scale=1.5957691216)
                        P.tt(hgl[:, kv, hc, :], hu[:], hpre[:], ALU.mult)
                chk(41)
                pz = nps()
                for kv in range(2):
                    for g in range(2):
                        o_ = pz[0:8, (kv * 2 + g) * 64:(kv * 2 + g + 1) * 64]
                        for hc in range(2):
                            P.mm(o_, hgl[:, kv, hc, g * 8:(g + 1) * 8], W2[:, kv, hc, :], start=(hc == 0), stop=False)
                        P.mm(o_, ones_bf[0:1, 0:8], b2r[0:1, kv, :], start=False, stop=True)
                P.cp(cblk[:].r("b k g d -> b (k g d)"), pz[0:8, 0:256], eng="act")
                chk(42)
                cC = rct[:, 0, :].r("p (o e) -> p o e", o=1).b([8, 2, 8])
                sC = rct[:, 1, :].r("p (o e) -> p o e", o=1).b([8, 2, 8])
                P.cp(cblkr[:], cblk[:, 0, :, :], eng="pool")
                x1, x2 = cblk[:, 0, :, 0:8], cblk[:, 0, :, 8:16]
                P.tt(ra[0:8, 0:2, :], x1, cC, ALU.mult)
                P.tt(rb[0:8, 0:2, :], x2, sC, ALU.mult)
                P.tt(cblkr[:, :, 0:8], ra[0:8, 0:2, :], rb[0:8, 0:2, :], ALU.subtract)
                P.tt(ra[0:8, 0:2, :], x2, cC, ALU.mult)
                P.tt(rb[0:8, 0:2, :], x1, sC, ALU.mult)
                P.tt(cblkr[:, :, 8:16], ra[0:8, 0:2, :], rb[0:8, 0:2, :], ALU.add)
                pz = nps()
                P.tr(pz[:, 0:8], cblkr[:].r("b g d -> b (g d)"), ident[0:8, 0:8])
                P.cp(KcT[:, n * 8:(n + 1) * 8], pz[:, 0:8], eng="act")
                chk(43)
                P.cp(cv[:, :, 0:64], cblk[:, 1, :, :], eng="pool")
                kt_n, po = (n * 8) // 128, (n * 8) % 128
                if n == 0:
                    P.dma(Vc[1:8, 0, :, :], cv[1:8, :, :])
                else:
                    P.dma(Vc[po:po + 8, kt_n, :, :], cv[:, :, :])
                chk(44)
                P.cp(rawT[0:64, :, :, 0:16], rawT[0:64, :, :, 128:144], eng="pool")
                P.cp(rawT[64:128, :, :, 0:15], rawT[64:128, :, :, 128:143], eng="pool")

                chk(5)
                def finish(br, g):
                    P.cp(OTs[:], pO[0:65, :], eng="act")
                    pz_ = pa[2]
                    for j in range(4):
                        P.tr(pz_[:, j * 65:(j + 1) * 65], OTs[:, j * 128:(j + 1) * 128], ident[0:65, 0:65])
                    P.cp(Otok[:, br, g * 4:(g + 1) * 4, :], pz_[:, 0:260].r("p (j e) -> p j e", e=65),
                         eng=("act" if g else "dve"))

                ETb = [ET, ET2, ET3]
                PTb = [PT, PT2, PT3]
                pSb = [pS[0], pS[1], pa[0]]
                pMb = [pM, pI, pa[1]]

                def attend(br, g, items):
                    nI = len(items)
                    NB = 3 if items[0].get("extra") is None else 2

                    def s1(i):
                        it = items[i]
                        P.mm(pSb[i % NB][:], it["kT"], QT[g * 64:(g + 1) * 64, :], start=True, stop=True)
                        if it.get("selkt") is not None:
                            kt_ = it["selkt"]
                            xb_ = Xb[i % 3]
                            P.cp(xb_[:].r("p (b o) -> p b o", o=64),
                                 selm[:, g, 2 * kt_:2 * kt_ + 2].r("p (b o) -> p b o", o=1).b([128, 2, 64]), eng="pool")
                            P.tr(pMb[i % NB][:, 0:128], xb_[:], ident[:])

                    def s2(i):
                        it = items[i]
                        et, pt = ETb[i % NB], PTb[i % NB]
                        P.act(et[:], pSb[i % NB][:], AF.Exp, scale=0.125)
                        src = et
                        mk = it.get("mk")
                        if it.get("selkt") is not None:
                            mk = pMb[i % NB][:, 0:128]
                            if it.get("causal"):
                                P.tt(sc2[:], mk, C["tri"][:, 0, :], ALU.mult)
                                mk = sc2[:]
                        if mk is not None:
                            P.tt(pt[:].r("p (j q) -> p j q", j=4), et[:].r("p (j q) -> p j q", j=4),
                                 mk.r("p (o q) -> p o q", o=1).b([128, 4, 128]), ALU.mult)
                            src = pt
                        if it.get("z0"):
                            P.memset(src[0:1, :], 0.0, eng="dve")
                        P.mm(pO[0:65, :], it["va"], src[:], start=(i == 0), stop=(i == nI - 1))
                        if it.get("extra") is not None:
                            it["extra"](src, i == 0, i == nI - 1)

                    for i in range(min(NB - 1, nI)):
                        s1(i)
                    for i in range(nI):
                        if i + NB - 1 < nI:
                            s1(i + NB - 1)
                        s2(i)
                    finish(br, g)

                nkt = n // 16 + 1
                for g in range(0 if "cmp" in SKIP else 2):
                    lst = []
                    for kt in range(nkt):
                        mk = C["cmpmask"][:, n % 16, :] if kt == nkt - 1 else None

                        def extra(src, first, last_, kt=kt):
                            P.mm(pI[:], C["smap"][:, kt, :], src[:], start=first, stop=last_)
                            P.mm(pM[:], ones_bf[:], src[:], start=first, stop=last_)
                        lst.append(dict(kT=KcT[g * 64:(g + 1) * 64, kt * 128:(kt + 1) * 128], va=Vc[:, kt, g, :], mk=mk,
                                        z0=(kt == 0), extra=extra))
                    attend(0, g, lst)
                    P.ts(zr[:], pM[:], 1e-30, ALU.max)
                    P.recip(zr[:], zr[:])
                    P.tt(zr[:], pI[:], zr[:], ALU.mult)
                    P.reduce(impT[:, g, :], zr[:].r("p (j q) -> p q j", j=4), ALU.add)
                for g in range(2):
                    pz = nps()
                    P.tr(pz[:, 0:128], impT[:, g, :], ident[:])
                    P.tt(score[:], pz[:, 0:128], C["selM"][:, 127 - 2 * n:255 - 2 * n], ALU.mult)
                    P.tt(score[:], score[:], C["selA"][:, 127 - 2 * n:255 - 2 * n], ALU.add)
                    P.memset(score[:, 0:1], 1e6, eng="dve")
                    P.max8(mx8[:], score[:])
                    P.match_replace(sc2[:], mx8[:], score[:], -1e30)
                    P.max8(mx8[:], sc2[:])
                    P.ts(thr[:], mx8[:, 7:8], -1e5, ALU.max)
                    P.ts(selm[:, g, :], score[:], thr[:, 0:1], ALU.is_ge)
                chk(6)
                for g in range(0 if "sel" in SKIP else 2):
                    lst = []
                    for kt in range(n + 1):
                        lst.append(dict(kT=KsT[g * 64:(g + 1) * 64, kt * 128:(kt + 1) * 128], va=Vs[:, kt, g, :], selkt=kt,
                                        causal=(kt == n)))
                    attend(1, g, lst)
                chk(7)
                for g in range(0 if "win" in SKIP else 2):
                    lst = []
                    kts = list(range(max(0, n - 4), n + 1))
                    kts = [k_ for k_ in kts if k_ not in (n, n - 4)] + [k_ for k_ in kts if k_ in (n - 4, n)]
                    for kt in kts:
                        if kt == n:
                            mk = C["tri"][:, 0, :]
                        elif kt == n - 4:
                            mk = C["tri"][:, 1, :]
                        else:
                            mk = None
                        lst.append(dict(kT=KwT[g * 64:(g + 1) * 64, (kt % 8) * 128:(kt % 8 + 1) * 128], va=Vw[:, kt % 8, g, :], mk=mk))
                    attend(2, g, lst)
                P.act(gates[:], Ptok[:, 1280:1304], AF.Sigmoid)
                P.ts(coef[:], Otok[:, :, :, 64], 1e-30, ALU.max)
                P.recip(coef[:], coef[:])
                P.tt(coef[:], coef[:], gates[:].r("p (h b) -> p b h", b=3), ALU.mult)
                ov = onsa[:].r("p (h d) -> p h d", h=8)
                tv = zr[:].r("p (h d) -> p h d", h=8)
                P.tt(ov, Otok[:, 0, :, 0:64], coef[:, 0, :].r("p (h o) -> p h o", o=1).b([128, 8, 64]), ALU.mult)
                for br in (1, 2):
                    P.tt(tv, Otok[:, br, :, 0:64], coef[:, br, :].r("p (h o) -> p h o", o=1).b([128, 8, 64]), ALU.mult)
                    P.tt(ov, ov, tv, ALU.add)
                pz = nps()
                for j in range(4):
                    P.tr(pz[:, j * 128:(j + 1) * 128], onsa[:, j * 128:(j + 1) * 128], ident[:])
                P.cp(mixT[:, 0:4, :], pz[:].r("p (c t) -> p c t", c=4), eng="act")

                chk(8)
                def rg_chain():
                    for c2 in range(0 if "rg" in SKIP else 2):
                        pr = rgp[:, l, c2, :]
                        P.cp(xcat[:, c2, 3:131], Ffm[:, c2, :], eng="pool"); yield
                        P.ts(xc[:, c2, :], xcat[:, c2, 0:128], pr[:, 0:1], ALU.mult, pr[:, 4:5], ALU.add); yield
                        for k in range(1, 4):
                            P.stt(xc[:, c2, :], xcat[:, c2, k:k + 128], pr[:, k:k + 1], xc[:, c2, :], ALU.mult, ALU.add); yield
                        P.cp(xcb[:, c2, :], xc[:, c2, :], eng="pool"); yield
                        pz = nps()
                        P.mm(pz[:, 0:128], BD[:, c2, 0, :], xcb[:, c2, :])
                        P.act(rg_r[:], pz[:, 0:128], AF.Sigmoid, bias=pr[:, 5:6]); yield
                        pz = nps()
                        P.mm(pz[:, 0:128], BD[:, c2, 1, :], xcb[:, c2, :])
                        P.act(rg_i[:], pz[:, 0:128], AF.Sigmoid, bias=pr[:, 6:7]); yield
                        P.act(rg_a[:], rg_r[:], AF.Exp, scale=rgc[:, l, c2, :]); yield
                        P.act(rg_b[:], rg_r[:], AF.Exp, scale=rgc2[:, l, c2, :]); yield
                        P.ts(rg_b[:], rg_b[:], -1.0, ALU.mult, 1.0, ALU.add); yield
                        P.ts(rg_b[:], rg_b[:], 0.0, ALU.max); yield
                        P.act(rg_b[:], rg_b[:], AF.Sqrt); yield
                        P.tt(rg_i[:], rg_i[:], xc[:, c2, :], ALU.mult); yield
                        P.tt(rg_b[:], rg_b[:], rg_i[:], ALU.mult); yield
                        P.scan(rg_h[:, c2, :], rg_a[:], rg_b[:], hst[:, c2, :], ALU.mult, ALU.add); yield
                        P.cp(hst[:, c2, :], rg_h[:, c2, 127:128], eng="pool"); yield
                        gx = Ffm[:, 2 + c2, :]
                        P.tt(gl[:], gx, gx, ALU.mult); yield
                        P.ts(gl[:], gl[:], 0.044715, ALU.mult, 1.0, ALU.add); yield
                        P.tt(gl[:], gl[:], gx, ALU.mult); yield
                        P.act(gl[:], gl[:], AF.Sigmoid, scale=1.5957691216); yield
                        P.tt(gl[:], gl[:], gx, ALU.mult); yield
                        P.tt(mixT[:, 4 + c2, :], gl[:], rg_h[:, c2, :], ALU.mult); yield
                        P.cp(xcat[:, c2, 0:3], xcat[:, c2, 128:131], eng="pool"); yield

                def hg_chain():
                    if "hg" in SKIP:
                        return
                    hq = Ffm[:, 4:6, :]
                    hf = Ffm[:, 6:8, :]
                    hgv = Ffm[:, 8:10, :]
                    P.act(f2[:], hf, AF.Sigmoid); yield
                    P.tt(f2[:], f2[:], oml[:, :, l:l + 1].b([128, 2, 128]), ALU.mult); yield
                    P.tt(f2[:], f2[:], lb[:, :, l:l + 1].b([128, 2, 128]), ALU.add); yield
                    P.act(g2[:], f2[:], AF.Ln); yield
                    P.ts(k2[:], f2[:], -1.0, ALU.mult, 1.0, ALU.add); yield
                    P.act(qs2[:], hq, AF.Sigmoid); yield
                    P.tt(qs2[:], qs2[:], hq, ALU.mult); yield
                    for c2 in range(2):
                        P.scan(bc2[:, c2, :], ones_f[:, :], g2[:, c2, :], 0.0, ALU.mult, ALU.add); yield
                    bcv = bc2[:].r("p c (h t) -> p c h t", h=2)
                    d1 = g2[:].r("p c (h t) -> p c h t", h=2)
                    d2 = f2[:].r("p c (h t) -> p c h t", h=2)
                    P.tt(d1, bcv, bcv[:, :, :, 31:32].b([128, 2, 2, 64]), ALU.subtract); yield
                    P.tt(d2, bcv[:, :, :, 63:64].b([128, 2, 2, 64]), bcv, ALU.subtract); yield
                    P.act(ebl2[:], bcv[:, :, :, 63], AF.Exp); yield
                    P.act(exA[:], bc2[:], AF.Exp); yield
                    P.tt(qE[:], qs2[:], exA[:], ALU.mult); yield
                    P.act(exB[:], g2[:], AF.Exp); yield
                    P.tt(qe[:], qs2[:], exB[:], ALU.mult); yield
                    P.act(exA[:], g2[:], AF.Exp, scale=-1.0); yield
                    P.tt(ke[:], k2[:], exA[:], ALU.mult); yield
                    P.act(exB[:], f2[:], AF.Exp); yield
                    P.tt(kdT[:], k2[:], exB[:], ALU.mult); yield
                    for c2 in range(2):
                        pz = nps()
                        P.tr(pz[:, 0:128], kdT[:, c2, :], ident[:])
                        P.cp(kdtok[:, c2, :], pz[:, 0:128], eng="act"); yield
                    for c2 in range(2):
                        for ch in range(2):
                            sl = slice(ch * 64, (ch + 1) * 64)
                            pz = nps()
                            for hh in range(2):
                                hp = slice(hh * 64, (hh + 1) * 64)
                                P.mm(pz[sl, hh * 64:(hh + 1) * 64], ke[hp, c2, sl], qe[hp, c2, sl])
                            P.tt(Am[sl, :, :], pz[sl, 0:128].r("p (h t) -> p h t", h=2),
                                 C["tri64"][sl, :].r("p (o t) -> p o t", o=1).b([64, 2, 64]), ALU.mult); yield
                            pz2 = nps()
                            for hh in range(2):
                                hp = slice(hh * 64, (hh + 1) * 64)
                                h = c2 * 2 + hh
                                P.mm(pz2[hp, 0:64], vtok[sl, h * 64:(h + 1) * 64], Am[sl, hh, :], start=True, stop=False)
                                P.mm(pz2[hp, 0:64], Sbf[hp, c2, :], qE[hp, c2, sl], start=False, stop=True)
                                P.mm(pz2[hp, 64:128], kdtok[sl, c2, hp], vtok[sl, h * 64:(h + 1) * 64])
                            P.cp(oT[:, c2, sl], pz2[:, 0:64], eng="act"); yield
                            P.stt(S32[:, c2, :], S32[:, c2, :], ebl2[:, c2, ch:ch + 1], pz2[:, 64:128], ALU.mult, ALU.add); yield
                            P.cp(Sbf[:, c2, :], S32[:, c2, :], eng="pool"); yield
                    P.act(osq2[:], oT[:], AF.Square); yield
                    pz = nps()
                    for c2 in range(2):
                        P.mm(pz[:, c2 * 128:(c2 + 1) * 128], C["blockones"][:], osq2[:, c2, :])
                    P.act(exB[:], pz[:, 0:256].r("p (c t) -> p c t", c=2), AF.Sqrt, bias=EPS, scale=1.0 / 64); yield
                    P.recip(exB[:], exB[:]); yield
                    P.tt(exB[:], exB[:], oT[:], ALU.mult); yield
                    P.tt(exB[:], exB[:], hgp[:, :, 1, l:l + 1].b([128, 2, 128]), ALU.mult); yield
                    P.act(exA[:], hgv, AF.Sigmoid); yield
                    P.tt(exA[:], exA[:], hgv, ALU.mult); yield
                    P.tt(mixT[:, 6:8, :], exB[:], exA[:], ALU.mult); yield

                gens = [rg_chain(), hg_chain()]
                while gens:
                    for gen in list(gens):
                        try:
                            next(gen)
                        except StopIteration:
                            gens.remove(gen)
                if n == NT - 1:
                    for c2 in range(2):
                        P.dma(o_rgh[l, c2 * 128:(c2 + 1) * 128].r("(p o) -> p o", o=1), hst[:, c2, :])
                    pz = nps()
                    for c in range(8):
                        P.mm(pz[:, 0:256], yT[:, c, :], Win[:, c, IN_OFF["rgx"]:IN_OFF["rgx"] + 256], start=(c == 0),
                             stop=(c == 7))
                    P.cp(score[:, 0:128], pz[:, 0:128], eng="act")
                    P.cp(sc2[:, 0:128], pz[:, 128:256], eng="act")
                    P.dma(o_rgc[l, :, 0:128], score[125:128, 0:128])
                    P.dma(o_rgc[l, :, 128:256], sc2[125:128, 0:128])

                chk(9)
                if n == NT - 1:
                    for c2 in range(2):
                        P.dma(o_hgs[l, c2 * 2:c2 * 2 + 2].r("h k v -> (h k) v"), S32[:, c2, :])

                chk(10)
                for dc in range(8):
                    pz = nps()
                    for k in range(8):
                        P.mm(pz[:, 0:128], Wout[:, k, dc * 128:(dc + 1) * 128], mixT[:, k, :], start=(k == 0), stop=(k == 7))
                    P.tt(xT[:, dc, :], xT[:, dc, :], pz[:, 0:128], ALU.add)
                P.dma(xs_d[:, t0:t0 + 128].r("(c p) t -> p c t", p=128), xT[:])
                chk(11)
            pes.close()
            P.barrier()
            chk(12)
            if NS:
                des = contextlib.ExitStack()
                cur[0] = des
                decode_mixer(l, lsb, cur, Wout, W1, W2, b2r, cb1, BD)
                des.close()

        chk(13)
        with contextlib.ExitStack() as les:
            def lsb(name, shape, dt=F32):
                return T("s_" + name + "_f%d" % l, les.enter_context(nc.sbuf_tensor("s_" + name + "_f%d" % l, list(shape), dt)))
            P.barrier()
            Wup = lsb("Wup", [128, 8, 4 * D], BF16)
            Wdn = lsb("Wdn", [128, 32, D], BF16)
            for c in range(8):
                P.dma(Wup[:, c, :], w_up_d[l, c * 128:(c + 1) * 128, :], eng="pool", max_dma_last_dim=4096)
            for c in range(32):
                P.dma(Wdn[:, c, :], w_dn_d[l, c * 128:(c + 1) * 128, :], eng="pool", max_dma_last_dim=4096)
            FB = 256
            xB = lsb("xB", [128, 8, FB])
            yB = lsb("yB", [128, 8, FB], BF16)
            rsB = lsb("rsB", [128, FB])
            HT = lsb("HT", [128, 32, FB], BF16)
            tB = V(HT.t[:, 0:16, :].rearrange("p a b -> p (a b)").bitcast(F32).rearrange("p (c t) -> p c t", c=8), HT[:].key)
            sqB = V(HT.t[:, 16:24, :], HT[:].key)
            hr = lsb("hr", [128, FB])
            ytok = lsb("ytok", [128, D])
            for blk in range(Tn // FB):
                t0 = blk * FB
                P.dma(xB[:], xs_d[:, t0:t0 + FB].r("(c p) t -> p c t", p=128))

                def norm(gv, outv):
                    P.act(sqB, xB[:], AF.Square)
                    pz = nps()
                    for c in range(8):
                        P.mm(pz[:, :FB], ones_bf[:], sqB[:, c, :], start=(c == 0), stop=(c == 7))
                    P.act(rsB[:], pz[:, :FB], AF.Sqrt, bias=EPS, scale=1.0 / D)
                    P.recip(rsB[:], rsB[:])
                    P.tt(tB, xB[:], rsB[:].r("p (o t) -> p o t", o=1).b([128, 8, FB]), ALU.mult)
                    P.tt(outv, tB, gv.r("p (c o) -> p c o", o=1).b([128, 8, FB]), ALU.mult)
                norm(normT[:, l, 1, :], yB[:])
                for f in range(32):
                    pz = nps()
                    for c in range(8):
                        P.mm(pz[:, :FB], Wup[:, c, f * 128:(f + 1) * 128], yB[:, c, :], start=(c == 0), stop=(c == 7))
                    P.act(hr[:], pz[:, :FB], AF.Relu)
                    P.tt(HT[:, f, :], hr[:], hr[:], ALU.mult, eng="pool")
                for dc in range(8):
                    pz = nps()
                    for f in range(32):
                        P.mm(pz[:, :FB], Wdn[:, f, dc * 128:(dc + 1) * 128], HT[:, f, :], start=(f == 0), stop=(f == 31))
                    P.tt(xB[:, dc, :], xB[:, dc, :], pz[:, :FB], ALU.add)
                if l < L - 1:
                    P.dma(xs_d[:, t0:t0 + FB].r("(c p) t -> p c t", p=128), xB[:])
                else:
                    norm(fnormT[:], tB)
                    for tt_ in range(FB // 128):
                        for c in range(8):
                            pz = nps()
                            P.tr(pz[:, 0:128], tB[:, c, tt_ * 128:(tt_ + 1) * 128], ident[:])
                            P.cp(ytok[:, c * 128:(c + 1) * 128], pz[:, 0:128], eng=("act" if c % 2 else "dve"))
                        P.dma(y_out[t0 + tt_ * 128:t0 + (tt_ + 1) * 128, :], ytok[:])
            if NS:
                decode_ffn(l, lsb, Wup, Wdn)


FM_COLS = [IN_OFF["rgx"], IN_OFF["rgx"] + 128, IN_OFF["rgg"], IN_OFF["rgg"] + 128,
           IN_OFF["hq"], IN_OFF["hq"] + 128, IN_OFF["hf"], IN_OFF["hf"] + 128,
           IN_OFF["hg"], IN_OFF["hg"] + 128]


def rope_tok(src_t, dst_t, cosv, sinv, ra, rb, npart):
    for (h0, nh) in ((0, 8), (12, 2), (16, 2)):
        src = src_t[:, h0 * 64:(h0 + nh) * 64].r("p (h d) -> p h d", d=64)
        dstv = dst_t[:, h0 * 64:(h0 + nh) * 64].r("p (h d) -> p h d", d=64)
        cb = cosv.r("p (o e) -> p o e", o=1).b([npart, nh, 8])
        sbv = sinv.r("p (o e) -> p o e", o=1).b([npart, nh, 8])
        x1, x2 = src[:, :, 0:8], src[:, :, 8:16]
        P.tt(ra[:, :nh, :], x1, cb, ALU.mult)
        P.tt(rb[:, :nh, :], x2, sbv, ALU.mult)
        P.tt(dstv[:, :, 0:8], ra[:, :nh, :], rb[:, :nh, :], ALU.subtract)
        P.tt(ra[:, :nh, :], x2, cb, ALU.mult)
        P.tt(rb[:, :nh, :], x1, sbv, ALU.mult)
        P.tt(dstv[:, :, 8:16], ra[:, :nh, :], rb[:, :nh, :], ALU.add)


def gelu_tanh(out, x, tmp):
    P.tt(tmp, x, x, ALU.mult)
    P.ts(tmp, tmp, 0.044715, ALU.mult, 1.0, ALU.add)
    P.tt(tmp, tmp, x, ALU.mult)
    P.act(tmp, tmp, AF.Sigmoid, scale=1.5957691216)
    P.tt(out, tmp, x, ALU.mult)


def decode_mixer(l, lsb, cur, Wout, W1, W2, b2r, cb1, BD):
    NCB = PAST // 16 - 1
    NSB = PAST // 64 + 1
    WT = WB // 128
    ysT = lsb("ysT", [128, 8, NS], BF16)
    ntmp = lsb("ntmp", [128, 8, NS])
    rmsnorm_T(xsT[:], normT[:, l, 0, :], ysT[:], NS, ntmp[:])
    Ps = lsb("Ps", [NS, D_IN])
    Fs = lsb("Fs", [128, 10, NS])
    outer = cur[0]
    wes = contextlib.ExitStack()
    cur[0] = wes
    Win = lsb("WinD", [128, 8, D_IN], BF16)
    for c in range(8):
        P.dma(Win[:, c, :], w_in_d[l, c * 128:(c + 1) * 128, :], eng="pool", max_dma_last_dim=4096)
    for c0 in range(0, D_IN, 512):
        cw = min(512, D_IN - c0)
        pz = nps()
        for c in range(8):
            P.mm(pz[0:NS, :cw], ysT[:, c, :], Win[:, c, c0:c0 + cw], start=(c == 0), stop=(c == 7))
        P.cp(Ps[:, c0:c0 + cw], pz[0:NS, :cw], eng="act")
    for i, c0 in enumerate(FM_COLS):
        pz = nps()
        for c in range(8):
            P.mm(pz[:, :NS], Win[:, c, c0:c0 + 128], ysT[:, c, :], start=(c == 0), stop=(c == 7))
        P.cp(Fs[:, i, :], pz[:, :NS], eng="dve")
    wes.close()
    cur[0] = outer
    P.barrier()
    Rs = lsb("Rs", [NS, 1280])
    ras = lsb("ras", [NS, 8, 8])
    rbs = lsb("rbs", [NS, 8, 8])
    P.cp(Rs[:], Ps[:, 0:1280], eng="pool")
    rope_tok(Ps, Rs, C["ropeS"][:, 0, :], C["ropeS"][:, 1, :], ras, rbs, NS)
    P.dma(os_cmp[l], Rs[:, 512:768])
    P.dma(os_sel[l], Rs[:, 768:1024])
    P.dma(os_win.s("a")[l, :, 0:WB - 1, :], cwin_d[l, :, 1:WB, :])
    P.dma(os_win.s("b")[l, :, WB - 1, :], Rs[:, 1024:1280])
    QsT = lsb("QsT", [128, 4, NS], BF16)
    pz = nps()
    for j in range(4):
        P.tr(pz[:, j * NS:(j + 1) * NS], Rs[:, j * 128:(j + 1) * 128], ident[0:NS, 0:NS])
    P.cp(QsT[:].r("p j s -> p (j s)"), pz[:, 0:4 * NS], eng="act")

    idxl = lsb("idxl", [128, NS * NPG], I32)
    P.ts(idxl[:], idxt[:], float(l * NPOOL * 128), ALU.add)
    OTall = lsb("OTall", [65, 3, 2, 4, NS])
    impTall = lsb("impTall", [40, 2 * NS])
    P.memset(impTall[:], 0.0)
    pg = [lsb("pg%d" % i, [128, NPG, 256]) for i in range(2)]
    rawS = lsb("rawS", [128, 2, 2, PAST], BF16)
    pgbf = lsb("pgbf", [128, 256], BF16)
    hp_ = lsb("hp_", [128, 2 * NCB])
    hu_ = lsb("hu_", [128, 2 * NCB])
    hglS = lsb("hglS", [128, 2, 2, 2 * NCB], BF16)
    cblkS = lsb("cblkS", [NCB, 2, 2, 64])
    cblkrS = lsb("cblkrS", [NCB, 2, 64])
    rcs = lsb("rcs", [NCB, 2, 8])
    rds = lsb("rds", [NCB, 2, 8])
    KcS = lsb("KcS", [128, NCB], BF16)
    cvS = lsb("cvS", [NCB, 2, 65], BF16)
    P.memset(cvS[:, :, 64:65], 1.0)
    ETs = lsb("ETs", [128, 4 * max(NPG, 4)], BF16)
    PTs = lsb("PTs", [128, 4 * max(NPG, 4)], BF16)
    zs = lsb("zs", [40, 4])
    zq = lsb("zq", [40, 4])
    def fetch1(s_):
        for k in range(NPG):
            P.idma(pg[s_ % 2][:, k, :], ccmp_d[:], idxl[:, s_ * NPG + k:s_ * NPG + k + 1])

    fetch1(0)
    for s in range(NS):
        pgb = pg[s % 2]
        if s + 1 < NS:
            fetch1(s + 1)
        for k in range(NPG):
            P.cp(pgbf[:], pgb[:, k, :], eng="pool")
            pz = nps()
            for kv in range(2):
                for g in range(2):
                    cs_ = pgbf[:, kv * 128 + g * 64:kv * 128 + (g + 1) * 64]
                    for h2 in range(2):
                        P.mm(pz[h2 * 64:(h2 + 1) * 64, (kv * 2 + g) * 128:(kv * 2 + g + 1) * 128], cs_, ident_bf[:])
            P.cp(rawS[0:64, :, :, k * 128:(k + 1) * 128], pz[0:64, :].r("p (k g t) -> p k g t", k=2, g=2), eng="act")
            if k == 0:
                P.cp(rawS[64:128, :, :, 0:127], pz[64:128, :].r("p (k g t) -> p k g t", k=2, g=2)[:, :, :, 1:128], eng="dve")
            else:
                P.cp(rawS[64:128, :, :, k * 128 - 1:(k + 1) * 128 - 1], pz[64:128, :].r("p (k g t) -> p k g t", k=2, g=2), eng="dve")
        for kv in range(2):
            for hc in range(2):
                pz = nps()
                for g in range(2):
                    for s16 in range(16):
                        rhs = rawS[:, kv, g, 2 * s16:2 * s16 + 16 * (NCB - 1) + 1:16]
                        P.mm(pz[:, g * NCB:(g + 1) * NCB], W1[:, kv, s16, hc * 128:(hc + 1) * 128],
                             rhs, start=(s16 == 0), stop=(s16 == 15))
                P.ts(hp_[:], pz[:, 0:2 * NCB], cb1[:, kv, hc:hc + 1], ALU.add)
                gelu_tanh(hglS[:, kv, hc, :], hp_[:], hu_[:])
        pz = nps()
        for kv in range(2):
            for g in range(2):
                o_ = pz[0:NCB, (kv * 2 + g) * 64:(kv * 2 + g + 1) * 64]
                for hc in range(2):
                    P.mm(o_, hglS[:, kv, hc, g * NCB:(g + 1) * NCB], W2[:, kv, hc, :], start=(hc == 0), stop=False)
                P.mm(o_, ones_bf[0:1, 0:NCB], b2r[0:1, kv, :], start=False, stop=True)
        P.cp(cblkS[:].r("b k g d -> b (k g d)"), pz[0:NCB, 0:256], eng="act")
        cC = C["ropeCS"][:, 0, :].r("p (o e) -> p o e", o=1).b([NCB, 2, 8])
        sC = C["ropeCS"][:, 1, :].r("p (o e) -> p o e", o=1).b([NCB, 2, 8])
        P.cp(cblkrS[:], cblkS[:, 0, :, :], eng="pool")
        x1, x2 = cblkS[:, 0, :, 0:8], cblkS[:, 0, :, 8:16]
        P.tt(rcs[:], x1, cC, ALU.mult)
        P.tt(rds[:], x2, sC, ALU.mult)
        P.tt(cblkrS[:, :, 0:8], rcs[:], rds[:], ALU.subtract)
        P.tt(rcs[:], x2, cC, ALU.mult)
        P.tt(rds[:], x1, sC, ALU.mult)
        P.tt(cblkrS[:, :, 8:16], rcs[:], rds[:], ALU.add)
        pz = nps()
        P.tr(pz[:, 0:NCB], cblkrS[:].r("b g d -> b (g d)"), ident[0:NCB, 0:NCB])
        P.cp(KcS[:], pz[:, 0:NCB], eng="act")
        P.cp(cvS[:, :, 0:64], cblkS[:, 1, :, :], eng="pool")
        for g in range(2):
            P.mm(pS[0][0:NCB, 0:4], KcS[g * 64:(g + 1) * 64, :], QsT[g * 64:(g + 1) * 64, :, s])
            P.act(ETs[0:NCB, 0:4], pS[0][0:NCB, 0:4], AF.Exp, scale=0.125)
            P.mm(pO[0:65, 0:4], cvS[:, g, :], ETs[0:NCB, 0:4])
            P.mm(pI[0:NSB, 0:4], C["smapS"][:, :], ETs[0:NCB, 0:4])
            P.mm(pM[0:NSB, 0:4], ones_bf[0:NCB, 0:NSB], ETs[0:NCB, 0:4])
            P.cp(OTall[:, 0, g, :, s], pO[0:65, 0:4], eng="act")
            P.ts(zs[0:NSB, :], pM[0:NSB, 0:4], 1e-30, ALU.max)
            P.recip(zs[0:NSB, :], zs[0:NSB, :])
            P.tt(zq[0:NSB, :], pI[0:NSB, 0:4], zs[0:NSB, :], ALU.mult)
            P.reduce(impTall[0:NSB, 2 * s + g:2 * s + g + 1], zq[0:NSB, :], ALU.add)
    scoreS = lsb("scoreS", [2 * NS, 40])
    sc2S = lsb("sc2S", [2 * NS, 40])
    mx8S = lsb("mx8S", [2 * NS, 8])
    thrS = lsb("thrS", [2 * NS, 1])
    selmS = lsb("selmS", [2 * NS, 40])
    selmST = lsb("selmST", [40, 2 * NS], BF16)
    pz = nps()
    P.tr(pz[0:2 * NS, 0:40], impTall[:], ident[0:40, 0:40])
    P.tt(scoreS[:], pz[0:2 * NS, 0:40], C["selMS"][:], ALU.mult)
    P.tt(scoreS[:], scoreS[:], C["selAS"][:], ALU.add)
    P.max8(mx8S[:], scoreS[:])
    P.match_replace(sc2S[:], mx8S[:], scoreS[:], -1e30)
    P.max8(mx8S[:], sc2S[:])
    P.ts(thrS[:], mx8S[:, 7:8], -1e5, ALU.max)
    P.ts(selmS[:], scoreS[:], thrS[:, 0:1], ALU.is_ge)
    pz = nps()
    P.tr(pz[0:40, 0:2 * NS], selmS[:], ident[0:2 * NS, 0:2 * NS])
    P.cp(selmST[:], pz[0:40, 0:2 * NS], eng="act")
    KsS = lsb("KsS", [128, PAST], BF16)
    VsS = lsb("VsS", [128, NPG, 2, 65], BF16)
    P.memset(VsS[:, :, :, 64:65], 1.0)
    wbuf = lsb("wbuf", [128, WT, 256])
    KwS = lsb("KwS", [128, WB], BF16)
    VwS = lsb("VwS", [128, WT, 2, 65], BF16)
    P.memset(VwS[:, :, :, 64:65], 1.0)
    def fetch2(s_):
        for k in range(NPG):
            P.idma(pg[s_ % 2][:, k, :], csel_d[:], idxl[:, s_ * NPG + k:s_ * NPG + k + 1])

    fetch2(0)
    for s in range(NS):
        pgb = pg[s % 2]
        if s + 1 < NS:
            fetch2(s + 1)
        P.dma(wbuf[:], cwin_d[l, s].r("(t p) c -> p t c", p=128))
        for k in range(NPG):
            pz = nps()
            P.tr(pz[:, 0:128], pgb[:, k, 0:128], ident[:])
            P.cp(KsS[:, k * 128:(k + 1) * 128], pz[:, 0:128], eng=("act" if k % 2 else "dve"))
            P.cp(VsS[:, k, :, 0:64], pgb[:, k, 128:256].r("p (g d) -> p g d", g=2), eng="pool")
        for t in range(WT):
            pz = nps()
            P.tr(pz[:, 0:128], wbuf[:, t, 0:128], ident[:])
            P.cp(KwS[:, t * 128:(t + 1) * 128], pz[:, 0:128], eng=("act" if t % 2 else "dve"))
            P.cp(VwS[:, t, :, 0:64], wbuf[:, t, 128:256].r("p (g d) -> p g d", g=2), eng="pool")
        for g in range(2):
            sp = pS[g]
            for k in range(NPG):
                P.mm(sp[:, k * 4:(k + 1) * 4], KsS[g * 64:(g + 1) * 64, k * 128:(k + 1) * 128], QsT[g * 64:(g + 1) * 64, :, s])
            for k in range(NPG):
                P.mm(pM[:, k:k + 1], C["DS"][:, k * 128:(k + 1) * 128], selmST[:, 2 * s + g:2 * s + g + 1])
            P.act(ETs[:, 0:4 * NPG], sp[:, 0:4 * NPG], AF.Exp, scale=0.125)
            P.tt(PTs[:, 0:4 * NPG].r("p (k h) -> p k h", h=4), ETs[:, 0:4 * NPG].r("p (k h) -> p k h", h=4),
                 pM[:, 0:NPG].r("p (k o) -> p k o", o=1).b([128, NPG, 4]), ALU.mult)
            for k in range(NPG):
                P.mm(pO[0:65, 0:4], VsS[:, k, g, :], PTs[:, k * 4:(k + 1) * 4], start=(k == 0), stop=(k == NPG - 1))
            P.cp(OTall[:, 1, g, :, s], pO[0:65, 0:4], eng="act")
            for t in range(WT):
                P.mm(sp[:, t * 4:(t + 1) * 4], KwS[g * 64:(g + 1) * 64, t * 128:(t + 1) * 128], QsT[g * 64:(g + 1) * 64, :, s])
            P.act(ETs[:, 0:4 * WT], sp[:, 0:4 * WT], AF.Exp, scale=0.125)
            if WB == 512:
                P.memset(ETs[0:1, 0:4], 0.0, eng="dve")
            for t in range(WT):
                P.mm(pO[0:65, 0:4], VwS[:, t, g, :], ETs[:, t * 4:(t + 1) * 4], start=(t == 0), stop=(t == WT - 1))
            P.cp(OTall[:, 2, g, :, s], pO[0:65, 0:4], eng="act")
    OtokS = lsb("OtokS", [NS, 3, 8, 65])
    for br in range(3):
        for g in range(2):
            pz = nps()
            for j in range(4):
                P.tr(pz[0:NS, j * 65:(j + 1) * 65], OTall[:, br, g, j, :], ident[0:65, 0:65])
            P.cp(OtokS[:, br, g * 4:(g + 1) * 4, :], pz[0:NS, 0:260].r("p (j e) -> p j e", e=65), eng="act")
    prod = lsb("prod", [NS, 4, 2, 64])
    dots = lsb("dots", [NS, 4, 2])
    enew = lsb("enew", [NS, 4, 2])
    tmpo = lsb("tmpo", [NS, 2, 4, 64])
    qv = Rs[:, 0:512].r("p (j g d) -> p j g d", j=4, g=2)
    for br, kc0, vc0 in ((1, 768, 896), (2, 1024, 1152)):
        kn = Rs[:, kc0:kc0 + 128].r("p (o g d) -> p o g d", o=1, g=2).b([NS, 4, 2, 64])
        P.tt(prod[:], qv, kn, ALU.mult)
        P.reduce(dots[:], prod[:], ALU.add)
        P.act(enew[:], dots[:], AF.Exp, scale=0.125)
        ev = enew[:].r("p j g -> p g j")
        vn = Rs[:, vc0:vc0 + 128].r("p (g o d) -> p g o d", g=2, o=1).b([NS, 2, 4, 64])
        P.tt(tmpo[:], vn, ev.r("p g (j o) -> p g j o", o=1).b([NS, 2, 4, 64]), ALU.mult)
        ob = OtokS[:, br, :, 0:64].r("p (g j) d -> p g j d", g=2)
        P.tt(ob, ob, tmpo[:], ALU.add)
        zb = OtokS[:, br, :, 64].r("p (g j) -> p g j", g=2)
        P.tt(zb, zb, ev, ALU.add)
    gatesS = lsb("gatesS", [NS, 24])
    coefS = lsb("coefS", [NS, 3, 8])
    onsaS = lsb("onsaS", [NS, 512])
    mixTs = lsb("mixTs", [128, 8, NS], BF16)
    P.act(gatesS[:], Ps[:, 1280:1304], AF.Sigmoid)
    P.ts(coefS[:], OtokS[:, :, :, 64], 1e-30, ALU.max)
    P.recip(coefS[:], coefS[:])
    P.tt(coefS[:], coefS[:], gatesS[:].r("p (h b) -> p b h", b=3), ALU.mult)
    for h in range(8):
        ov = onsaS[:, h * 64:(h + 1) * 64]
        P.ts(ov, OtokS[:, 0, h, 0:64], coefS[:, 0, h:h + 1], ALU.mult)
        P.stt(ov, OtokS[:, 1, h, 0:64], coefS[:, 1, h:h + 1], ov, ALU.mult, ALU.add)
        P.stt(ov, OtokS[:, 2, h, 0:64], coefS[:, 2, h:h + 1], ov, ALU.mult, ALU.add)
    pz = nps()
    for j in range(4):
        P.tr(pz[:, j * NS:(j + 1) * NS], onsaS[:, j * 128:(j + 1) * 128], ident[0:NS, 0:NS])
    P.cp(mixTs[:, 0:4, :].r("p c s -> p (c s)"), pz[:, 0:4 * NS], eng="act")
    rgct = lsb("rgct", [NS, 3, 256])
    rght = lsb("rght", [NS, 256])
    P.dma(rgct[:], srgc_d[l])
    P.dma(rght[:], srgh_d[l])
    P.dma(os_rgc.s("a")[l, :, 0:2, :], srgc_d[l, :, 1:3, :])
    P.dma(os_rgc.s("b")[l, :, 2, :], Ps[:, IN_OFF["rgx"]:IN_OFF["rgx"] + 256])
    xcs = lsb("xcs", [128, 2, 4, NS])
    h0T = lsb("h0T", [128, 2, NS])
    xcd = lsb("xcd", [128, NS])
    xcdb = lsb("xcdb", [128, NS], BF16)
    r_ = lsb("r_", [128, NS])
    i_ = lsb("i_", [128, NS])
    a_ = lsb("a_", [128, NS])
    b_ = lsb("b_", [128, NS])
    hT_ = lsb("hT_", [128, 2, NS])
    g1 = lsb("g1", [128, NS])
    g2 = lsb("g2", [128, NS])
    htok = lsb("htok", [NS, 256])
    for c2 in range(2):
        pz = nps()
        for k in range(3):
            P.tr(pz[:, k * NS:(k + 1) * NS], rgct[:, k, c2 * 128:(c2 + 1) * 128], ident[0:NS, 0:NS])
        P.tr(pz[:, 3 * NS:4 * NS], rght[:, c2 * 128:(c2 + 1) * 128], ident[0:NS, 0:NS])
        P.cp(xcs[:, c2, 0:3, :].r("p k s -> p (k s)"), pz[:, 0:3 * NS], eng="act")
        P.cp(h0T[:, c2, :], pz[:, 3 * NS:4 * NS], eng="act")
        P.cp(xcs[:, c2, 3, :], Fs[:, c2, :], eng="pool")
        pr = rgp[:, l, c2, :]
        P.ts(xcd[:], xcs[:, c2, 0, :], pr[:, 0:1], ALU.mult, pr[:, 4:5], ALU.add)
        for k in range(1, 4):
            P.stt(xcd[:], xcs[:, c2, k, :], pr[:, k:k + 1], xcd[:], ALU.mult, ALU.add)
        P.cp(xcdb[:], xcd[:], eng="pool")
        pz = nps()
        P.mm(pz[:, 0:NS], BD[:, c2, 0, :], xcdb[:])
        P.act(r_[:], pz[:, 0:NS], AF.Sigmoid, bias=pr[:, 5:6])
        pz = nps()
        P.mm(pz[:, 0:NS], BD[:, c2, 1, :], xcdb[:])
        P.act(i_[:], pz[:, 0:NS], AF.Sigmoid, bias=pr[:, 6:7])
        P.act(a_[:], r_[:], AF.Exp, scale=rgc[:, l, c2, :])
        P.act(b_[:], r_[:], AF.Exp, scale=rgc2[:, l, c2, :])
        P.ts(b_[:], b_[:], -1.0, ALU.mult, 1.0, ALU.add)
        P.ts(b_[:], b_[:], 0.0, ALU.max)
        P.act(b_[:], b_[:], AF.Sqrt)
        P.tt(i_[:], i_[:], xcd[:], ALU.mult)
        P.tt(b_[:], b_[:], i_[:], ALU.mult)
        P.tt(a_[:], a_[:], h0T[:, c2, :], ALU.mult)
        P.tt(hT_[:, c2, :], a_[:], b_[:], ALU.add)
        gelu_tanh(g2[:], Fs[:, 2 + c2, :], g1[:])
        P.tt(mixTs[:, 4 + c2, :], g2[:], hT_[:, c2, :], ALU.mult)
        pz = nps()
        P.tr(pz[0:NS, 0:128], hT_[:, c2, :], ident[:])
        P.cp(htok[:, c2 * 128:(c2 + 1) * 128], pz[0:NS, 0:128], eng="act")
    P.dma(os_rgh[l], htok[:])
    Sst = lsb("Sst", [128, NS, 2, 64])
    for hh in range(2):
        P.dma(Sst[hh * 64:(hh + 1) * 64, :, :, :], shgs_d[l].r("s (c hh) k v -> hh k s c v", hh=2)[hh])
    fT = lsb("fT", [128, 2, NS])
    kT_ = lsb("kT_", [128, 2, NS])
    qT_ = lsb("qT_", [128, 2, NS])
    for c2 in range(2):
        P.act(fT[:, c2, :], Fs[:, 6 + c2, :], AF.Sigmoid)
        P.ts(fT[:, c2, :], fT[:, c2, :], oml[:, c2, l:l + 1], ALU.mult, lb[:, c2, l:l + 1], ALU.add)
        P.ts(kT_[:, c2, :], fT[:, c2, :], -1.0, ALU.mult, 1.0, ALU.add)
        P.act(qT_[:, c2, :], Fs[:, 4 + c2, :], AF.Sigmoid)
        P.tt(qT_[:, c2, :], qT_[:, c2, :], Fs[:, 4 + c2, :], ALU.mult)
    vdiag = lsb("vdiag", [NS, NS, 128])
    t2c = lsb("t2c", [128, 4, 2, 64])
    fb = fT[:].r("p c (s o) -> p s c o", o=1).b([128, NS, 2, 64])
    kb = kT_[:].r("p c (s o) -> p s c o", o=1).b([128, NS, 2, 64])
    P.tt(Sst[:], Sst[:], fb, ALU.mult)
    SPC = 4
    for hh in range(2):
        hp = slice(hh * 64, (hh + 1) * 64)
        v_hh = Ps[:, IN_OFF["hi"]:IN_OFF["hi"] + 256].r("p (c hh v) -> p hh c v", hh=2, v=64)[:, hh]
        P.tt(vdiag[:].r("p s (c v) -> p s c v", c=2), v_hh.r("p (o c) v -> p o c v", o=1).b([NS, NS, 2, 64]),
             ident[0:NS, 0:NS].r("p (s o t) -> p s o t", o=1, t=1).b([NS, NS, 2, 64]), ALU.mult)
        for q4 in range((NS + SPC - 1) // SPC):
            ns_ = min(SPC, NS - q4 * SPC)
            pz = nps()
            P.mm(pz[hp, 0:ns_ * 128], ones_f[0:NS, 0:64], vdiag[:, q4 * SPC:q4 * SPC + ns_, :].r("p s c -> p (s c)"))
            sl = slice(q4 * SPC, q4 * SPC + ns_)
            P.tt(t2c[hp, 0:ns_], pz[hp, 0:ns_ * 128].r("p (s c v) -> p s c v", s=ns_, c=2), kb[hp, sl, :, :], ALU.mult)
            P.tt(Sst[hp, sl, :, :], Sst[hp, sl, :, :], t2c[hp, 0:ns_], ALU.add)
    Snew = Sst
    for hh in range(2):
        P.dma(os_hgs[l].r("s (c hh) k v -> hh k s c v", hh=2)[hh], Snew[hh * 64:(hh + 1) * 64, :, :, :])
    pz = nps()
    for s in range(NS):
        for c2 in range(2):
            for hh in range(2):
                hp = slice(hh * 64, (hh + 1) * 64)
                P.mm(pz[hp, c2 * NS + s:c2 * NS + s + 1], Snew[hp, s, c2, :], qT_[hp, c2, s:s + 1])
    oTs = lsb("oTs", [128, 2, NS])
    P.cp(oTs[:].r("p c s -> p (c s)"), pz[:, 0:2 * NS], eng="act")
    osq_ = lsb("osq_", [128, NS], BF16)
    ors_ = lsb("ors_", [128, NS])
    for c2 in range(2):
        P.act(osq_[:], oTs[:, c2, :], AF.Square)
        pz = nps()
        P.mm(pz[:, 0:NS], C["blockones"][:], osq_[:])
        P.act(ors_[:], pz[:, 0:NS], AF.Sqrt, bias=EPS, scale=1.0 / 64)
        P.recip(ors_[:], ors_[:])
        P.tt(ors_[:], ors_[:], oTs[:, c2, :], ALU.mult)
        P.act(g1[:], Fs[:, 8 + c2, :], AF.Sigmoid)
        P.tt(g1[:], g1[:], Fs[:, 8 + c2, :], ALU.mult)
        P.stt(mixTs[:, 6 + c2, :], ors_[:], hgp[:, c2, 1, l:l + 1], g1[:], ALU.mult, ALU.mult)
    for dc in range(8):
        pz = nps()
        for k in range(8):
            P.mm(pz[:, 0:NS], Wout[:, k, dc * 128:(dc + 1) * 128], mixTs[:, k, :], start=(k == 0), stop=(k == 7))
        P.tt(xsT[:, dc, :], xsT[:, dc, :], pz[:, 0:NS], ALU.add)


def decode_ffn(l, lsb, Wup, Wdn):
    ysB = lsb("ysB", [128, 8, NS], BF16)
    HTs = lsb("HTs", [128, 32, NS], BF16)
    hrs = lsb("hrs", [128, NS])
    ntmpf = lsb("ntmpf", [128, 8, NS])
    rmsnorm_T(xsT[:], normT[:, l, 1, :], ysB[:], NS, ntmpf[:])
    for f in range(32):
        pz = nps()
        for c in range(8):
            P.mm(pz[:, :NS], Wup[:, c, f * 128:(f + 1) * 128], ysB[:, c, :], start=(c == 0), stop=(c == 7))
        P.act(hrs[:], pz[:, :NS], AF.Relu)
        P.tt(HTs[:, f, :], hrs[:], hrs[:], ALU.mult, eng="pool")
    for dc in range(8):
        pz = nps()
        for f in range(32):
            P.mm(pz[:, :NS], Wdn[:, f, dc * 128:(dc + 1) * 128], HTs[:, f, :], start=(f == 0), stop=(f == 31))
        P.tt(xsT[:, dc, :], xsT[:, dc, :], pz[:, :NS], ALU.add)
    if l == L - 1:
        yfs = lsb("yfs", [128, 8, NS])
        ystok = lsb("ystok", [NS, D])
        sqv = sq[:, :, :NS]
        P.act(sqv, xsT[:], AF.Square)
        pz = nps()
        for c in range(8):
            P.mm(pz[:, :NS], ones_bf[:], sq[:, c, :NS], start=(c == 0), stop=(c == 7))
        P.act(rstd[:, :NS], pz[:, :NS], AF.Sqrt, bias=EPS, scale=1.0 / D)
        P.recip(rstd[:, :NS], rstd[:, :NS])
        P.tt(yfs[:], xsT[:], rstd[:, :NS].r("p (o t) -> p o t", o=1).b([128, 8, NS]), ALU.mult)
        P.tt(yfs[:], yfs[:], fnormT[:].r("p (c o) -> p c o", o=1).b([128, 8, NS]), ALU.mult)
        for c in range(8):
            pz = nps()
            P.tr(pz[0:NS, 0:128], yfs[:, c, :], ident[:])
            P.cp(ystok[:, c * 128:(c + 1) * 128], pz[0:NS, 0:128], eng="act")
        P.dma(ys_out[:], ystok[:])


QPERM = np.concatenate([np.arange(h * 64, (h + 1) * 64) for h in (0, 4, 1, 5, 2, 6, 3, 7)])


def host_inputs(cfg, inp, consts, b, core=0):
    L = cfg["L"]
    f = lambda a: np.ascontiguousarray(np.asarray(a, dtype=np.float32))
    m = {}
    m["x_prompt"] = f(inp["x_prompt"][b])
    w_in = np.asarray(inp["w_in"], np.float32).copy()
    w_in[:, :, 0:512] = w_in[:, :, QPERM]
    m["w_in"] = f(w_in)
    m["w_out"] = f(inp["w_out"])
    m["w_up"] = f(inp["w_up"])
    m["w_down"] = f(inp["w_down"])
    nm = np.stack([np.asarray(inp["norm_mix"]), np.asarray(inp["norm_ffn"])], 1)
    m["normT"] = f(nm.reshape(L, 2, 8, 128).transpose(3, 0, 1, 2))
    m["fnormT"] = f(np.asarray(inp["final_norm"]).reshape(8, 128).T)
    m["cmp_w1"] = f(inp["cmp_w1"])
    m["cmp_posT"] = f(np.asarray(inp["cmp_pos"]).reshape(L, 2, 16, 2, 64).transpose(3, 4, 0, 1, 2).reshape(128, L, 2, 16))
    m["cmp_b1T"] = f(np.asarray(inp["cmp_b1"]).reshape(L, 2, 2, 128).transpose(3, 0, 1, 2))
    m["cmp_w2"] = f(inp["cmp_w2"])
    m["cmp_b2"] = f(inp["cmp_b2"])
    cw = np.asarray(inp["rg_conv_w"])
    rp = np.zeros((L, 9, 256), np.float32)
    rp[:, 0:4] = cw
    rp[:, 4] = np.asarray(inp["rg_conv_b"])
    rp[:, 5] = np.asarray(inp["rg_ba"])
    rp[:, 6] = np.asarray(inp["rg_bx"])
    rp[:, 7] = np.asarray(inp["rg_lambda"])
    m["rg_pT"] = f(rp.reshape(L, 9, 2, 128).transpose(3, 0, 2, 1))
    m["rg_wa"] = f(inp["rg_wa"])
    m["rg_wx"] = f(inp["rg_wx"])
    hp = np.stack([np.asarray(inp["hg_lower_bounds"]), np.asarray(inp["hg_gain"])], 0)
    m["hg_pT"] = f(hp.reshape(2, L, 2, 128).transpose(3, 2, 0, 1))
    for k, v in consts.items():
        m["c_" + k] = f(v)
    NS = cfg.get("NS", 0)
    if NS:
        sl = slice(core * NS, (core + 1) * NS)
        past = cfg["past"]
        npg = past // 128
        wb = min(512, past)
        m["x_sample"] = f(np.asarray(inp["x_sample"])[sl, 0, :])
        m["page_table"] = np.ascontiguousarray(np.asarray(inp["page_table"])[sl].reshape(1, NS * npg).astype(np.int32))
        m["cache_cmp"] = f(inp["cache_nsa_cmp_kv"]).reshape(-1, 256)
        m["cache_sel"] = f(inp["cache_nsa_sel_kv"]).reshape(-1, 256)
        m["cache_win"] = f(np.asarray(inp["cache_nsa_win_kv"])[:, sl]).reshape(L, NS, wb, 256)
        m["st_rgh"] = f(np.asarray(inp["state_rglru_h"])[:, sl])
        m["st_rgc"] = f(np.asarray(inp["state_rglru_conv"])[:, sl])
        m["st_hgs"] = f(np.asarray(inp["state_hgrn_s"])[:, sl])
    return m


_CACHE = {}


def run(cfg, inp, ncores=8):
    key = tuple(sorted(cfg.items()))
    if key not in _CACHE:
        _CACHE[key] = build(cfg)
    nc, consts = _CACHE[key]
    B = np.asarray(inp["x_prompt"]).shape[0]
    maps = [host_inputs(cfg, inp, consts, c % B, c) for c in range(ncores)]
    res = run_bass_kernel_spmd(nc, maps, core_ids=list(range(ncores)))
    return res.results


def kernel(**inp):
    xp = np.asarray(inp["x_prompt"])
    B, Tn, _ = xp.shape
    L = np.asarray(inp["w_in"]).shape[0]
    NDEC = np.asarray(inp["x_sample"]).shape[0]
    npg = np.asarray(inp["page_table"]).shape[1]
    past = npg * 128
    ncores = 8
    NS = NDEC // ncores
    npool = np.asarray(inp["cache_nsa_cmp_kv"]).shape[1]
    cfg = dict(T=Tn, L=L, past=past, NS=NS, npool=npool)
    r = run(cfg, inp, ncores)
    wb = min(512, past)
    wt = min(512, Tn)
    f = np.float32
    y_prompt = np.stack([r[b]["y_prompt"] for b in range(B)], 0).astype(f)
    y_sample = np.concatenate([r[c]["y_sample"] for c in range(ncores)], 0).reshape(NDEC, 1, D).astype(f)

    def pst(name, shp):
        return np.stack([np.asarray(r[b][name]).reshape((L,) + shp) for b in range(B)], 1).astype(f)

    def sst(name, shp):
        return np.concatenate([np.asarray(r[c][name]).reshape((L, NS) + shp) for c in range(ncores)], 1).astype(f)

    return (y_prompt, y_sample,
            pst("p_cmp_kv", (Tn, 2, 2, 64)), pst("p_sel_kv", (Tn, 2, 2, 64)), pst("p_win_kv", (wt, 2, 2, 64)),
            pst("p_rg_h", (256,)), pst("p_rg_conv", (3, 256)), pst("p_hg_s", (4, 64, 64)),
            sst("s_cmp_kv", (1, 2, 2, 64)), sst("s_sel_kv", (1, 2, 2, 64)), sst("s_win_kv", (wb, 2, 2, 64)),
            sst("s_rg_h", (256,)), sst("s_rg_conv", (3, 256)), sst("s_hg_s", (4, 64, 64)))
```
